# Optimizing a Trainium2 kernel written in Bass

```python
import math
import jax
import jax.numpy as jnp
from jax import lax
import numpy as np

D_MODEL = 2048
BATCH = 2
SEQ = 8192
DEPTH = 4

N_MIXERS = 2
N_DN_LAYERS = (DEPTH + 1) // 2
N_LRU_LAYERS = DEPTH // 2
RMS_EPS = 1e-6
CONV_WIDTH = 4
CONV_PAD = (2, 1)

DN_KEY_HEADS = 16
DN_VALUE_HEADS = 32
DN_HEAD_DIM = 128
DN_KEY_DIM = DN_KEY_HEADS * DN_HEAD_DIM
DN_VALUE_DIM = DN_VALUE_HEADS * DN_HEAD_DIM
DN_CONV_DIM = 2 * DN_KEY_DIM + DN_VALUE_DIM
DN_IN_DIM = DN_CONV_DIM + DN_VALUE_DIM + 4 * DN_VALUE_HEADS
DN_CHUNK = 64

LRU_WIDTH = D_MODEL
LRU_BLOCKS = 8
LRU_BLOCK_DIM = LRU_WIDTH // LRU_BLOCKS
LRU_C = 8.0

FFN_DIM = 5632

kernel_name = "bidir_hybrid_gdn_rglru_macaron"


def rmsnorm(x, w):
    xf = x.astype(jnp.float32)
    y = xf * lax.rsqrt(jnp.mean(xf * xf, axis=-1, keepdims=True) + RMS_EPS)
    return (y * w.astype(jnp.float32)).astype(x.dtype)


def l2norm(x):
    return x * lax.rsqrt(jnp.sum(x * x, axis=-1, keepdims=True) + RMS_EPS)


def swiglu_ffn(x, w_gate_up, w_down):
    g, u = jnp.split(x @ w_gate_up, 2, axis=-1)
    return (jax.nn.silu(g) * u) @ w_down


def centred_depthwise_conv(x, w):
    return lax.conv_general_dilated(
        x, w[:, None, :].astype(x.dtype), window_strides=(1,), padding=[CONV_PAD],
        dimension_numbers=("NWC", "WIO", "NWC"), feature_group_count=x.shape[-1])


def chunk_gated_delta_rule(q, k, v, g, beta):
    n_b, n_h, seq, d_k = k.shape
    d_v = v.shape[-1]
    n_c = seq // DN_CHUNK
    to_chunks = lambda t: jnp.moveaxis(t.reshape(n_b, n_h, n_c, DN_CHUNK, *t.shape[3:]), 2, 0)
    xs = (to_chunks(q), to_chunks(k), to_chunks(v), to_chunks(g), to_chunks(beta))
    tril = jnp.tril(jnp.ones((DN_CHUNK, DN_CHUNK), dtype=bool))
    strict = jnp.tril(jnp.ones((DN_CHUNK, DN_CHUNK), dtype=bool), -1)
    eye = jnp.eye(DN_CHUNK, dtype=jnp.float32)

    def step(state, inp):
        q_c, k_c, v_c, g_c, b_c = inp
        g_cum = jnp.cumsum(g_c, axis=-1)
        decay = jnp.exp(jnp.where(tril, g_cum[..., :, None] - g_cum[..., None, :], -jnp.inf))
        k_beta = k_c * b_c[..., None]
        a_mat = jnp.where(strict, jnp.einsum('nhid,nhjd->nhij', k_beta, k_c) * decay, 0.0) + eye
        rhs = jnp.concatenate([v_c * b_c[..., None], k_beta * jnp.exp(g_cum)[..., None]], axis=-1)
        sol = lax.linalg.triangular_solve(a_mat, rhs, left_side=True, lower=True, unit_diagonal=True)
        u_c, w_c = sol[..., :d_v], sol[..., d_v:]
        v_new = u_c - jnp.einsum('nhcd,nhde->nhce', w_c, state)
        attn = jnp.where(tril, jnp.einsum('nhid,nhjd->nhij', q_c, k_c) * decay, 0.0)
        o_c = (jnp.einsum('nhcd,nhde->nhce', q_c * jnp.exp(g_cum)[..., None], state)
               + jnp.einsum('nhij,nhje->nhie', attn, v_new))
        g_last = g_cum[..., -1]
        k_dec = k_c * jnp.exp(g_last[..., None] - g_cum)[..., None]
        state = state * jnp.exp(g_last)[..., None, None] + jnp.einsum('nhcd,nhce->nhde', k_dec, v_new)
        return state, o_c

    init = jnp.zeros((n_b, n_h, d_k, d_v), jnp.float32)
    _, o = lax.scan(step, init, xs)
    return jnp.moveaxis(o, 0, 2).reshape(n_b, n_h, seq, d_v)


def gated_deltanet_mixer(x, w_in, conv_w, a_log, dt_bias, norm_w, w_out):
    n_b, seq, _ = x.shape
    f32 = jnp.float32
    proj = x @ w_in
    qkv, z, ba = jnp.split(proj, [DN_CONV_DIM, DN_CONV_DIM + DN_VALUE_DIM], axis=-1)
    qkv = jax.nn.silu(centred_depthwise_conv(qkv, conv_w)).astype(f32)
    q, k, v = jnp.split(qkv, [DN_KEY_DIM, 2 * DN_KEY_DIM], axis=-1)
    rep = DN_VALUE_HEADS // DN_KEY_HEADS
    q = l2norm(q.reshape(n_b, seq, DN_KEY_HEADS, DN_HEAD_DIM)) * (DN_HEAD_DIM ** -0.5)
    k = l2norm(k.reshape(n_b, seq, DN_KEY_HEADS, DN_HEAD_DIM))
    q = jnp.repeat(q, rep, axis=2).transpose(0, 2, 1, 3)
    k = jnp.repeat(k, rep, axis=2).transpose(0, 2, 1, 3)
    v = v.reshape(n_b, seq, DN_VALUE_HEADS, DN_HEAD_DIM).transpose(0, 2, 1, 3)
    ba = ba.astype(f32).reshape(n_b, seq, 2, 2, DN_VALUE_HEADS)
    beta = jax.nn.sigmoid(ba[:, :, :, 0]).transpose(2, 0, 3, 1)
    g = (-jnp.exp(a_log.astype(f32)) * jax.nn.softplus(ba[:, :, :, 1] + dt_bias.astype(f32))).transpose(2, 0, 3, 1)
    flip = lambda t: jnp.flip(t, axis=2)
    o = chunk_gated_delta_rule(
        jnp.concatenate([q, flip(q)], 0), jnp.concatenate([k, flip(k)], 0),
        jnp.concatenate([v, flip(v)], 0), jnp.concatenate([g[0], flip(g[1])], 0),
        jnp.concatenate([beta[0], flip(beta[1])], 0))
    o = (o[:n_b] + flip(o[n_b:])).transpose(0, 2, 1, 3)
    z = z.astype(f32).reshape(n_b, seq, DN_VALUE_HEADS, DN_HEAD_DIM)
    o = rmsnorm(o, norm_w) * jax.nn.silu(z)
    return o.reshape(n_b, seq, DN_VALUE_DIM).astype(x.dtype) @ w_out


def linear_recurrence_op(c1, c2):
    a1, b1 = c1
    a2, b2 = c2
    return a1 * a2, a2 * b1 + b2


def rglru_mixer(x, w_in, conv_w, conv_b, w_gate_a, b_gate_a, w_gate_x, b_gate_x, lam, w_out):
    n_b, seq, _ = x.shape
    f32 = jnp.float32
    xb, gate = jnp.split(x @ w_in, 2, axis=-1)
    xb = (centred_depthwise_conv(xb, conv_w) + conv_b.astype(xb.dtype)).astype(f32)
    xh = xb.reshape(n_b, seq, LRU_BLOCKS, LRU_BLOCK_DIM)
    r = jax.nn.sigmoid(jnp.einsum('bshi,dhij->dbshj', xh, w_gate_a.astype(f32)).reshape(2, n_b, seq, LRU_WIDTH)
                       + b_gate_a.astype(f32)[:, None, None, :])
    i = jax.nn.sigmoid(jnp.einsum('bshi,dhij->dbshj', xh, w_gate_x.astype(f32)).reshape(2, n_b, seq, LRU_WIDTH)
                       + b_gate_x.astype(f32)[:, None, None, :])
    log_a = -LRU_C * r * jax.nn.softplus(-lam.astype(f32))[:, None, None, :]
    a = jnp.exp(log_a)
    b = jnp.sqrt(-jnp.expm1(2.0 * log_a)) * (i * xb[None])
    _, h_fwd = lax.associative_scan(linear_recurrence_op, (a[0], b[0]), axis=1)
    _, h_bwd = lax.associative_scan(linear_recurrence_op, (a[1], b[1]), axis=1, reverse=True)
    y = (h_fwd + h_bwd) * jax.nn.gelu(gate.astype(f32))
    return y.astype(x.dtype) @ w_out


def setup_inputs(seed: int = 0) -> dict:
    key = jax.random.key(seed)
    ks = jax.random.split(key, 24)
    f32 = jnp.float32
    normal = lambda k, shape, scale: jax.random.normal(k, shape, f32) * scale
    gain = lambda k, shape: 1.0 + 0.01 * jax.random.normal(k, shape, f32)
    x = normal(ks[0], (BATCH, SEQ, D_MODEL), 1.0)
    ffn1_norm = gain(ks[1], (DEPTH, D_MODEL))
    ffn1_w_gate_up = normal(ks[2], (DEPTH, D_MODEL, 2 * FFN_DIM), D_MODEL ** -0.5)
    ffn1_w_down = normal(ks[3], (DEPTH, FFN_DIM, D_MODEL), FFN_DIM ** -0.5)
    mix_norm = gain(ks[4], (DEPTH, D_MODEL))
    ffn2_norm = gain(ks[5], (DEPTH, D_MODEL))
    ffn2_w_gate_up = normal(ks[6], (DEPTH, D_MODEL, 2 * FFN_DIM), D_MODEL ** -0.5)
    ffn2_w_down = normal(ks[7], (DEPTH, FFN_DIM, D_MODEL), FFN_DIM ** -0.5)
    dn_w_in = normal(ks[8], (N_DN_LAYERS, D_MODEL, DN_IN_DIM), D_MODEL ** -0.5)
    dn_conv_w = normal(ks[9], (N_DN_LAYERS, CONV_WIDTH, DN_CONV_DIM), CONV_WIDTH ** -0.5)
    dn_a_log = jnp.log(jax.random.uniform(ks[10], (N_DN_LAYERS, 2, DN_VALUE_HEADS), f32, 1.0, 16.0))
    dt = jnp.exp(jax.random.uniform(ks[11], (N_DN_LAYERS, 2, DN_VALUE_HEADS), f32,
                                    math.log(1e-3), math.log(1e-1)))
    dn_dt_bias = dt + jnp.log(-jnp.expm1(-dt))
    dn_out_norm = gain(ks[12], (N_DN_LAYERS, DN_HEAD_DIM))
    dn_w_out = normal(ks[13], (N_DN_LAYERS, DN_VALUE_DIM, D_MODEL), DN_VALUE_DIM ** -0.5)
    lru_w_in = normal(ks[14], (N_LRU_LAYERS, D_MODEL, 2 * LRU_WIDTH), D_MODEL ** -0.5)
    lru_conv_w = normal(ks[15], (N_LRU_LAYERS, CONV_WIDTH, LRU_WIDTH), CONV_WIDTH ** -0.5)
    lru_conv_b = normal(ks[16], (N_LRU_LAYERS, LRU_WIDTH), 0.01)
    gshape = (N_LRU_LAYERS, 2, LRU_BLOCKS, LRU_BLOCK_DIM, LRU_BLOCK_DIM)
    lru_w_gate_a = normal(ks[17], gshape, LRU_BLOCK_DIM ** -0.5)
    lru_b_gate_a = normal(ks[18], (N_LRU_LAYERS, 2, LRU_WIDTH), 0.01)
    lru_w_gate_x = normal(ks[19], gshape, LRU_BLOCK_DIM ** -0.5)
    lru_b_gate_x = normal(ks[20], (N_LRU_LAYERS, 2, LRU_WIDTH), 0.01)
    u = jax.random.uniform(ks[21], (N_LRU_LAYERS, 2, LRU_WIDTH), f32, 0.9 ** 2, 0.999 ** 2)
    sp = -0.5 * jnp.log(u)
    lru_lambda = -(sp + jnp.log(-jnp.expm1(-sp)))
    lru_w_out = normal(ks[22], (N_LRU_LAYERS, LRU_WIDTH, D_MODEL), LRU_WIDTH ** -0.5)
    final_norm = gain(ks[23], (D_MODEL,))
    return {"x": x, "ffn1_norm": ffn1_norm, "ffn1_w_gate_up": ffn1_w_gate_up, "ffn1_w_down": ffn1_w_down,
            "mix_norm": mix_norm, "ffn2_norm": ffn2_norm, "ffn2_w_gate_up": ffn2_w_gate_up,
            "ffn2_w_down": ffn2_w_down, "dn_w_in": dn_w_in, "dn_conv_w": dn_conv_w, "dn_a_log": dn_a_log,
            "dn_dt_bias": dn_dt_bias, "dn_out_norm": dn_out_norm, "dn_w_out": dn_w_out,
            "lru_w_in": lru_w_in, "lru_conv_w": lru_conv_w, "lru_conv_b": lru_conv_b,
            "lru_w_gate_a": lru_w_gate_a, "lru_b_gate_a": lru_b_gate_a, "lru_w_gate_x": lru_w_gate_x,
            "lru_b_gate_x": lru_b_gate_x, "lru_lambda": lru_lambda, "lru_w_out": lru_w_out,
            "final_norm": final_norm}


def reference(x, ffn1_norm, ffn1_w_gate_up, ffn1_w_down, mix_norm, ffn2_norm, ffn2_w_gate_up, ffn2_w_down,
              dn_w_in, dn_conv_w, dn_a_log, dn_dt_bias, dn_out_norm, dn_w_out,
              lru_w_in, lru_conv_w, lru_conv_b, lru_w_gate_a, lru_b_gate_a, lru_w_gate_x, lru_b_gate_x,
              lru_lambda, lru_w_out, final_norm):
    h = x
    for layer in range(DEPTH):
        h = h + 0.5 * swiglu_ffn(rmsnorm(h, ffn1_norm[layer]), ffn1_w_gate_up[layer], ffn1_w_down[layer])
        hn = rmsnorm(h, mix_norm[layer])
        j = layer // N_MIXERS
        if layer % N_MIXERS == 0:
            h = h + gated_deltanet_mixer(hn, dn_w_in[j], dn_conv_w[j], dn_a_log[j], dn_dt_bias[j],
                                         dn_out_norm[j], dn_w_out[j])
        else:
            h = h + rglru_mixer(hn, lru_w_in[j], lru_conv_w[j], lru_conv_b[j], lru_w_gate_a[j],
                                lru_b_gate_a[j], lru_w_gate_x[j], lru_b_gate_x[j], lru_lambda[j],
                                lru_w_out[j])
        h = h + 0.5 * swiglu_ffn(rmsnorm(h, ffn2_norm[layer]), ffn2_w_gate_up[layer], ffn2_w_down[layer])
    return rmsnorm(h, final_norm)
```

```python
import contextlib
import numpy as np
import concourse.bass as bass
import concourse.mybir as mybir
from concourse.bass_utils import run_bass_kernel_spmd

F32 = mybir.dt.float32
BF16 = mybir.dt.bfloat16
AF = mybir.ActivationFunctionType
ALU = mybir.AluOpType

D = 2048
FF = 5632
NCORES = 8
EPS = 1e-6
SAME_ENGINE_SYNC = True


class Sched:
    ENG = ("pe", "act", "dve", "pool", "sp")

    def __init__(self, nc):
        self.nc = nc
        self.streams = {e: [] for e in self.ENG}
        self.ccnt = {e: 0 for e in self.ENG}
        self.dcnt = {}
        self.dq = {}
        self.seen = {e: {} for e in self.ENG}
        self.last_w = {}
        self.rd = {}
        self.stack = contextlib.ExitStack()
        self.nbuf = 0
        self.psum_banks = []
        self.psum_i = 0

    def sbuf(self, shape, dt, name=None):
        self.nbuf += 1
        name = "sb_" + (name or f"{self.nbuf}")
        return self.stack.enter_context(self.nc.sbuf_tensor(name, list(shape), dt))

    def psum(self, shape, dt=F32, name=None):
        self.nbuf += 1
        name = "ps_" + (name or f"{self.nbuf}")
        return self.stack.enter_context(self.nc.psum_tensor(name, list(shape), dt))

    def _deps(self, eng, reads, writes):
        deps = []
        for k in reads:
            if k in self.last_w:
                deps.append(self.last_w[k])
        for k in writes:
            if k in self.last_w:
                deps.append(self.last_w[k])
            deps.extend(self.rd.get(k, ()))
        need = {}
        for sk, v in deps:
            if sk == ("c", "pe") and eng == "pe":
                continue
            if sk == ("c", eng) and not SAME_ENGINE_SYNC:
                continue
            if v > need.get(sk, 0):
                need[sk] = v
        for sk, v in need.items():
            if v > self.seen[eng].get(sk, 0):
                self.seen[eng][sk] = v
                self.streams[eng].append(("wait", sk, v))

    def _post(self, tok, reads, writes):
        for k in reads:
            self.rd.setdefault(k, []).append(tok)
        for k in writes:
            self.last_w[k] = tok
            self.rd[k] = []

    def op(self, eng, fn, reads=(), writes=()):
        self._deps(eng, reads, writes)
        self.ccnt[eng] += 1
        tok = (("c", eng), self.ccnt[eng])
        self.streams[eng].append(("op", fn, tok))
        self._post(tok, reads, writes)

    def dma(self, eng, out, in_, reads=(), writes=(), sem=None):
        assert sem is not None
        self._deps(eng, reads, writes)
        sk = ("d", sem)
        prev = self.dcnt.get(sk, 0)
        if prev > self.seen[eng].get(sk, 0):
            self.seen[eng][sk] = prev
            self.streams[eng].append(("wait", sk, prev))
        self.dcnt[sk] = prev + 16
        self.dq.setdefault(eng, set()).add(sk)
        tok = (sk, self.dcnt[sk])
        self.streams[eng].append(("dma", (out, in_), tok))
        self._post(tok, reads, writes)

    def mm(self, ps, lhsT, rhs, start, stop, reads, writes):
        self.op("pe", lambda e: e.matmul(ps, lhsT, rhs, start=start, stop=stop), reads, writes)

    def tr(self, ps, in_, ident, reads, writes):
        self.op("pe", lambda e: e.transpose(ps, in_, ident), reads, writes)

    def act(self, out, in_, func, reads, writes, bias=None, scale=None):
        kw = {}
        if bias is not None:
            kw["bias"] = bias
        if scale is not None:
            kw["scale"] = scale
        self.op("act", lambda e: e.activation(out=out, in_=in_, func=func, **kw), reads, writes)

    def tt(self, eng, out, in0, in1, op, reads, writes):
        self.op(eng, lambda e: e.tensor_tensor(out, in0, in1, op), reads, writes)

    def ts(self, eng, out, in0, s1, s2, op0, op1, reads, writes):
        if s2 is None:
            self.op(eng, lambda e: e.tensor_scalar(out, in0, s1, None, op0), reads, writes)
        else:
            self.op(eng, lambda e: e.tensor_scalar(out, in0, s1, s2, op0, op1), reads, writes)

    def stt(self, eng, out, in0, scalar, in1, op0, op1, reads, writes):
        self.op(eng, lambda e: e.scalar_tensor_tensor(out, in0, scalar, in1, op0, op1), reads, writes)

    def copy(self, eng, out, in_, reads, writes):
        if eng == "act":
            self.op(eng, lambda e: e.copy(out, in_), reads, writes)
        else:
            self.op(eng, lambda e: e.tensor_copy(out, in_), reads, writes)

    def memset(self, eng, ap, val, writes):
        self.op(eng, lambda e: e.memset(ap, val), (), writes)

    def finish(self):
        nc = self.nc
        st = self.stack
        sems = {}
        for e in self.ENG:
            sems[("c", e)] = st.enter_context(nc.semaphore(f"c_{e}"))
        for i, sk in enumerate(self.dcnt):
            sems[sk] = st.enter_context(nc.semaphore(f"d_{i}"))
        for e, sks in self.dq.items():
            for sk in sks:
                if self.dcnt[sk] > self.seen[e].get(sk, 0):
                    self.streams[e].append(("wait", sk, self.dcnt[sk]))
        streams = self.streams

        def replay(name, eng):
            for item in streams[name]:
                if item[0] == "wait":
                    eng.wait_ge(sems[item[1]], item[2])
                elif item[0] == "op":
                    item[1](eng).then_inc(sems[item[2][0]], 1)
                else:
                    o, i = item[1]
                    eng.dma_start(out=o, in_=i).then_inc(sems[item[2][0]], 16)

        with nc.Block() as block:
            @block.tensor
            def _(e):
                replay("pe", e)

            @block.scalar
            def _(e):
                replay("act", e)

            @block.vector
            def _(e):
                replay("dve", e)

            @block.gpsimd
            def _(e):
                replay("pool", e)

            @block.sync
            def _(e):
                replay("sp", e)
        st.close()


class Rot:
    def __init__(self, S, name, n, shape, dt, psum=False):
        self.tiles = [(S.psum(shape, dt, f"{name}{i}") if psum else S.sbuf(shape, dt, f"{name}{i}")) for i in range(n)]
        self.keys = [(name, i) for i in range(n)]
        self.i = 0

    def next(self):
        t, k = self.tiles[self.i], self.keys[self.i]
        self.i = (self.i + 1) % len(self.tiles)
        return t, k


def dense_phase(nc, T, ops, TT=512):
    S = Sched(nc)
    KC = D // 128
    FC = FF // 128
    NT = T // TT
    dram = {}

    def din(name, shape):
        dram[name] = nc.dram_tensor(name, list(shape), F32, kind="ExternalInput").ap()
        return dram[name]

    def dout(name, shape):
        dram[name] = nc.dram_tensor(name, list(shape), F32, kind="ExternalOutput").ap()
        return dram[name]

    hT_in = din("hT_in", [D, T])
    need_h_out = any(o["op"] == "store_h" for o in ops)
    for idx, o in enumerate(ops):
        if o["op"] == "ffn":
            o["nw"] = din(f"nw{idx}", [128, KC])
            o["wgu"] = din(f"wgu{idx}", [D, 2 * FF])
            o["wd"] = din(f"wd{idx}", [FF, D])
        elif o["op"] == "outproj":
            o["oT"] = din(f"oT{idx}", [o["dm"], T])
            o["wo"] = din(f"wo{idx}", [o["dm"], D])
        elif o["op"] == "inproj":
            o["nw"] = din(f"nw{idx}", [128, KC])
            o["wi"] = din(f"wi{idx}", [D, o["n"]])
            o["out"] = dout(f"proj{idx}", [o["n"], T])
        elif o["op"] == "final":
            o["nw"] = din(f"nw{idx}", [128, KC])
            o["out"] = dout(f"y{idx}", [D, T])
        elif o["op"] == "store_h":
            o["out"] = dout(f"hT_out{idx}", [D, T])

    h = S.sbuf([128, KC, TT], F32, "h")
    xn = S.sbuf([128, KC, TT], BF16, "xn")
    actb = S.sbuf([128, FC, TT], BF16, "actb")
    ones = S.sbuf([128, 128], BF16, "ones")
    nws = {}
    for idx, o in enumerate(ops):
        if "nw" in o:
            nws[idx] = S.sbuf([128, KC], F32, f"nwt{idx}")
    WG = 256
    wgu_pool = Rot(S, "wgu", 2, [128, 2, KC, WG], BF16)
    wd_pool = Rot(S, "wd", 2, [128, FC, WG], BF16)
    sq_pool = Rot(S, "sq", 2, [128, TT], BF16)
    tmp_pool = Rot(S, "tmp", 3, [128, TT], F32)
    rstd = S.sbuf([128, TT], F32, "rstd")
    psp = Rot(S, "psb", 8, [128, TT], F32, psum=True)

    S.memset("dve", ones[:], 1.0, ["ones"])
    for idx in nws:
        S.dma("sp", nws[idx][:], ops[idx]["nw"], (), [("nw", idx)], sem=("nw", idx))

    def rmsnorm(nwidx, out_fp32_cb=None):
        ss, ssk = psp.next()
        for c in range(KC):
            sq, sqk = sq_pool.next()
            S.act(sq[:], h[:, c, :], AF.Square, [("h", c)], [sqk])
            S.mm(ss[:], ones[:], sq[:], c == 0, c == KC - 1, ["ones", sqk], [ssk])
        S.act(rstd[:], ss[:], AF.Sqrt, [ssk], ["rstd"], bias=EPS_AP[0][:, 0:1], scale=1.0 / D)
        S.op("dve", lambda e: e.reciprocal(rstd[:], rstd[:]), ["rstd"], ["rstd"])
        nwt = nws[nwidx]
        for c in range(KC):
            if out_fp32_cb is None:
                S.stt("dve", xn[:, c, :], h[:, c, :], nwt[:, c:c + 1], rstd[:], ALU.mult, ALU.mult,
                      [("h", c), ("nw", nwidx), "rstd"], [("xn", c)])
            else:
                out_fp32_cb(c, nwt)

    EPS_AP = [S.sbuf([128, 1], F32, "epsb")]
    S.memset("dve", EPS_AP[0][:], EPS, ["epsb"])

    def linear_acc(w_ap, kc, ncols, x_tile, xkeys, w_pool_kind, epilogue):
        for g0 in range(0, ncols, WG):
            gw = min(WG, ncols - g0)
            wt, wk = wd_pool.next()
            src = w_ap[:, g0:g0 + gw].rearrange("(c p) n -> p c n", p=128)
            half = (kc + 1) // 2
            S.dma("pool", wt[:, 0:half, 0:gw], src[:, 0:half, :], (), [(wk, 0)], sem=(wk, 0))
            if kc > half:
                S.dma("pool", wt[:, half:kc, 0:gw], src[:, half:kc, :], (), [(wk, 1)], sem=(wk, 1))
            for j in range(gw // 128):
                ps, pk = psp.next()
                for c in range(kc):
                    S.mm(ps[:], wt[:, c, j * 128:(j + 1) * 128], x_tile[:, c, :], c == 0, c == kc - 1,
                         [(wk, 0 if c < half else 1), xkeys(c)], [pk])
                epilogue(g0 // 128 + j, ps, pk)

    for t in range(NT):
        tsl = slice(t * TT, (t + 1) * TT)
        hv = hT_in.rearrange("(c p) t -> p c t", p=128)
        for c0 in range(0, KC, 4):
            S.dma("sp", h[:, c0:c0 + 4, :], hv[:, c0:c0 + 4, tsl], (), [("h", c) for c in range(c0, c0 + 4)], sem=("h", c0))
        for idx, o in enumerate(ops):
            if o["op"] == "ffn":
                rmsnorm(idx)
                for g0 in range(0, FF, WG):
                    wt, wk = wgu_pool.next()
                    for s in range(2):
                        src = o["wgu"][:, s * FF + g0: s * FF + g0 + WG].rearrange("(c p) n -> p c n", p=128)
                        S.dma("pool", wt[:, s, :, :], src, (), [(wk, s)], sem=(wk, s))
                    for j in range(WG // 128):
                        fc = g0 // 128 + j
                        gp, gk = psp.next()
                        up, uk = psp.next()
                        for c in range(KC):
                            S.mm(gp[:], wt[:, 0, c, j * 128:(j + 1) * 128], xn[:, c, :], c == 0, c == KC - 1,
                                 [(wk, 0), ("xn", c)], [gk])
                        for c in range(KC):
                            S.mm(up[:], wt[:, 1, c, j * 128:(j + 1) * 128], xn[:, c, :], c == 0, c == KC - 1,
                                 [(wk, 1), ("xn", c)], [uk])
                        tm, tk = tmp_pool.next()
                        S.act(tm[:], gp[:], AF.Silu, [gk], [tk])
                        S.tt("dve", actb[:, fc, :], tm[:], up[:], ALU.mult, [tk, uk], [("act", fc)])

                def epi_down(dc, ps, pk):
                    S.stt("dve", h[:, dc, :], ps[:], 0.5, h[:, dc, :], ALU.mult, ALU.add,
                          [pk, ("h", dc)], [("h", dc)])
                linear_acc(o["wd"], FC, D, actb, lambda c: ("act", c), "wd", epi_down)
            elif o["op"] == "outproj":
                kc = o["dm"] // 128
                ov = o["oT"].rearrange("(c p) t -> p c t", p=128)
                for c0 in range(0, kc, 8):
                    S.dma("pool", actb[:, c0:c0 + 8, :], ov[:, c0:c0 + 8, tsl], (),
                          [("act", c) for c in range(c0, c0 + 8)], sem=("actb", c0))

                def epi_out(dc, ps, pk):
                    S.tt("dve", h[:, dc, :], ps[:], h[:, dc, :], ALU.add, [pk, ("h", dc)], [("h", dc)])
                linear_acc(o["wo"], kc, D, actb, lambda c: ("act", c), "wd", epi_out)
            elif o["op"] == "inproj":
                rmsnorm(idx)
                outv = o["out"]

                def epi_in(nc_, ps, pk, outv=outv):
                    tm, tk = tmp_pool.next()
                    S.copy("act", tm[:], ps[:], [pk], [tk])
                    S.dma("sp", outv[nc_ * 128:(nc_ + 1) * 128, tsl], tm[:], [tk], [], sem=tk)
                linear_acc(o["wi"], KC, o["n"], xn, lambda c: ("xn", c), "wd", epi_in)
            elif o["op"] == "final":
                outv = o["out"].rearrange("(c p) t -> p c t", p=128)

                def cb(c, nwt, outv=outv, idx=idx):
                    tm, tk = tmp_pool.next()
                    S.stt("dve", tm[:], h[:, c, :], nwt[:, c:c + 1], rstd[:], ALU.mult, ALU.mult,
                          [("h", c), ("nw", idx), "rstd"], [tk])
                    S.dma("sp", outv[:, c, tsl], tm[:], [tk], [], sem=tk)
                rmsnorm(idx, cb)
            elif o["op"] == "store_h":
                outv = o["out"].rearrange("(c p) t -> p c t", p=128)
                for c0 in range(0, KC, 4):
                    S.dma("sp", outv[:, c0:c0 + 4, tsl], h[:, c0:c0 + 4, :],
                          [("h", c) for c in range(c0, c0 + 4)], [], sem=("h", c0))
    S.finish()
    return nc


def lru_phase(nc, Sq, TT=512):
    S = Sched(nc)
    NT = Sq // TT
    NCH = 4

    def din(name, shape):
        return nc.dram_tensor(name, list(shape), F32, kind="ExternalInput").ap()

    xbT = din("xbT", [512, Sq])
    gateT = din("gateT", [512, Sq])
    cw_d = din("cw", [128, NCH, 4])
    cb_d = din("cb", [128, NCH])
    wga_d = din("wga", [2, 2, 256, 256])
    wgx_d = din("wgx", [2, 2, 256, 256])
    bga_d = din("bga", [128, 2, NCH])
    bgx_d = din("bgx", [128, 2, NCH])
    lam_d = din("lam", [128, 2, NCH])
    yT = nc.dram_tensor("yT", [512, Sq], F32, kind="ExternalOutput").ap()

    cw = S.sbuf([128, NCH, 4], F32, "cw")
    cb = S.sbuf([128, NCH], F32, "cb")
    bga = S.sbuf([128, 2, NCH], F32, "bga")
    bgx = S.sbuf([128, 2, NCH], F32, "bgx")
    lam = S.sbuf([128, 2, NCH], F32, "lam")
    nsp8 = S.sbuf([128, 2, NCH], F32, "nsp8")
    wga = S.sbuf([128, 8, 256], BF16, "wga")
    wgx = S.sbuf([128, 8, 256], BF16, "wgx")
    carry = S.sbuf([128, NCH], F32, "carry")
    S.dma("sp", cw[:], cw_d, (), ["cw"], sem="cw")
    S.dma("sp", cb[:], cb_d, (), ["cb"], sem="cb")
    S.dma("sp", bga[:], bga_d, (), ["bga"], sem="bga")
    S.dma("sp", bgx[:], bgx_d, (), ["bgx"], sem="bgx")
    S.dma("sp", lam[:], lam_d, (), ["lam"], sem="lam")
    S.dma("pool", wga[:], wga_d.rearrange("d b (ic p) j -> p (d b ic) j", p=128), (), ["wga"], sem="wga")
    S.dma("pool", wgx[:], wgx_d.rearrange("d b (ic p) j -> p (d b ic) j", p=128), (), ["wgx"], sem="wgx")
    S.act(nsp8[:], lam[:], AF.Exp, ["lam"], ["nsp8"], scale=-1.0)
    S.act(nsp8[:], nsp8[:], AF.Ln, ["nsp8"], ["nsp8"], bias=1.0)
    S.ts("dve", nsp8[:], nsp8[:], -8.0, None, ALU.mult, None, ["nsp8"], ["nsp8"])

    xr_pool = Rot(S, "xr", 2, [128, 2, TT + 3], F32)
    xc_pool = Rot(S, "xc", 2, [128, 2, TT], F32)
    xcb_pool = Rot(S, "xcb", 2, [128, 2, TT], BF16)
    tp = Rot(S, "lt", 12, [128, TT], F32)
    psp = Rot(S, "lps", 6, [128, TT], F32, psum=True)
    GC = 2.0 * float(np.sqrt(2.0 / np.pi))

    for d in (0, 1):
        order = list(range(NT)) if d == 0 else list(range(NT - 1, -1, -1))
        for ti, t in enumerate(order):
            t0 = t * TT
            for blk in range(2):
                xr, xk = xr_pool.next()
                lo, hi = t0 - 2, t0 + TT + 1
                slo, shi = max(lo, 0), min(hi, Sq)
                if lo < 0 or hi > Sq:
                    S.memset("pool", xr[:], 0.0, [xk])
                src = xbT.rearrange("(c p) s -> p c s", p=128)[:, 2 * blk:2 * blk + 2, slo:shi]
                S.dma("sp", xr[:, :, slo - lo:shi - lo], src, (), [xk], sem=xk)
                xc, xck = xc_pool.next()
                xcb, xcbk = xcb_pool.next()
                for jc in range(2):
                    c = 2 * blk + jc
                    S.ts("dve", xc[:, jc, :], xr[:, jc, 0:TT], cw[:, c, 0:1], cb[:, c:c + 1], ALU.mult, ALU.add,
                         [xk, "cw", "cb"], [(xck, jc)])
                    for j in range(1, 4):
                        S.stt("dve", xc[:, jc, :], xr[:, jc, j:j + TT], cw[:, c, j:j + 1], xc[:, jc, :],
                              ALU.mult, ALU.add, [xk, "cw", (xck, jc)], [(xck, jc)])
                    S.copy("pool", xcb[:, jc, :], xc[:, jc, :], [(xck, jc)], [(xcbk, jc)])
                for jc in range(2):
                    c = 2 * blk + jc
                    pr, prk = psp.next()
                    pi, pik = psp.next()
                    for ic in range(2):
                        S.mm(pr[:], wga[:, d * 4 + blk * 2 + ic, jc * 128:(jc + 1) * 128], xcb[:, ic, :],
                             ic == 0, ic == 1, ["wga", (xcbk, ic)], [prk])
                    for ic in range(2):
                        S.mm(pi[:], wgx[:, d * 4 + blk * 2 + ic, jc * 128:(jc + 1) * 128], xcb[:, ic, :],
                             ic == 0, ic == 1, ["wgx", (xcbk, ic)], [pik])
                    r, rk = tp.next()
                    S.act(r[:], pr[:], AF.Sigmoid, [prk, "bga"], [rk], bias=bga[:, d, c:c + 1])
                    gi, gik = tp.next()
                    S.act(gi[:], pi[:], AF.Sigmoid, [pik, "bgx"], [gik], bias=bgx[:, d, c:c + 1])
                    a, ak = tp.next()
                    S.act(a[:], r[:], AF.Exp, [rk, "nsp8"], [ak], scale=nsp8[:, d, c:c + 1])
                    S.tt("pool", r[:], a[:], a[:], ALU.mult, [ak], [rk])
                    S.act(r[:], r[:], AF.Sqrt, [rk], [rk], bias=1.0, scale=-1.0)
                    S.tt("pool", gi[:], gi[:], xc[:, jc, :], ALU.mult, [gik, (xck, jc)], [gik])
                    S.tt("dve", gi[:], gi[:], r[:], ALU.mult, [gik, rk], [gik])
                    hh, hk = tp.next()
                    init = 0.0 if ti == 0 else carry[:, c:c + 1]
                    ckey = ("carry", c)
                    if d == 0:
                        S.op("dve", lambda e, hh=hh, a=a, gi=gi, init=init: e.tensor_tensor_scan(
                            hh[:], a[:], gi[:], init, ALU.mult, ALU.add), [ak, gik, ckey], [hk])
                        S.copy("pool", carry[:, c:c + 1], hh[:, TT - 1:TT], [hk], [ckey])
                        S.dma("sp", yT[c * 128:(c + 1) * 128, t0:t0 + TT], hh[:], [hk], [("y", c, t)], sem=hk)
                    else:
                        S.op("dve", lambda e, hh=hh, a=a, gi=gi, init=init: e.tensor_tensor_scan(
                            hh[:, ::-1], a[:, ::-1], gi[:, ::-1], init, ALU.mult, ALU.add), [ak, gik, ckey], [hk])
                        S.copy("pool", carry[:, c:c + 1], hh[:, 0:1], [hk], [ckey])
                        hf, hfk = tp.next()
                        S.dma("sp", hf[:], yT[c * 128:(c + 1) * 128, t0:t0 + TT], [("y", c, t)], [hfk], sem=hfk)
                        g, gk = tp.next()
                        S.dma("sp", g[:], gateT[c * 128:(c + 1) * 128, t0:t0 + TT], (), [gk], sem=gk)
                        u, uk = tp.next()
                        S.act(u[:], g[:], AF.Square, [gk], [uk])
                        S.ts("pool", u[:], u[:], 0.044715, 1.0, ALU.mult, ALU.add, [uk], [uk])
                        S.tt("pool", u[:], u[:], g[:], ALU.mult, [uk, gk], [uk])
                        S.act(u[:], u[:], AF.Sigmoid, [uk], [uk], scale=GC)
                        S.tt("pool", u[:], u[:], g[:], ALU.mult, [uk, gk], [uk])
                        S.tt("dve", hh[:], hh[:], hf[:], ALU.add, [hk, hfk], [hk])
                        S.tt("dve", hh[:], hh[:], u[:], ALU.mult, [hk, uk], [hk])
                        S.dma("sp", yT[c * 128:(c + 1) * 128, t0:t0 + TT], hh[:], [hk], [("y", c, t)], sem=hk)
    S.finish()
    return nc


def dn_consts():
    m = np.arange(128)[:, None]
    i = np.arange(128)[None, :]
    same = (m // 64) == (i // 64)
    cm = np.stack([(m <= i) & same, (m < i) & same, (m >= i) & same, (m > i) & same, m == i]).astype(np.float32)
    cm = np.ascontiguousarray(cm.transpose(1, 0, 2))
    c01 = np.stack([(np.arange(128) < 64), (np.arange(128) >= 64)], axis=1).astype(np.float32)
    return cm, np.ascontiguousarray(c01)


def dn_phase(nc, Sq, stages=(0, 1, 2), dbg=False):
    S = Sched(nc)
    NB = Sq // 128
    TT0 = 512 if Sq >= 512 else Sq
    NKH, NVH = 4, 8
    LE, LT, GE, GT, ID = 0, 1, 2, 3, 4

    def din(name, shape):
        return nc.dram_tensor(name, list(shape), F32, kind="ExternalInput").ap()

    qT_d = din("qT", [NKH * 128, Sq])
    kT_d = din("kT", [NKH * 128, Sq])
    vT_d = din("vT", [NVH * 128, Sq])
    zT_d = din("zT", [NVH * 128, Sq])
    beta_d = din("betaT", [16, Sq])
    a_d = din("aT", [16, Sq])
    cwq_d = din("cwq", [128, NKH, 4])
    cwk_d = din("cwk", [128, NKH, 4])
    cwv_d = din("cwv", [128, NVH, 4])
    alog_d = din("alog", [16, 1])
    dtb_d = din("dtb", [16, 1])
    onw_d = din("onw", [128, 1])
    cm_d = din("cmask", [128, 5, 128])
    c01_d = din("c01", [128, 2])
    oT_d = nc.dram_tensor("oT", [NVH * 128, Sq], F32, kind="ExternalOutput").ap()
    skind = "ExternalOutput" if dbg else "Internal"
    kTn_s = nc.dram_tensor("kTn_s", [128, NKH, Sq], BF16, kind=skind).ap()
    qTn_s = nc.dram_tensor("qTn_s", [128, NKH, Sq], BF16, kind=skind).ap()
    ktok_s = nc.dram_tensor("ktok_s", [Sq, NKH, 128], BF16, kind=skind).ap()
    vtok_s = nc.dram_tensor("vtok_s", [Sq, NVH, 128], BF16, kind=skind).ap()
    bg_s = nc.dram_tensor("bg_s", [Sq, 32], F32, kind=skind).ap()
    o_s = nc.dram_tensor("o_s", [2, Sq, NVH, 128], F32, kind=skind).ap()

    cm = S.sbuf([128, 5, 128], F32, "cm")
    c01 = S.sbuf([128, 2], F32, "c01")
    cwq = S.sbuf([128, NKH, 4], F32, "cwq")
    cwk = S.sbuf([128, NKH, 4], F32, "cwk")
    cwv = S.sbuf([128, NVH, 4], F32, "cwv")
    alog = S.sbuf([16, 1], F32, "alog")
    dtb = S.sbuf([16, 1], F32, "dtb")
    onw = S.sbuf([128, 1], F32, "onw")
    for t, dsrc, nm in ((cm, cm_d, "cm"), (c01, c01_d, "c01"), (cwq, cwq_d, "cwq"), (cwk, cwk_d, "cwk"),
                        (cwv, cwv_d, "cwv"), (alog, alog_d, "alog"), (dtb, dtb_d, "dtb"), (onw, onw_d, "onw")):
        S.dma("sp", t[:], dsrc, (), [nm], sem=nm)
    idb = S.sbuf([128, 128], BF16, "idb")
    S.copy("dve", idb[:], cm[:, ID, :], ["cm"], ["idb"])
    onesb = S.sbuf([128, 128], BF16, "onesb")
    S.memset("dve", onesb[:], 1.0, ["onesb"])
    onesf = S.sbuf([128, 128], F32, "onesf")
    S.memset("dve", onesf[:], 1.0, ["onesf"])
    epsb = S.sbuf([128, 1], F32, "epsb")
    S.memset("dve", epsb[:], EPS, ["epsb"])
    negA = S.sbuf([16, 1], F32, "negA")
    S.act(negA[:], alog[:], AF.Exp, ["alog"], ["negA"])
    S.ts("dve", negA[:], negA[:], -1.0, None, ALU.mult, None, ["negA"], ["negA"])

    psA = Rot(S, "dpa", 3, [128, 4, 128], F32, psum=True)
    psT = Rot(S, "dpt", 1, [128, 8, 128], BF16, psum=True)
    psS = Rot(S, "dps", 1, [128, 4, 32], F32, psum=True)
    NLB = 3
    psL_t = [S.psum([128, 4, 128], F32, f"dpl{i}") for i in range(NLB)]
    psL_i = [0, 0, 0]

    def next_half(hf):
        i = psL_i[0]
        psL_i[0] = (i + 1) % NLB
        return psL_t[i], ("dpl", i, hf)

    def next_whole():
        i = psL_i[0]
        psL_i[0] = (i + 1) % NLB
        return psL_t[i], [("dpl", i, 0), ("dpl", i, 1)]

    raw_p = Rot(S, "s0raw", 3, [128, TT0 + 3], F32)
    x_p = Rot(S, "s0x", 3, [128, TT0], F32)
    sq_p = Rot(S, "s0sq", 2, [128, TT0], BF16)
    rs_p = Rot(S, "s0rs", 2, [128, TT0], F32)
    xn_p = Rot(S, "s0xn", 3, [128, TT0], BF16)
    tk_p = Rot(S, "s0tk", 3, [128, TT0 // 128, 128], BF16)
    sc_p = Rot(S, "s0sc", 2, [16, TT0], F32)
    bgo_p = Rot(S, "s0bgo", 2, [128, TT0 // 128, 32], F32)
    QS = float(128 ** -0.5)
    nb0 = TT0 // 128
    if 0 in stages:
        for t0 in range(0, Sq, TT0):
            for kind, src_d, cwt, nh in (("k", kT_d, cwk, NKH), ("q", qT_d, cwq, NKH), ("v", vT_d, cwv, NVH)):
                for hh in range(nh):
                    raw, rk = raw_p.next()
                    lo, hi = t0 - 2, t0 + TT0 + 1
                    slo, shi = max(lo, 0), min(hi, Sq)
                    if lo < 0 or hi > Sq:
                        S.memset("pool", raw[:], 0.0, [rk])
                    S.dma("sp", raw[:, slo - lo:shi - lo], src_d[hh * 128:(hh + 1) * 128, slo:shi], (), [rk], sem=rk)
                    x, xk = x_p.next()
                    S.ts("dve", x[:], raw[:, 0:TT0], cwt[:, hh, 0:1], None, ALU.mult, None, [rk, "cw" + kind], [xk])
                    for j in range(1, 4):
                        S.stt("dve", x[:], raw[:, j:j + TT0], cwt[:, hh, j:j + 1], x[:], ALU.mult, ALU.add,
                              [rk, "cw" + kind, xk], [xk])
                    xn, xnk = xn_p.next()
                    if kind == "v":
                        S.act(xn[:], x[:], AF.Silu, [xk], [xnk])
                    else:
                        S.act(x[:], x[:], AF.Silu, [xk], [xk])
                        sq, sqk = sq_p.next()
                        S.tt("pool", sq[:], x[:], x[:], ALU.mult, [xk], [sqk])
                        ps, pk = psA.next()
                        psv = ps[:].rearrange("p a b -> p (a b)")
                        S.mm(psv[:, 0:TT0], onesb[:], sq[:], True, True, ["onesb", sqk], [pk])
                        rs, rsk = rs_p.next()
                        S.act(rs[:], psv[:, 0:TT0], AF.Sqrt, [pk, "epsb"], [rsk], bias=epsb[:, 0:1])
                        S.op("dve", lambda e, rs=rs: e.reciprocal(rs[:], rs[:]), [rsk], [rsk])
                        if kind == "q":
                            S.stt("dve", xn[:], x[:], QS, rs[:], ALU.mult, ALU.mult, [xk, rsk], [xnk])
                        else:
                            S.tt("dve", xn[:], x[:], rs[:], ALU.mult, [xk, rsk], [xnk])
                        dst = (kTn_s if kind == "k" else qTn_s)[:, hh, t0:t0 + TT0]
                        S.dma("sp", dst, xn[:], [xnk], [(kind + "Tn", t0, hh)], sem=xnk)
                    if kind in ("k", "v"):
                        pt, ptk = psT.next()
                        for jb in range(nb0):
                            S.tr(pt[:, jb, :], xn[:, jb * 128:(jb + 1) * 128], idb[:], [xnk, "idb"], [ptk])
                        tk, tkk = tk_p.next()
                        S.copy("act", tk[:], pt[:, 0:nb0, :], [ptk], [tkk])
                        dst_t = (ktok_s if kind == "k" else vtok_s).rearrange("(nb p) h d -> p nb h d", p=128)
                        S.dma("sp", dst_t[:, t0 // 128:t0 // 128 + nb0, hh, :], tk[:], [tkk], [(kind + "tok", t0, hh)], sem=tkk)
            bet, bk = sc_p.next()
            S.dma("sp", bet[:], beta_d[:, t0:t0 + TT0], (), [bk], sem=bk)
            av, ak = sc_p.next()
            S.dma("sp", av[:], a_d[:, t0:t0 + TT0], (), [ak], sem=ak)
            S.act(bet[:], bet[:], AF.Sigmoid, [bk], [bk])
            S.act(av[:], av[:], AF.Exp, [ak, "dtb"], [ak], bias=dtb[:, 0:1])
            S.act(av[:], av[:], AF.Ln, [ak], [ak], bias=1.0)
            S.ts("dve", av[:], av[:], negA[:, 0:1], None, ALU.mult, None, [ak, "negA"], [ak])
            pss, psk = psS.next()
            for jb in range(nb0):
                S.tr(pss[:, jb, 0:16], bet[:, jb * 128:(jb + 1) * 128], cm[0:16, ID, 0:16], [bk, "cm"], [psk])
                S.tr(pss[:, jb, 16:32], av[:, jb * 128:(jb + 1) * 128], cm[0:16, ID, 0:16], [ak, "cm"], [psk])
            bgo, bgk = bgo_p.next()
            S.copy("act", bgo[:], pss[:, 0:nb0, :], [psk], [bgk])
            S.dma("sp", bg_s.rearrange("(nb p) c -> p nb c", p=128)[:, t0 // 128:t0 // 128 + nb0, :], bgo[:],
                  [bgk], [("bg", t0)], sem=bgk)

    NSLOT = 3
    slots = {}
    for d in range(2):
        for r in range(NSLOT):
            nm = f"sl{d}{r}"
            slots[(d, r)] = dict(
                kT=S.sbuf([128, NKH, 128], BF16, nm + "kT"), qT=S.sbuf([128, NKH, 128], BF16, nm + "qT"),
                ktok=S.sbuf([128, NKH, 128], BF16, nm + "ktok"), vtok=S.sbuf([128, NVH, 128], BF16, nm + "vtok"),
                bg=S.sbuf([128, 32], F32, nm + "bg"), sc=S.sbuf([128, 48], F32, nm + "sc"),
                TU=S.sbuf([128, NVH, 128], BF16, nm + "TU"), AT=S.sbuf([128, NVH, 128], BF16, nm + "AT"),
                kdec=S.sbuf([128, NVH, 128], BF16, nm + "kdec"), nm=nm)
    gm2_p = Rot(S, "gm2", 2, [128, 16], F32)
    GsPs_p = Rot(S, "gsps", 2, [128, 2, NKH, 128], F32)
    Gm4_p = Rot(S, "gm4", 2, [128, 4, 128], F32)
    E4_p = Rot(S, "e4", 2, [128, 4, 128], F32)
    zb_p = Rot(S, "zb", 10, [128, 4, 128], BF16)
    st32 = {(d, hg): S.sbuf([128, 4, 128], F32, f"st32_{d}{hg}") for d in range(2) for hg in range(2)}
    stbf = {(d, hg): S.sbuf([128, 4, 128], BF16, f"stbf_{d}{hg}") for d in range(2) for hg in range(2)}
    R_p = Rot(S, "Rp", 4, [128, 4, 128], BF16)
    vn_p = Rot(S, "vnp", 4, [128, 4, 128], BF16)
    ot_p = Rot(S, "otp", 3, [128, 4, 128], F32)
    osb = {(d, r): S.sbuf([128, NVH, 128], F32, f"osb{d}{r}") for d in range(2) for r in range(2)}

    def blk_of(d, s):
        return s if d == 0 else NB - 1 - s

    def pre(d, s):
        blk = blk_of(d, s)
        sl = slots[(d, s % NSLOT)]
        nm = sl["nm"]
        c0 = blk * 128
        Mincl, Mrev = (LE, GT) if d == 0 else (GE, LT)
        MstrU, MinclU = (LT, LE) if d == 0 else (GT, GE)
        tb = (c0 // TT0) * TT0
        S.dma("sp", sl["kT"][:], kTn_s[:, :, c0:c0 + 128], [("kTn", tb, hh) for hh in range(NKH)], [nm + "kT"], sem=nm + "kT")
        S.dma("sp", sl["qT"][:], qTn_s[:, :, c0:c0 + 128], [("qTn", tb, hh) for hh in range(NKH)], [nm + "qT"], sem=nm + "qT")
        S.dma("sp", sl["ktok"][:], ktok_s[c0:c0 + 128, :, :], [("ktok", tb, hh) for hh in range(NKH)], [nm + "ktok"], sem=nm + "ktok")
        S.dma("sp", sl["vtok"][:], vtok_s[c0:c0 + 128, :, :], [("vtok", tb, hh) for hh in range(NVH)], [nm + "vtok"], sem=nm + "vtok")
        S.dma("sp", sl["bg"][:], bg_s[c0:c0 + 128, :], [("bg", (c0 // TT0) * TT0)], [nm + "bg"], sem=nm + "bg")
        bg, sc = sl["bg"], sl["sc"]
        gsel = bg[:, 16 + d * 8:16 + d * 8 + 8]
        gm2, gm2k = gm2_p.next()
        S.ts("pool", gm2[:, 0:8], gsel, c01[:, 0:1], None, ALU.mult, None, [nm + "bg", "c01"], [gm2k])
        S.ts("pool", gm2[:, 8:16], gsel, c01[:, 1:2], None, ALU.mult, None, [nm + "bg", "c01"], [gm2k])
        pss, psk = psS.next()
        pv = pss[:].rearrange("p a b -> p (a b)")
        S.mm(pv[:, 0:8], cm[:, Mincl, :], gsel, True, True, ["cm", nm + "bg"], [psk])
        S.mm(pv[:, 8:16], cm[:, Mrev, :], gsel, True, True, ["cm", nm + "bg"], [psk])
        S.mm(pv[:, 16:32], onesf[:], gm2[:], True, True, ["onesf", gm2k], [psk])
        S.act(sc[:, 0:32], pv[:, 0:32], AF.Exp, [psk], [nm + "sc"])
        S.ts("pool", sc[:, 32:40], sc[:, 0:8], -1.0, None, ALU.mult, None, [nm + "sc"], [nm + "sc"])
        S.ts("pool", sc[:, 40:48], bg[:, d * 8:d * 8 + 8], -1.0, None, ALU.mult, None, [nm + "bg"], [nm + "sc"])
        pG, pGk = psA.next()
        pP, pPk = psA.next()
        for kh in range(NKH):
            S.mm(pG[:, kh, :], sl["kT"][:, kh, :], sl["kT"][:, kh, :], True, True, [nm + "kT"], [pGk])
            S.mm(pP[:, kh, :], sl["kT"][:, kh, :], sl["qT"][:, kh, :], True, True, [nm + "kT", nm + "qT"], [pPk])
        gp, gpk = GsPs_p.next()
        S.tt("dve", gp[:, 0, :, :], pG[:], cm[:, MstrU:MstrU + 1, :].broadcast_to([128, NKH, 128]), ALU.mult,
             [pGk, "cm"], [(gpk, 0)])
        S.tt("dve", gp[:, 1, :, :], pP[:], cm[:, MinclU:MinclU + 1, :].broadcast_to([128, NKH, 128]), ALU.mult,
             [pPk, "cm"], [(gpk, 1)])
        for hg in range(2):
            gm4, gm4k = Gm4_p.next()
            pD, pDk = psA.next()
            for q in range(4):
                vh = 4 * hg + q
                S.ts("pool", gm4[:, q, :], cm[:, Mincl, :], bg[:, 16 + d * 8 + vh:16 + d * 8 + vh + 1], None, ALU.mult, None,
                     ["cm", nm + "bg"], [(gm4k, q)])
                S.mm(pD[:, q, :], cm[:, Mrev, :], gm4[:, q, :], True, True, ["cm", (gm4k, q)], [pDk])
            e4, e4k = E4_p.next()
            S.act(e4[:], pD[:], AF.Exp, [pDk], [e4k])
            zu, zuk = zb_p.next()
            for q in range(4):
                vh = 4 * hg + q
                kh = vh // 2
                S.tt("pool", sl["AT"][:, vh, :], e4[:, q, :], gp[:, 1, kh, :], ALU.mult, [e4k, (gpk, 1)], [(nm + "AT", vh)])
                S.ts("pool", e4[:, q, :], e4[:, q, :], sc[:, 40 + vh:41 + vh], None, ALU.mult, None, [e4k, nm + "sc"], [e4k])
                S.tt("pool", zu[:, q, :], e4[:, q, :], gp[:, 0, kh, :], ALU.mult, [e4k, (gpk, 0)], [(zuk, q)])
                S.ts("pool", sl["kdec"][:, vh, :], sl["ktok"][:, kh, :], sc[:, 8 + vh:9 + vh], None, ALU.mult, None,
                     [nm + "ktok", nm + "sc"], [(nm + "kdec", vh)])
            pt, ptk = psT.next()
            for q in range(4):
                S.tr(pt[:, q, :], zu[:, q, :], idb[:], [(zuk, q), "idb"], [ptk])
            z, zk = zb_p.next()
            S.copy("act", z[:], pt[:, 0:4, :], [ptk], [zk])
            zuk_all = [(zuk, q) for q in range(4)]
            tu, tuk = zb_p.next()
            tl, tlk = zb_p.next()
            idbb = idb[:].unsqueeze(1).broadcast_to([128, 4, 128])
            S.tt("pool", tu[:], zu[:], idbb, ALU.add, zuk_all + ["idb"], [tuk])
            S.tt("pool", tl[:], z[:], idbb, ALU.add, [zk, "idb"], [tlk])
            zp, zpk, zup, zupk = z, [zk], zu, zuk_all
            for k in range(1, 6):
                pB, pBk = psA.next()
                for q in range(4):
                    S.mm(pB[:, q, :], zp[:, q, :], zup[:, q, :], True, True, zpk + zupk, [pBk])
                if k < 5:
                    pA_, pAk = psA.next()
                    for q in range(4):
                        S.mm(pA_[:, q, :], zup[:, q, :], zp[:, q, :], True, True, zpk + zupk, [pAk])
                nzup, nzupk = zb_p.next()
                S.copy("dve", nzup[:], pB[:], [pBk], [nzupk])
                if k < 5:
                    nzp, nzpk = zb_p.next()
                    S.copy("act", nzp[:], pA_[:], [pAk], [nzpk])
                pC, pCk = psA.next()
                for q in range(4):
                    S.mm(pC[:, q, :], tl[:, q, :], nzup[:, q, :], True, True, [tlk, nzupk], [pCk])
                if k < 5:
                    pE, pEk = psA.next()
                    for q in range(4):
                        S.mm(pE[:, q, :], tu[:, q, :], nzp[:, q, :], True, True, [tuk, nzpk], [pEk])
                if k < 5:
                    ntu, ntuk = zb_p.next()
                    S.tt("dve", ntu[:], pC[:], tu[:], ALU.add, [pCk, tuk], [ntuk])
                    ntl, ntlk = zb_p.next()
                    S.tt("dve", ntl[:], pE[:], tl[:], ALU.add, [pEk, tlk], [ntlk])
                    tu, tuk, tl, tlk = ntu, ntuk, ntl, ntlk
                    zp, zpk, zup, zupk = nzp, [nzpk], nzup, [nzupk]
                else:
                    S.tt("dve", sl["TU"][:, 4 * hg:4 * hg + 4, :], pC[:], tu[:], ALU.add, [pCk, tuk],
                         [(nm + "TU", 4 * hg + q) for q in range(4)])

    def loop(s):
        for hi in range(2):
            groups = []
            for d in range(2):
                sl = slots[(d, s % NSLOT)]
                hf = hi if d == 0 else 1 - hi
                for hg in range(2):
                    groups.append((d, hg, sl, hf, slice(64 * hf, 64 * hf + 64), sl["nm"]))
            ksl = []
            for d, hg, sl, hf, pr, nm in groups:
                bank, bk = next_half(hf)
                for q in range(4):
                    vh = 4 * hg + q
                    S.mm(bank[pr, q, :], sl["kT"][:, vh // 2, 64 * hf:64 * hf + 64], stbf[(d, hg)][:, q, :], True, True,
                         [nm + "kT", ("stbf", d, hg)], [bk])
                ksl.append((bank, bk))
            Rl = []
            for (d, hg, sl, hf, pr, nm), (bank, bk) in zip(groups, ksl):
                R, Rk = R_p.next()
                for q in range(4):
                    vh = 4 * hg + q
                    S.stt("dve", R[pr, q, :], bank[pr, q, :], sl["sc"][pr, 32 + vh:33 + vh], sl["vtok"][pr, vh, :],
                          ALU.mult, ALU.add, [bk, nm + "sc", nm + "vtok"], [Rk])
                Rl.append((R, Rk))
            vl = []
            for (d, hg, sl, hf, pr, nm), (R, Rk) in zip(groups, Rl):
                bank, bk = next_half(hf)
                for q in range(4):
                    vh = 4 * hg + q
                    S.mm(bank[pr, q, :], sl["TU"][pr, vh, 64 * hf:64 * hf + 64], R[pr, q, :], True, True,
                         [(nm + "TU", vh), Rk], [bk])
                vn, vnk = vn_p.next()
                for q in range(4):
                    vh = 4 * hg + q
                    S.act(vn[pr, q, :], bank[pr, q, :], AF.Copy, [bk, nm + "bg"], [vnk],
                          scale=sl["bg"][pr, d * 8 + vh:d * 8 + vh + 1])
                vl.append((vn, vnk))
            for (d, hg, sl, hf, pr, nm), (vn, vnk) in zip(groups, vl):
                ob = osb[(d, s % 2)]
                b1, b1k = next_half(hf)
                for q in range(4):
                    vh = 4 * hg + q
                    S.mm(b1[pr, q, :], sl["AT"][pr, vh, 64 * hf:64 * hf + 64], vn[pr, q, :], True, True,
                         [(nm + "AT", vh), vnk], [b1k])
                b2, b2k = next_half(hf)
                for q in range(4):
                    vh = 4 * hg + q
                    S.mm(b2[pr, q, :], sl["qT"][:, vh // 2, 64 * hf:64 * hf + 64], stbf[(d, hg)][:, q, :], True, True,
                         [nm + "qT", ("stbf", d, hg)], [b2k])
                b3, b3k = next_whole()
                for q in range(4):
                    vh = 4 * hg + q
                    S.mm(b3[:, q, :], sl["kdec"][pr, vh, :], vn[pr, q, :], True, True, [(nm + "kdec", vh), vnk], b3k)
                ot, otk = ot_p.next()
                S.copy("act", ot[pr, :, :], b1[pr, :, :], [b1k], [otk])
                for q in range(4):
                    vh = 4 * hg + q
                    S.stt("dve", ob[pr, vh, :], b2[pr, q, :], sl["sc"][pr, vh:vh + 1], ot[pr, q, :], ALU.mult, ALU.add,
                          [b2k, nm + "sc", otk], [("osb", d, s % 2, vh, hf)])
                for q in range(4):
                    vh = 4 * hg + q
                    S.stt("dve", st32[(d, hg)][:, q, :], st32[(d, hg)][:, q, :],
                          sl["sc"][:, 16 + hf * 8 + vh:17 + hf * 8 + vh], b3[:, q, :], ALU.mult, ALU.add,
                          [("st32", d, hg), nm + "sc"] + b3k, [("st32", d, hg)])
                S.copy("act", stbf[(d, hg)][:], st32[(d, hg)][:], [("st32", d, hg)], [("stbf", d, hg)])
        for d in range(2):
            blk = blk_of(d, s)
            S.dma("sp", o_s[d, blk * 128:(blk + 1) * 128, :, :], osb[(d, s % 2)][:],
                  [("osb", d, s % 2, vh, hf) for vh in range(NVH) for hf in range(2)], [("o_s", d, blk)], sem=("osb", d, s % 2))

    if 1 in stages:
        for dd in range(2):
            for hg in range(2):
                S.memset("pool", st32[(dd, hg)][:], 0.0, [("st32", dd, hg)])
                S.memset("pool", stbf[(dd, hg)][:], 0.0, [("stbf", dd, hg)])
        for d in range(2):
            pre(d, 0)
        for s in range(NB):
            if s + 1 < NB:
                for d in range(2):
                    pre(d, s + 1)
            loop(s)

    if 2 in stages:
        of_p = Rot(S, "s2of", 2, [128, NVH, 128], F32)
        ob_p = Rot(S, "s2ob", 2, [128, NVH, 128], F32)
        on_p = Rot(S, "s2on", 2, [128, NVH, 128], BF16)
        z_p = Rot(S, "s2z", 2, [128, NVH, 128], F32)
        ss_p = Rot(S, "s2ss", 2, [128, NVH], F32)
        for blk in range(NB):
            c0 = blk * 128
            of, ofk = of_p.next()
            ob2, obk = ob_p.next()
            S.dma("sp", of[:], o_s[0, c0:c0 + 128, :, :], [("o_s", 0, blk)], [ofk], sem=ofk)
            S.dma("sp", ob2[:], o_s[1, c0:c0 + 128, :, :], [("o_s", 1, blk)], [obk], sem=obk)
            zt, ztk = z_p.next()
            S.dma("sp", zt[:], zT_d.rearrange("(h p) s -> p h s", p=128)[:, :, c0:c0 + 128], (), [ztk], sem=ztk)
            S.tt("pool", of[:], of[:], ob2[:], ALU.add, [ofk, obk], [ofk])
            sq2, sq2k = ob2, obk
            S.tt("pool", sq2[:], of[:], of[:], ALU.mult, [ofk], [sq2k])
            ss, ssk = ss_p.next()
            S.op("dve", lambda e, ss=ss, sq2=sq2: e.tensor_reduce(ss[:], sq2[:], mybir.AxisListType.X, ALU.add), [sq2k], [ssk])
            S.act(ss[:], ss[:], AF.Sqrt, [ssk, "epsb"], [ssk], bias=epsb[:, 0:1], scale=1.0 / 128)
            S.op("dve", lambda e, ss=ss: e.reciprocal(ss[:], ss[:]), [ssk], [ssk])
            on, onk = on_p.next()
            S.tt("dve", on[:], of[:], ss[:].unsqueeze(2).broadcast_to([128, NVH, 128]), ALU.mult, [ofk, ssk], [onk])
            S.act(zt[:], zt[:], AF.Silu, [ztk], [ztk])
            pt, ptk = psT.next()
            for vh in range(NVH):
                S.tr(pt[:, vh, :], on[:, vh, :], idb[:], [onk, "idb"], [ptk])
            oo, ook = of, ofk
            S.stt("dve", oo[:], pt[:], onw[:, 0:1], zt[:], ALU.mult, ALU.mult, [ptk, "onw", ztk, onk], [ook])
            S.dma("sp", oT_d.rearrange("(h p) s -> p h s", p=128)[:, :, c0:c0 + 128], oo[:], [ook], [], sem=ook)
    S.finish()
    return nc


SEQ = 8192
BATCH = 2
TPC = BATCH * SEQ // NCORES
DN_N = 12416
LRU_N = 4096
_PROG_CACHE = {}


def _c(a):
    return np.ascontiguousarray(a, dtype=np.float32)


def _nw(w):
    return _c(w.reshape(D // 128, 128).T)


def _run(nc, in_maps):
    import os, time
    t0 = time.time()
    r = run_bass_kernel_spmd(nc, in_maps, core_ids=list(range(NCORES))).results
    if os.environ.get("KDEBUG"):
        print(f"[kernel] launch took {time.time() - t0:.1f}s", flush=True)
    return r


def _dense_prog(sig):
    nc = bass.Bass("TRN2", target_bir_lowering=False)
    ops = []
    for o in sig:
        if o[0] == "outproj":
            ops.append(dict(op="outproj", dm=o[1]))
        elif o[0] == "inproj":
            ops.append(dict(op="inproj", n=o[1]))
        else:
            ops.append(dict(op=o[0]))
    dense_phase(nc, TPC, ops)
    return nc


def kernel(x, ffn1_norm, ffn1_w_gate_up, ffn1_w_down, mix_norm, ffn2_norm, ffn2_w_gate_up, ffn2_w_down,
           dn_w_in, dn_conv_w, dn_a_log, dn_dt_bias, dn_out_norm, dn_w_out,
           lru_w_in, lru_conv_w, lru_conv_b, lru_w_gate_a, lru_b_gate_a, lru_w_gate_x, lru_b_gate_x,
           lru_lambda, lru_w_out, final_norm):
    x = np.asarray(x, np.float32)
    depth = ffn1_norm.shape[0]
    seg = lambda c: (c // 4, slice((c % 4) * TPC, (c % 4 + 1) * TPC))
    hT = [_c(x[seg(c)[0], seg(c)[1], :].T) for c in range(NCORES)]
    mix_out = None
    cm, c01 = dn_consts()
    y = None
    for layer in range(depth + 1):
        sig = []
        common = {}
        if layer > 0:
            pl = layer - 1
            j = pl // 2
            wo = dn_w_out[j] if pl % 2 == 0 else lru_w_out[j]
            idx = len(sig)
            sig.append(("outproj", wo.shape[0]))
            common[f"wo{idx}"] = _c(wo)
            oidx = idx
            idx = len(sig)
            sig.append(("ffn",))
            common[f"nw{idx}"] = _nw(ffn2_norm[pl])
            common[f"wgu{idx}"] = _c(ffn2_w_gate_up[pl])
            common[f"wd{idx}"] = _c(ffn2_w_down[pl])
        if layer < depth:
            idx = len(sig)
            sig.append(("ffn",))
            common[f"nw{idx}"] = _nw(ffn1_norm[layer])
            common[f"wgu{idx}"] = _c(ffn1_w_gate_up[layer])
            common[f"wd{idx}"] = _c(ffn1_w_down[layer])
            j = layer // 2
            wi = dn_w_in[j] if layer % 2 == 0 else lru_w_in[j]
            idx = len(sig)
            sig.append(("inproj", wi.shape[1]))
            common[f"nw{idx}"] = _nw(mix_norm[layer])
            common[f"wi{idx}"] = _c(wi)
            pidx = idx
            sig.append(("store_h",))
            hidx = len(sig) - 1
        else:
            idx = len(sig)
            sig.append(("final",))
            common[f"nw{idx}"] = _nw(final_norm)
            fidx = idx
        sig = tuple(sig)
        if sig not in _PROG_CACHE:
            _PROG_CACHE[sig] = _dense_prog(sig)
        in_maps = []
        for c in range(NCORES):
            m = dict(common)
            m["hT_in"] = hT[c]
            if layer > 0:
                m[f"oT{oidx}"] = mix_out[c]
            in_maps.append(m)
        res = _run(_PROG_CACHE[sig], in_maps)
        if layer == depth:
            y = np.empty((BATCH, SEQ, D), np.float32)
            for c in range(NCORES):
                b, sl = seg(c)
                y[b, sl, :] = res[c][f"y{fidx}"].T
            break
        hT = [res[c][f"hT_out{hidx}"] for c in range(NCORES)]
        proj = [np.concatenate([res[b * 4 + s][f"proj{pidx}"] for s in range(4)], axis=1) for b in range(BATCH)]
        del res
        j = layer // 2
        in_maps = []
        if layer % 2 == 0:
            key = ("dn",)
            if key not in _PROG_CACHE:
                nc = bass.Bass("TRN2", target_bir_lowering=False)
                dn_phase(nc, SEQ)
                _PROG_CACHE[key] = nc
            cwv_all = dn_conv_w[j]
            for c in range(NCORES):
                b, g4 = c // 4, c % 4
                P = proj[b]
                kh0, vh0 = 4 * g4, 8 * g4
                ba_rows = lambda kind: np.concatenate([P[12288 + d * 64 + kind * 32 + vh0: 12288 + d * 64 + kind * 32 + vh0 + 8]
                                                       for d in range(2)], axis=0)
                cwl = lambda w, nh: _c(w.reshape(4, nh, 128).transpose(2, 1, 0))
                in_maps.append({
                    "qT": _c(P[kh0 * 128:(kh0 + 4) * 128]), "kT": _c(P[2048 + kh0 * 128:2048 + (kh0 + 4) * 128]),
                    "vT": _c(P[4096 + vh0 * 128:4096 + (vh0 + 8) * 128]), "zT": _c(P[8192 + vh0 * 128:8192 + (vh0 + 8) * 128]),
                    "betaT": _c(ba_rows(0)), "aT": _c(ba_rows(1)),
                    "cwq": cwl(cwv_all[:, kh0 * 128:(kh0 + 4) * 128], 4),
                    "cwk": cwl(cwv_all[:, 2048 + kh0 * 128:2048 + (kh0 + 4) * 128], 4),
                    "cwv": cwl(cwv_all[:, 4096 + vh0 * 128:4096 + (vh0 + 8) * 128], 8),
                    "alog": _c(dn_a_log[j][:, vh0:vh0 + 8].reshape(16, 1)),
                    "dtb": _c(dn_dt_bias[j][:, vh0:vh0 + 8].reshape(16, 1)),
                    "onw": _c(dn_out_norm[j].reshape(128, 1)), "cmask": cm, "c01": c01})
            res = _run(_PROG_CACHE[key], in_maps)
            outs = [res[c]["oT"] for c in range(NCORES)]
        else:
            key = ("lru",)
            if key not in _PROG_CACHE:
                nc = bass.Bass("TRN2", target_bir_lowering=False)
                lru_phase(nc, SEQ)
                _PROG_CACHE[key] = nc
            for c in range(NCORES):
                b, g4 = c // 4, c % 4
                P = proj[b]
                ch = slice(512 * g4, 512 * g4 + 512)
                pl1 = lambda v: _c(v.reshape(4, 128).T)
                pl2 = lambda v: _c(v.reshape(2, 4, 128).transpose(2, 0, 1))
                in_maps.append({
                    "xbT": _c(P[512 * g4:512 * g4 + 512]), "gateT": _c(P[2048 + 512 * g4:2048 + 512 * g4 + 512]),
                    "cw": _c(lru_conv_w[j][:, ch].reshape(4, 4, 128).transpose(2, 1, 0)), "cb": pl1(lru_conv_b[j][ch]),
                    "wga": _c(lru_w_gate_a[j][:, 2 * g4:2 * g4 + 2]), "wgx": _c(lru_w_gate_x[j][:, 2 * g4:2 * g4 + 2]),
                    "bga": pl2(lru_b_gate_a[j][:, ch]), "bgx": pl2(lru_b_gate_x[j][:, ch]), "lam": pl2(lru_lambda[j][:, ch])})
            res = _run(_PROG_CACHE[key], in_maps)
            outs = [res[c]["yT"] for c in range(NCORES)]
        del proj, in_maps
        full = [np.concatenate([outs[b * 4 + g] for g in range(4)], axis=0) for b in range(BATCH)]
        mix_out = [_c(full[c // 4][:, seg(c)[1]]) for c in range(NCORES)]
        del res, outs, full
    return y
```

```python
import contextlib
import numpy as np
import concourse.bass as bass
import concourse.mybir as mybir
from concourse.bass_utils import run_bass_kernel_spmd

F32 = mybir.dt.float32
BF16 = mybir.dt.bfloat16
AF = mybir.ActivationFunctionType
ALU = mybir.AluOpType

D = 2048
FF = 5632
NCORES = 8
EPS = 1e-6
SAME_ENGINE_SYNC = True


class Sched:
    ENG = ("pe", "act", "dve", "pool", "sp")

    def __init__(self, nc):
        self.nc = nc
        self.streams = {e: [] for e in self.ENG}
        self.ccnt = {e: 0 for e in self.ENG}
        self.dcnt = {}
        self.dq = {}
        self.seen = {e: {} for e in self.ENG}
        self.last_w = {}
        self.rd = {}
        self.stack = contextlib.ExitStack()
        self.nbuf = 0
        self.psum_banks = []
        self.psum_i = 0

    def sbuf(self, shape, dt, name=None):
        self.nbuf += 1
        name = "sb_" + (name or f"{self.nbuf}")
        return self.stack.enter_context(self.nc.sbuf_tensor(name, list(shape), dt))

    def psum(self, shape, dt=F32, name=None):
        self.nbuf += 1
        name = "ps_" + (name or f"{self.nbuf}")
        return self.stack.enter_context(self.nc.psum_tensor(name, list(shape), dt))

    def _deps(self, eng, reads, writes):
        deps = []
        for k in reads:
            if k in self.last_w:
                deps.append(self.last_w[k])
        for k in writes:
            if k in self.last_w:
                deps.append(self.last_w[k])
            deps.extend(self.rd.get(k, ()))
        need = {}
        for sk, v in deps:
            if sk == ("c", "pe") and eng == "pe":
                continue
            if sk == ("c", eng) and not SAME_ENGINE_SYNC:
                continue
            if v > need.get(sk, 0):
                need[sk] = v
        for sk, v in need.items():
            if v > self.seen[eng].get(sk, 0):
                self.seen[eng][sk] = v
                self.streams[eng].append(("wait", sk, v))

    def _post(self, tok, reads, writes):
        for k in reads:
            self.rd.setdefault(k, []).append(tok)
        for k in writes:
            self.last_w[k] = tok
            self.rd[k] = []

    def op(self, eng, fn, reads=(), writes=()):
        self._deps(eng, reads, writes)
        self.ccnt[eng] += 1
        tok = (("c", eng), self.ccnt[eng])
        self.streams[eng].append(("op", fn, tok))
        self._post(tok, reads, writes)

    def dma(self, eng, out, in_, reads=(), writes=(), sem=None):
        assert sem is not None
        self._deps(eng, reads, writes)
        sk = ("d", sem)
        prev = self.dcnt.get(sk, 0)
        if prev > self.seen[eng].get(sk, 0):
            self.seen[eng][sk] = prev
            self.streams[eng].append(("wait", sk, prev))
        self.dcnt[sk] = prev + 16
        self.dq.setdefault(eng, set()).add(sk)
        tok = (sk, self.dcnt[sk])
        self.streams[eng].append(("dma", (out, in_), tok))
        self._post(tok, reads, writes)

    def mm(self, ps, lhsT, rhs, start, stop, reads, writes):
        self.op("pe", lambda e: e.matmul(ps, lhsT, rhs, start=start, stop=stop), reads, writes)

    def tr(self, ps, in_, ident, reads, writes):
        self.op("pe", lambda e: e.transpose(ps, in_, ident), reads, writes)

    def act(self, out, in_, func, reads, writes, bias=None, scale=None):
        kw = {}
        if bias is not None:
            kw["bias"] = bias
        if scale is not None:
            kw["scale"] = scale
        self.op("act", lambda e: e.activation(out=out, in_=in_, func=func, **kw), reads, writes)

    def tt(self, eng, out, in0, in1, op, reads, writes):
        self.op(eng, lambda e: e.tensor_tensor(out, in0, in1, op), reads, writes)

    def ts(self, eng, out, in0, s1, s2, op0, op1, reads, writes):
        if s2 is None and eng == "pool" and op0 == ALU.mult:
            self.op(eng, lambda e: e.tensor_scalar(out, in0, s1, 1.0, ALU.mult, ALU.mult), reads, writes)
        elif s2 is None:
            self.op(eng, lambda e: e.tensor_scalar(out, in0, s1, None, op0), reads, writes)
        else:
            self.op(eng, lambda e: e.tensor_scalar(out, in0, s1, s2, op0, op1), reads, writes)

    def stt(self, eng, out, in0, scalar, in1, op0, op1, reads, writes):
        self.op(eng, lambda e: e.scalar_tensor_tensor(out, in0, scalar, in1, op0, op1), reads, writes)

    def copy(self, eng, out, in_, reads, writes):
        if eng == "act":
            self.op(eng, lambda e: e.copy(out, in_), reads, writes)
        else:
            self.op(eng, lambda e: e.tensor_copy(out, in_), reads, writes)

    def memset(self, eng, ap, val, writes):
        self.op(eng, lambda e: e.memset(ap, val), (), writes)

    def finish(self):
        nc = self.nc
        st = self.stack
        sems = {}
        for e in self.ENG:
            sems[("c", e)] = st.enter_context(nc.semaphore(f"c_{e}"))
        for i, sk in enumerate(self.dcnt):
            sems[sk] = st.enter_context(nc.semaphore(f"d_{i}"))
        for e, sks in self.dq.items():
            for sk in sks:
                if self.dcnt[sk] > self.seen[e].get(sk, 0):
                    self.streams[e].append(("wait", sk, self.dcnt[sk]))
        streams = self.streams

        def replay(name, eng):
            for item in streams[name]:
                if item[0] == "wait":
                    eng.wait_ge(sems[item[1]], item[2])
                elif item[0] == "op":
                    item[1](eng).then_inc(sems[item[2][0]], 1)
                else:
                    o, i = item[1]
                    eng.dma_start(out=o, in_=i).then_inc(sems[item[2][0]], 16)

        with nc.Block() as block:
            @block.tensor
            def _(e):
                replay("pe", e)

            @block.scalar
            def _(e):
                replay("act", e)

            @block.vector
            def _(e):
                replay("dve", e)

            @block.gpsimd
            def _(e):
                replay("pool", e)

            @block.sync
            def _(e):
                replay("sp", e)
        st.close()


class Rot:
    def __init__(self, S, name, n, shape, dt, psum=False):
        self.tiles = [(S.psum(shape, dt, f"{name}{i}") if psum else S.sbuf(shape, dt, f"{name}{i}")) for i in range(n)]
        self.keys = [(name, i) for i in range(n)]
        self.i = 0

    def next(self):
        t, k = self.tiles[self.i], self.keys[self.i]
        self.i = (self.i + 1) % len(self.tiles)
        return t, k


def dense_phase(nc, T, ops, TT=512):
    S = Sched(nc)
    KC = D // 128
    FC = FF // 128
    NT = T // TT
    dram = {}

    def din(name, shape):
        dram[name] = nc.dram_tensor(name, list(shape), F32, kind="ExternalInput").ap()
        return dram[name]

    def dout(name, shape):
        dram[name] = nc.dram_tensor(name, list(shape), F32, kind="ExternalOutput").ap()
        return dram[name]

    hT_in = din("hT_in", [D, T])
    need_h_out = any(o["op"] == "store_h" for o in ops)
    for idx, o in enumerate(ops):
        if o["op"] == "ffn":
            o["nw"] = din(f"nw{idx}", [128, KC])
            o["wgu"] = din(f"wgu{idx}", [D, 2 * FF])
            o["wd"] = din(f"wd{idx}", [FF, D])
        elif o["op"] == "outproj":
            o["oT"] = din(f"oT{idx}", [o["dm"], T])
            o["wo"] = din(f"wo{idx}", [o["dm"], D])
        elif o["op"] == "inproj":
            o["nw"] = din(f"nw{idx}", [128, KC])
            o["wi"] = din(f"wi{idx}", [D, o["n"]])
            o["out"] = dout(f"proj{idx}", [o["n"], T])
        elif o["op"] == "final":
            o["nw"] = din(f"nw{idx}", [128, KC])
            o["out"] = dout(f"y{idx}", [D, T])
        elif o["op"] == "store_h":
            o["out"] = dout(f"hT_out{idx}", [D, T])

    h = S.sbuf([128, KC, TT], F32, "h")
    xn = S.sbuf([128, KC, TT], BF16, "xn")
    actb = S.sbuf([128, FC, TT], BF16, "actb")
    ones = S.sbuf([128, 128], BF16, "ones")
    nws = {}
    for idx, o in enumerate(ops):
        if "nw" in o:
            nws[idx] = S.sbuf([128, KC], F32, f"nwt{idx}")
    WG = 256
    wgu_pool = Rot(S, "wgu", 2, [128, 2, KC, WG], BF16)
    wd_pool = Rot(S, "wd", 2, [128, FC, WG], BF16)
    sq_pool = Rot(S, "sq", 2, [128, TT], BF16)
    tmp_pool = Rot(S, "tmp", 3, [128, TT], F32)
    rstd = S.sbuf([128, TT], F32, "rstd")
    psp = Rot(S, "psb", 8, [128, TT], F32, psum=True)

    S.memset("dve", ones[:], 1.0, ["ones"])
    for idx in nws:
        S.dma("sp", nws[idx][:], ops[idx]["nw"], (), [("nw", idx)], sem=("nw", idx))

    def rmsnorm(nwidx, out_fp32_cb=None):
        ss, ssk = psp.next()
        for c in range(KC):
            sq, sqk = sq_pool.next()
            S.act(sq[:], h[:, c, :], AF.Square, [("h", c)], [sqk])
            S.mm(ss[:], ones[:], sq[:], c == 0, c == KC - 1, ["ones", sqk], [ssk])
        S.act(rstd[:], ss[:], AF.Sqrt, [ssk], ["rstd"], bias=EPS_AP[0][:, 0:1], scale=1.0 / D)
        S.op("dve", lambda e: e.reciprocal(rstd[:], rstd[:]), ["rstd"], ["rstd"])
        nwt = nws[nwidx]
        for c in range(KC):
            if out_fp32_cb is None:
                S.stt("dve", xn[:, c, :], h[:, c, :], nwt[:, c:c + 1], rstd[:], ALU.mult, ALU.mult,
                      [("h", c), ("nw", nwidx), "rstd"], [("xn", c)])
            else:
                out_fp32_cb(c, nwt)

    EPS_AP = [S.sbuf([128, 1], F32, "epsb")]
    S.memset("dve", EPS_AP[0][:], EPS, ["epsb"])

    def linear_acc(w_ap, kc, ncols, x_tile, xkeys, w_pool_kind, epilogue):
        for g0 in range(0, ncols, WG):
            gw = min(WG, ncols - g0)
            wt, wk = wd_pool.next()
            src = w_ap[:, g0:g0 + gw].rearrange("(c p) n -> p c n", p=128)
            half = (kc + 1) // 2
            S.dma("pool", wt[:, 0:half, 0:gw], src[:, 0:half, :], (), [(wk, 0)], sem=(wk, 0))
            if kc > half:
                S.dma("pool", wt[:, half:kc, 0:gw], src[:, half:kc, :], (), [(wk, 1)], sem=(wk, 1))
            for j in range(gw // 128):
                ps, pk = psp.next()
                for c in range(kc):
                    S.mm(ps[:], wt[:, c, j * 128:(j + 1) * 128], x_tile[:, c, :], c == 0, c == kc - 1,
                         [(wk, 0 if c < half else 1), xkeys(c)], [pk])
                epilogue(g0 // 128 + j, ps, pk)

    for t in range(NT):
        tsl = slice(t * TT, (t + 1) * TT)
        hv = hT_in.rearrange("(c p) t -> p c t", p=128)
        for c0 in range(0, KC, 4):
            S.dma("sp", h[:, c0:c0 + 4, :], hv[:, c0:c0 + 4, tsl], (), [("h", c) for c in range(c0, c0 + 4)], sem=("h", c0))
        for idx, o in enumerate(ops):
            if o["op"] == "ffn":
                rmsnorm(idx)
                for g0 in range(0, FF, WG):
                    wt, wk = wgu_pool.next()
                    for s in range(2):
                        src = o["wgu"][:, s * FF + g0: s * FF + g0 + WG].rearrange("(c p) n -> p c n", p=128)
                        S.dma("pool", wt[:, s, :, :], src, (), [(wk, s)], sem=(wk, s))
                    for j in range(WG // 128):
                        fc = g0 // 128 + j
                        gp, gk = psp.next()
                        up, uk = psp.next()
                        for c in range(KC):
                            S.mm(gp[:], wt[:, 0, c, j * 128:(j + 1) * 128], xn[:, c, :], c == 0, c == KC - 1,
                                 [(wk, 0), ("xn", c)], [gk])
                        for c in range(KC):
                            S.mm(up[:], wt[:, 1, c, j * 128:(j + 1) * 128], xn[:, c, :], c == 0, c == KC - 1,
                                 [(wk, 1), ("xn", c)], [uk])
                        tm, tk = tmp_pool.next()
                        S.act(tm[:], gp[:], AF.Silu, [gk], [tk])
                        S.tt("dve", actb[:, fc, :], tm[:], up[:], ALU.mult, [tk, uk], [("act", fc)])

                def epi_down(dc, ps, pk):
                    S.stt("dve", h[:, dc, :], ps[:], 0.5, h[:, dc, :], ALU.mult, ALU.add,
                          [pk, ("h", dc)], [("h", dc)])
                linear_acc(o["wd"], FC, D, actb, lambda c: ("act", c), "wd", epi_down)
            elif o["op"] == "outproj":
                kc = o["dm"] // 128
                ov = o["oT"].rearrange("(c p) t -> p c t", p=128)
                for c0 in range(0, kc, 8):
                    S.dma("pool", actb[:, c0:c0 + 8, :], ov[:, c0:c0 + 8, tsl], (),
                          [("act", c) for c in range(c0, c0 + 8)], sem=("actb", c0))

                def epi_out(dc, ps, pk):
                    S.tt("dve", h[:, dc, :], ps[:], h[:, dc, :], ALU.add, [pk, ("h", dc)], [("h", dc)])
                linear_acc(o["wo"], kc, D, actb, lambda c: ("act", c), "wd", epi_out)
            elif o["op"] == "inproj":
                rmsnorm(idx)
                outv = o["out"]

                def epi_in(nc_, ps, pk, outv=outv):
                    tm, tk = tmp_pool.next()
                    S.copy("act", tm[:], ps[:], [pk], [tk])
                    S.dma("sp", outv[nc_ * 128:(nc_ + 1) * 128, tsl], tm[:], [tk], [], sem=tk)
                linear_acc(o["wi"], KC, o["n"], xn, lambda c: ("xn", c), "wd", epi_in)
            elif o["op"] == "final":
                outv = o["out"].rearrange("(c p) t -> p c t", p=128)

                def cb(c, nwt, outv=outv, idx=idx):
                    tm, tk = tmp_pool.next()
                    S.stt("dve", tm[:], h[:, c, :], nwt[:, c:c + 1], rstd[:], ALU.mult, ALU.mult,
                          [("h", c), ("nw", idx), "rstd"], [tk])
                    S.dma("sp", outv[:, c, tsl], tm[:], [tk], [], sem=tk)
                rmsnorm(idx, cb)
            elif o["op"] == "store_h":
                outv = o["out"].rearrange("(c p) t -> p c t", p=128)
                for c0 in range(0, KC, 4):
                    S.dma("sp", outv[:, c0:c0 + 4, tsl], h[:, c0:c0 + 4, :],
                          [("h", c) for c in range(c0, c0 + 4)], [], sem=("h", c0))
    S.finish()
    return nc


def lru_phase(nc, Sq, TT=512):
    S = Sched(nc)
    NT = Sq // TT
    NCH = 4

    def din(name, shape):
        return nc.dram_tensor(name, list(shape), F32, kind="ExternalInput").ap()

    xbT = din("xbT", [512, Sq])
    gateT = din("gateT", [512, Sq])
    cw_d = din("cw", [128, NCH, 4])
    cb_d = din("cb", [128, NCH])
    wga_d = din("wga", [2, 2, 256, 256])
    wgx_d = din("wgx", [2, 2, 256, 256])
    bga_d = din("bga", [128, 2, NCH])
    bgx_d = din("bgx", [128, 2, NCH])
    lam_d = din("lam", [128, 2, NCH])
    yT = nc.dram_tensor("yT", [512, Sq], F32, kind="ExternalOutput").ap()

    cw = S.sbuf([128, NCH, 4], F32, "cw")
    cb = S.sbuf([128, NCH], F32, "cb")
    bga = S.sbuf([128, 2, NCH], F32, "bga")
    bgx = S.sbuf([128, 2, NCH], F32, "bgx")
    lam = S.sbuf([128, 2, NCH], F32, "lam")
    nsp8 = S.sbuf([128, 2, NCH], F32, "nsp8")
    wga = S.sbuf([128, 8, 256], BF16, "wga")
    wgx = S.sbuf([128, 8, 256], BF16, "wgx")
    carry = S.sbuf([128, NCH], F32, "carry")
    S.dma("sp", cw[:], cw_d, (), ["cw"], sem="cw")
    S.dma("sp", cb[:], cb_d, (), ["cb"], sem="cb")
    S.dma("sp", bga[:], bga_d, (), ["bga"], sem="bga")
    S.dma("sp", bgx[:], bgx_d, (), ["bgx"], sem="bgx")
    S.dma("sp", lam[:], lam_d, (), ["lam"], sem="lam")
    S.dma("pool", wga[:], wga_d.rearrange("d b (ic p) j -> p (d b ic) j", p=128), (), ["wga"], sem="wga")
    S.dma("pool", wgx[:], wgx_d.rearrange("d b (ic p) j -> p (d b ic) j", p=128), (), ["wgx"], sem="wgx")
    S.act(nsp8[:], lam[:], AF.Exp, ["lam"], ["nsp8"], scale=-1.0)
    S.act(nsp8[:], nsp8[:], AF.Ln, ["nsp8"], ["nsp8"], bias=1.0)
    S.ts("dve", nsp8[:], nsp8[:], -8.0, None, ALU.mult, None, ["nsp8"], ["nsp8"])

    xr_pool = Rot(S, "xr", 2, [128, 2, TT + 3], F32)
    xc_pool = Rot(S, "xc", 2, [128, 2, TT], F32)
    xcb_pool = Rot(S, "xcb", 2, [128, 2, TT], BF16)
    tp = Rot(S, "lt", 12, [128, TT], F32)
    psp = Rot(S, "lps", 6, [128, TT], F32, psum=True)
    GC = 2.0 * float(np.sqrt(2.0 / np.pi))

    for d in (0, 1):
        order = list(range(NT)) if d == 0 else list(range(NT - 1, -1, -1))
        for ti, t in enumerate(order):
            t0 = t * TT
            for blk in range(2):
                xr, xk = xr_pool.next()
                lo, hi = t0 - 2, t0 + TT + 1
                slo, shi = max(lo, 0), min(hi, Sq)
                if lo < 0 or hi > Sq:
                    S.memset("pool", xr[:], 0.0, [xk])
                src = xbT.rearrange("(c p) s -> p c s", p=128)[:, 2 * blk:2 * blk + 2, slo:shi]
                S.dma("sp", xr[:, :, slo - lo:shi - lo], src, (), [xk], sem=xk)
                xc, xck = xc_pool.next()
                xcb, xcbk = xcb_pool.next()
                for jc in range(2):
                    c = 2 * blk + jc
                    S.ts("dve", xc[:, jc, :], xr[:, jc, 0:TT], cw[:, c, 0:1], cb[:, c:c + 1], ALU.mult, ALU.add,
                         [xk, "cw", "cb"], [(xck, jc)])
                    for j in range(1, 4):
                        S.stt("dve", xc[:, jc, :], xr[:, jc, j:j + TT], cw[:, c, j:j + 1], xc[:, jc, :],
                              ALU.mult, ALU.add, [xk, "cw", (xck, jc)], [(xck, jc)])
                    S.copy("pool", xcb[:, jc, :], xc[:, jc, :], [(xck, jc)], [(xcbk, jc)])
                for jc in range(2):
                    c = 2 * blk + jc
                    pr, prk = psp.next()
                    pi, pik = psp.next()
                    for ic in range(2):
                        S.mm(pr[:], wga[:, d * 4 + blk * 2 + ic, jc * 128:(jc + 1) * 128], xcb[:, ic, :],
                             ic == 0, ic == 1, ["wga", (xcbk, ic)], [prk])
                    for ic in range(2):
                        S.mm(pi[:], wgx[:, d * 4 + blk * 2 + ic, jc * 128:(jc + 1) * 128], xcb[:, ic, :],
                             ic == 0, ic == 1, ["wgx", (xcbk, ic)], [pik])
                    r, rk = tp.next()
                    S.act(r[:], pr[:], AF.Sigmoid, [prk, "bga"], [rk], bias=bga[:, d, c:c + 1])
                    gi, gik = tp.next()
                    S.act(gi[:], pi[:], AF.Sigmoid, [pik, "bgx"], [gik], bias=bgx[:, d, c:c + 1])
                    a, ak = tp.next()
                    S.act(a[:], r[:], AF.Exp, [rk, "nsp8"], [ak], scale=nsp8[:, d, c:c + 1])
                    S.tt("pool", r[:], a[:], a[:], ALU.mult, [ak], [rk])
                    S.act(r[:], r[:], AF.Sqrt, [rk], [rk], bias=1.0, scale=-1.0)
                    S.tt("pool", gi[:], gi[:], xc[:, jc, :], ALU.mult, [gik, (xck, jc)], [gik])
                    S.tt("dve", gi[:], gi[:], r[:], ALU.mult, [gik, rk], [gik])
                    hh, hk = tp.next()
                    init = 0.0 if ti == 0 else carry[:, c:c + 1]
                    ckey = ("carry", c)
                    if d == 0:
                        S.op("dve", lambda e, hh=hh, a=a, gi=gi, init=init: e.tensor_tensor_scan(
                            hh[:], a[:], gi[:], init, ALU.mult, ALU.add), [ak, gik, ckey], [hk])
                        S.copy("pool", carry[:, c:c + 1], hh[:, TT - 1:TT], [hk], [ckey])
                        S.dma("sp", yT[c * 128:(c + 1) * 128, t0:t0 + TT], hh[:], [hk], [("y", c, t)], sem=hk)
                    else:
                        S.op("dve", lambda e, hh=hh, a=a, gi=gi, init=init: e.tensor_tensor_scan(
                            hh[:, ::-1], a[:, ::-1], gi[:, ::-1], init, ALU.mult, ALU.add), [ak, gik, ckey], [hk])
                        S.copy("pool", carry[:, c:c + 1], hh[:, 0:1], [hk], [ckey])
                        hf, hfk = tp.next()
                        S.dma("sp", hf[:], yT[c * 128:(c + 1) * 128, t0:t0 + TT], [("y", c, t)], [hfk], sem=hfk)
                        g, gk = tp.next()
                        S.dma("sp", g[:], gateT[c * 128:(c + 1) * 128, t0:t0 + TT], (), [gk], sem=gk)
                        u, uk = tp.next()
                        S.act(u[:], g[:], AF.Square, [gk], [uk])
                        S.ts("pool", u[:], u[:], 0.044715, 1.0, ALU.mult, ALU.add, [uk], [uk])
                        S.tt("pool", u[:], u[:], g[:], ALU.mult, [uk, gk], [uk])
                        S.act(u[:], u[:], AF.Sigmoid, [uk], [uk], scale=GC)
                        S.tt("pool", u[:], u[:], g[:], ALU.mult, [uk, gk], [uk])
                        S.tt("dve", hh[:], hh[:], hf[:], ALU.add, [hk, hfk], [hk])
                        S.tt("dve", hh[:], hh[:], u[:], ALU.mult, [hk, uk], [hk])
                        S.dma("sp", yT[c * 128:(c + 1) * 128, t0:t0 + TT], hh[:], [hk], [("y", c, t)], sem=hk)
    S.finish()
    return nc


def dn_consts():
    m = np.arange(128)[:, None]
    i = np.arange(128)[None, :]
    same = (m // 64) == (i // 64)
    cm = np.stack([(m <= i) & same, (m < i) & same, (m >= i) & same, (m > i) & same, m == i]).astype(np.float32)
    cm = np.ascontiguousarray(cm.transpose(1, 0, 2))
    c01 = np.stack([(np.arange(128) < 64), (np.arange(128) >= 64)], axis=1).astype(np.float32)
    return cm, np.ascontiguousarray(c01)


def dn_phase(nc, Sq, stages=(0, 1, 2), dbg=False):
    S = Sched(nc)
    NB = Sq // 128
    TT0 = 512 if Sq >= 512 else Sq
    NKH, NVH = 4, 8
    LE, LT, GE, GT, ID = 0, 1, 2, 3, 4

    def din(name, shape):
        return nc.dram_tensor(name, list(shape), F32, kind="ExternalInput").ap()

    qT_d = din("qT", [NKH * 128, Sq])
    kT_d = din("kT", [NKH * 128, Sq])
    vT_d = din("vT", [NVH * 128, Sq])
    zT_d = din("zT", [NVH * 128, Sq])
    beta_d = din("betaT", [16, Sq])
    a_d = din("aT", [16, Sq])
    cwq_d = din("cwq", [128, NKH, 4])
    cwk_d = din("cwk", [128, NKH, 4])
    cwv_d = din("cwv", [128, NVH, 4])
    alog_d = din("alog", [16, 1])
    dtb_d = din("dtb", [16, 1])
    onw_d = din("onw", [128, 1])
    cm_d = din("cmask", [128, 5, 128])
    c01_d = din("c01", [128, 2])
    oT_d = nc.dram_tensor("oT", [NVH * 128, Sq], F32, kind="ExternalOutput").ap()
    skind = "ExternalOutput" if dbg else "Internal"
    kTn_s = nc.dram_tensor("kTn_s", [128, NKH, Sq], BF16, kind=skind).ap()
    qTn_s = nc.dram_tensor("qTn_s", [128, NKH, Sq], BF16, kind=skind).ap()
    ktok_s = nc.dram_tensor("ktok_s", [Sq, NKH, 128], BF16, kind=skind).ap()
    vtok_s = nc.dram_tensor("vtok_s", [Sq, NVH, 128], BF16, kind=skind).ap()
    bg_s = nc.dram_tensor("bg_s", [Sq, 32], F32, kind=skind).ap()
    o_s = nc.dram_tensor("o_s", [2, Sq, NVH, 128], F32, kind=skind).ap()

    cm = S.sbuf([128, 5, 128], F32, "cm")
    c01 = S.sbuf([128, 2], F32, "c01")
    cwq = S.sbuf([128, NKH, 4], F32, "cwq")
    cwk = S.sbuf([128, NKH, 4], F32, "cwk")
    cwv = S.sbuf([128, NVH, 4], F32, "cwv")
    alog = S.sbuf([16, 1], F32, "alog")
    dtb = S.sbuf([16, 1], F32, "dtb")
    onw = S.sbuf([128, 1], F32, "onw")
    for t, dsrc, nm in ((cm, cm_d, "cm"), (c01, c01_d, "c01"), (cwq, cwq_d, "cwq"), (cwk, cwk_d, "cwk"),
                        (cwv, cwv_d, "cwv"), (alog, alog_d, "alog"), (dtb, dtb_d, "dtb"), (onw, onw_d, "onw")):
        S.dma("sp", t[:], dsrc, (), [nm], sem=nm)
    idb = S.sbuf([128, 128], BF16, "idb")
    S.copy("dve", idb[:], cm[:, ID, :], ["cm"], ["idb"])
    onesb = S.sbuf([128, 128], BF16, "onesb")
    S.memset("dve", onesb[:], 1.0, ["onesb"])
    onesf = S.sbuf([128, 128], F32, "onesf")
    S.memset("dve", onesf[:], 1.0, ["onesf"])
    epsb = S.sbuf([128, 1], F32, "epsb")
    S.memset("dve", epsb[:], EPS, ["epsb"])
    negA = S.sbuf([16, 1], F32, "negA")
    S.act(negA[:], alog[:], AF.Exp, ["alog"], ["negA"])
    S.ts("dve", negA[:], negA[:], -1.0, None, ALU.mult, None, ["negA"], ["negA"])

    psA = Rot(S, "dpa", 3, [128, 4, 128], F32, psum=True)
    psT = Rot(S, "dpt", 1, [128, 8, 128], BF16, psum=True)
    psS = Rot(S, "dps", 1, [128, 4, 32], F32, psum=True)
    NLB = 3
    psL_t = [S.psum([128, 4, 128], F32, f"dpl{i}") for i in range(NLB)]
    psL_i = [0, 0, 0]

    def next_half(hf):
        i = psL_i[0]
        psL_i[0] = (i + 1) % NLB
        return psL_t[i], ("dpl", i, hf)

    def next_whole():
        i = psL_i[0]
        psL_i[0] = (i + 1) % NLB
        return psL_t[i], [("dpl", i, 0), ("dpl", i, 1)]

    raw_p = Rot(S, "s0raw", 3, [128, TT0 + 3], F32)
    x_p = Rot(S, "s0x", 3, [128, TT0], F32)
    sq_p = Rot(S, "s0sq", 2, [128, TT0], BF16)
    rs_p = Rot(S, "s0rs", 2, [128, TT0], F32)
    xn_p = Rot(S, "s0xn", 3, [128, TT0], BF16)
    tk_p = Rot(S, "s0tk", 3, [128, TT0 // 128, 128], BF16)
    sc_p = Rot(S, "s0sc", 2, [16, TT0], F32)
    bgo_p = Rot(S, "s0bgo", 2, [128, TT0 // 128, 32], F32)
    QS = float(128 ** -0.5)
    nb0 = TT0 // 128
    if 0 in stages:
        for t0 in range(0, Sq, TT0):
            for kind, src_d, cwt, nh in (("k", kT_d, cwk, NKH), ("q", qT_d, cwq, NKH), ("v", vT_d, cwv, NVH)):
                for hh in range(nh):
                    raw, rk = raw_p.next()
                    lo, hi = t0 - 2, t0 + TT0 + 1
                    slo, shi = max(lo, 0), min(hi, Sq)
                    if lo < 0 or hi > Sq:
                        S.memset("pool", raw[:], 0.0, [rk])
                    S.dma("sp", raw[:, slo - lo:shi - lo], src_d[hh * 128:(hh + 1) * 128, slo:shi], (), [rk], sem=rk)
                    x, xk = x_p.next()
                    S.ts("dve", x[:], raw[:, 0:TT0], cwt[:, hh, 0:1], None, ALU.mult, None, [rk, "cw" + kind], [xk])
                    for j in range(1, 4):
                        S.stt("dve", x[:], raw[:, j:j + TT0], cwt[:, hh, j:j + 1], x[:], ALU.mult, ALU.add,
                              [rk, "cw" + kind, xk], [xk])
                    xn, xnk = xn_p.next()
                    if kind == "v":
                        S.act(xn[:], x[:], AF.Silu, [xk], [xnk])
                    else:
                        S.act(x[:], x[:], AF.Silu, [xk], [xk])
                        sq, sqk = sq_p.next()
                        S.tt("pool", sq[:], x[:], x[:], ALU.mult, [xk], [sqk])
                        ps, pk = psA.next()
                        psv = ps[:].rearrange("p a b -> p (a b)")
                        S.mm(psv[:, 0:TT0], onesb[:], sq[:], True, True, ["onesb", sqk], [pk])
                        rs, rsk = rs_p.next()
                        S.act(rs[:], psv[:, 0:TT0], AF.Sqrt, [pk, "epsb"], [rsk], bias=epsb[:, 0:1])
                        S.op("dve", lambda e, rs=rs: e.reciprocal(rs[:], rs[:]), [rsk], [rsk])
                        if kind == "q":
                            S.stt("dve", xn[:], x[:], QS, rs[:], ALU.mult, ALU.mult, [xk, rsk], [xnk])
                        else:
                            S.tt("dve", xn[:], x[:], rs[:], ALU.mult, [xk, rsk], [xnk])
                        dst = (kTn_s if kind == "k" else qTn_s)[:, hh, t0:t0 + TT0]
                        S.dma("pool", dst, xn[:], [xnk], [(kind + "Tn", t0, hh)], sem=xnk)
                    if kind in ("k", "v"):
                        pt, ptk = psT.next()
                        for jb in range(nb0):
                            S.tr(pt[:, jb, :], xn[:, jb * 128:(jb + 1) * 128], idb[:], [xnk, "idb"], [ptk])
                        tk, tkk = tk_p.next()
                        S.copy("act", tk[:], pt[:, 0:nb0, :], [ptk], [tkk])
                        dst_t = (ktok_s if kind == "k" else vtok_s).rearrange("(nb p) h d -> p nb h d", p=128)
                        S.dma("act", dst_t[:, t0 // 128:t0 // 128 + nb0, hh, :], tk[:], [tkk], [(kind + "tok", t0, hh)], sem=tkk)
            bet, bk = sc_p.next()
            S.dma("sp", bet[:], beta_d[:, t0:t0 + TT0], (), [bk], sem=bk)
            av, ak = sc_p.next()
            S.dma("sp", av[:], a_d[:, t0:t0 + TT0], (), [ak], sem=ak)
            S.act(bet[:], bet[:], AF.Sigmoid, [bk], [bk])
            S.act(av[:], av[:], AF.Exp, [ak, "dtb"], [ak], bias=dtb[:, 0:1])
            S.act(av[:], av[:], AF.Ln, [ak], [ak], bias=1.0)
            S.ts("dve", av[:], av[:], negA[:, 0:1], None, ALU.mult, None, [ak, "negA"], [ak])
            pss, psk = psS.next()
            for jb in range(nb0):
                S.tr(pss[:, jb, 0:16], bet[:, jb * 128:(jb + 1) * 128], cm[0:16, ID, 0:16], [bk, "cm"], [psk])
                S.tr(pss[:, jb, 16:32], av[:, jb * 128:(jb + 1) * 128], cm[0:16, ID, 0:16], [ak, "cm"], [psk])
            bgo, bgk = bgo_p.next()
            S.copy("act", bgo[:], pss[:, 0:nb0, :], [psk], [bgk])
            S.dma("act", bg_s.rearrange("(nb p) c -> p nb c", p=128)[:, t0 // 128:t0 // 128 + nb0, :], bgo[:],
                  [bgk], [("bg", t0)], sem=bgk)

    NSLOT = 2
    slots = {}
    for d in range(2):
        for r in range(NSLOT):
            nm = f"sl{d}{r}"
            slots[(d, r)] = dict(
                kT=S.sbuf([128, NKH, 128], BF16, nm + "kT"), qT=S.sbuf([128, NKH, 128], BF16, nm + "qT"),
                ktok=S.sbuf([128, NKH, 128], BF16, nm + "ktok"), vtok=S.sbuf([128, NVH, 128], BF16, nm + "vtok"),
                bg=S.sbuf([128, 32], F32, nm + "bg"), sc=S.sbuf([128, 48], F32, nm + "sc"),
                TU=S.sbuf([128, NVH, 128], BF16, nm + "TU"), AT=S.sbuf([128, NVH, 128], BF16, nm + "AT"),
                kdec=S.sbuf([128, NVH, 128], BF16, nm + "kdec"), nm=nm)
    gm2_p = Rot(S, "gm2", 2, [128, 16], F32)
    GsPs = {d: S.sbuf([128, 2, NKH, 128], F32, f"gsps{d}") for d in range(2)}
    Gm4_p = Rot(S, "gm4", 2, [128, 4, 128], F32)
    E4 = {(d, hg): S.sbuf([128, 4, 128], F32, f"e4_{d}{hg}") for d in range(2) for hg in range(2)}
    zb_pg = {(d, hg): Rot(S, f"zb{d}{hg}", 9, [128, 4, 128], BF16) for d in range(2) for hg in range(2)}
    st32 = {(d, hg): S.sbuf([128, 4, 128], F32, f"st32_{d}{hg}") for d in range(2) for hg in range(2)}
    stbf = {(d, hg): S.sbuf([128, 4, 128], BF16, f"stbf_{d}{hg}") for d in range(2) for hg in range(2)}
    R_p = Rot(S, "Rp", 4, [128, 4, 128], BF16)
    vn_p = Rot(S, "vnp", 4, [128, 4, 128], BF16)
    ot_p = Rot(S, "otp", 3, [128, 4, 128], F32)
    osb = {(d, r): S.sbuf([128, NVH, 128], F32, f"osb{d}{r}") for d in range(2) for r in range(2)}

    def blk_of(d, s):
        return s if d == 0 else NB - 1 - s

    def pre_setup(d, s):
        blk = blk_of(d, s)
        sl = slots[(d, s % NSLOT)]
        nm = sl["nm"]
        c0 = blk * 128
        Mincl, Mrev = (LE, GT) if d == 0 else (GE, LT)
        MstrU, MinclU = (LT, LE) if d == 0 else (GT, GE)
        tb = (c0 // TT0) * TT0
        S.dma("sp", sl["kT"][:], kTn_s[:, :, c0:c0 + 128], [("kTn", tb, hh) for hh in range(NKH)], [nm + "kT"], sem=nm + "kT")
        S.dma("sp", sl["qT"][:], qTn_s[:, :, c0:c0 + 128], [("qTn", tb, hh) for hh in range(NKH)], [nm + "qT"], sem=nm + "qT")
        S.dma("sp", sl["ktok"][:], ktok_s[c0:c0 + 128, :, :], [("ktok", tb, hh) for hh in range(NKH)], [nm + "ktok"], sem=nm + "ktok")
        S.dma("sp", sl["vtok"][:], vtok_s[c0:c0 + 128, :, :], [("vtok", tb, hh) for hh in range(NVH)], [nm + "vtok"], sem=nm + "vtok")
        S.dma("sp", sl["bg"][:], bg_s[c0:c0 + 128, :], [("bg", (c0 // TT0) * TT0)], [nm + "bg"], sem=nm + "bg")
        bg, sc = sl["bg"], sl["sc"]
        gsel = bg[:, 16 + d * 8:16 + d * 8 + 8]
        gm2, gm2k = gm2_p.next()
        S.ts("pool", gm2[:, 0:8], gsel, c01[:, 0:1], None, ALU.mult, None, [nm + "bg", "c01"], [gm2k])
        S.ts("pool", gm2[:, 8:16], gsel, c01[:, 1:2], None, ALU.mult, None, [nm + "bg", "c01"], [gm2k])
        pss, psk = psS.next()
        pv = pss[:].rearrange("p a b -> p (a b)")
        S.mm(pv[:, 0:8], cm[:, Mincl, :], gsel, True, True, ["cm", nm + "bg"], [psk])
        S.mm(pv[:, 8:16], cm[:, Mrev, :], gsel, True, True, ["cm", nm + "bg"], [psk])
        S.mm(pv[:, 16:32], onesf[:], gm2[:], True, True, ["onesf", gm2k], [psk])
        S.act(sc[:, 0:32], pv[:, 0:32], AF.Exp, [psk], [nm + "sc"])
        S.ts("pool", sc[:, 32:40], sc[:, 0:8], -1.0, None, ALU.mult, None, [nm + "sc"], [nm + "sc"])
        S.ts("pool", sc[:, 40:48], bg[:, d * 8:d * 8 + 8], -1.0, None, ALU.mult, None, [nm + "bg"], [nm + "sc"])
        pG, pGk = psA.next()
        pP, pPk = psA.next()
        for kh in range(NKH):
            S.mm(pG[:, kh, :], sl["kT"][:, kh, :], sl["kT"][:, kh, :], True, True, [nm + "kT"], [pGk])
            S.mm(pP[:, kh, :], sl["kT"][:, kh, :], sl["qT"][:, kh, :], True, True, [nm + "kT", nm + "qT"], [pPk])
        gp, gpk = GsPs[d], ("gsps", d)
        S.tt("dve", gp[:, 0, :, :], pG[:], cm[:, MstrU:MstrU + 1, :].broadcast_to([128, NKH, 128]), ALU.mult,
             [pGk, "cm"], [(gpk, 0)])
        S.tt("dve", gp[:, 1, :, :], pP[:], cm[:, MinclU:MinclU + 1, :].broadcast_to([128, NKH, 128]), ALU.mult,
             [pPk, "cm"], [(gpk, 1)])

    def pre_group(d, s, hg):
        sl = slots[(d, s % NSLOT)]
        nm = sl["nm"]
        Mincl, Mrev = (LE, GT) if d == 0 else (GE, LT)
        bg, sc = sl["bg"], sl["sc"]
        gp, gpk = GsPs[d], ("gsps", d)
        zb = zb_pg[(d, hg)]
        gm4, gm4k = Gm4_p.next()
        pD, pDk = psA.next()
        for q in range(4):
            vh = 4 * hg + q
            S.ts("pool", gm4[:, q, :], cm[:, Mincl, :], bg[:, 16 + d * 8 + vh:16 + d * 8 + vh + 1], None, ALU.mult, None,
                 ["cm", nm + "bg"], [(gm4k, q)])
            S.mm(pD[:, q, :], cm[:, Mrev, :], gm4[:, q, :], True, True, ["cm", (gm4k, q)], [pDk])
        e4, e4k = E4[(d, hg)], ("e4", d, hg)
        S.act(e4[:], pD[:], AF.Exp, [pDk], [e4k])
        yield
        zu, zuk = zb.next()
        for q in range(4):
            vh = 4 * hg + q
            kh = vh // 2
            S.tt("pool", sl["AT"][:, vh, :], e4[:, q, :], gp[:, 1, kh, :], ALU.mult, [e4k, (gpk, 1)], [(nm + "AT", vh)])
            S.stt("dve", zu[:, q, :], e4[:, q, :], sc[:, 40 + vh:41 + vh], gp[:, 0, kh, :], ALU.mult, ALU.mult,
                  [e4k, nm + "sc", (gpk, 0)], [(zuk, q)])
            S.act(sl["kdec"][:, vh, :], sl["ktok"][:, kh, :], AF.Copy, [nm + "ktok", nm + "sc"], [(nm + "kdec", vh)],
                  scale=sc[:, 8 + vh:9 + vh])
        pt, ptk = psT.next()
        for q in range(4):
            S.tr(pt[:, q, :], zu[:, q, :], idb[:], [(zuk, q), "idb"], [ptk])
        z, zk = zb.next()
        S.copy("act", z[:], pt[:, 0:4, :], [ptk], [zk])
        zuk_all = [(zuk, q) for q in range(4)]
        tu, tuk = zb.next()
        tl, tlk = zb.next()
        idbb = idb[:].unsqueeze(1).broadcast_to([128, 4, 128])
        S.tt("pool", tu[:], zu[:], idbb, ALU.add, zuk_all + ["idb"], [tuk])
        S.tt("pool", tl[:], z[:], idbb, ALU.add, [zk, "idb"], [tlk])
        zp, zpk, zup, zupk = z, [zk], zu, zuk_all
        yield
        for k in range(1, 6):
            pB, pBk = psA.next()
            for q in range(4):
                S.mm(pB[:, q, :], zp[:, q, :], zup[:, q, :], True, True, zpk + zupk, [pBk])
            if k < 5:
                pA_, pAk = psA.next()
                for q in range(4):
                    S.mm(pA_[:, q, :], zup[:, q, :], zp[:, q, :], True, True, zpk + zupk, [pAk])
            nzup, nzupk = zb.next()
            S.copy("dve", nzup[:], pB[:], [pBk], [nzupk])
            if k < 5:
                nzp, nzpk = zb.next()
                S.copy("act", nzp[:], pA_[:], [pAk], [nzpk])
            yield
            pC, pCk = psA.next()
            for q in range(4):
                S.mm(pC[:, q, :], tl[:, q, :], nzup[:, q, :], True, True, [tlk, nzupk], [pCk])
            if k < 5:
                pE, pEk = psA.next()
                for q in range(4):
                    S.mm(pE[:, q, :], tu[:, q, :], nzp[:, q, :], True, True, [tuk, nzpk], [pEk])
                ntu, ntuk = zb.next()
                S.tt("dve", ntu[:], pC[:], tu[:], ALU.add, [pCk, tuk], [ntuk])
                ntl, ntlk = zb.next()
                S.tt("dve", ntl[:], pE[:], tl[:], ALU.add, [pEk, tlk], [ntlk])
                tu, tuk, tl, tlk = ntu, ntuk, ntl, ntlk
                zp, zpk, zup, zupk = nzp, [nzpk], nzup, [nzupk]
            else:
                S.tt("dve", sl["TU"][:, 4 * hg:4 * hg + 4, :], pC[:], tu[:], ALU.add, [pCk, tuk],
                     [(nm + "TU", 4 * hg + q) for q in range(4)])
            yield

    def loop(s):
        for hi in range(2):
            groups = []
            for d in range(2):
                sl = slots[(d, s % NSLOT)]
                hf = hi if d == 0 else 1 - hi
                for hg in range(2):
                    groups.append((d, hg, sl, hf, slice(64 * hf, 64 * hf + 64), sl["nm"]))
            ksl = []
            for d, hg, sl, hf, pr, nm in groups:
                bank, bk = next_half(hf)
                for q in range(4):
                    vh = 4 * hg + q
                    S.mm(bank[pr, q, :], sl["kT"][:, vh // 2, 64 * hf:64 * hf + 64], stbf[(d, hg)][:, q, :], True, True,
                         [nm + "kT", ("stbf", d, hg)], [bk])
                ksl.append((bank, bk))
            yield
            Rl = []
            for (d, hg, sl, hf, pr, nm), (bank, bk) in zip(groups, ksl):
                R, Rk = R_p.next()
                for q in range(4):
                    vh = 4 * hg + q
                    S.stt("dve", R[pr, q, :], bank[pr, q, :], sl["sc"][pr, 32 + vh:33 + vh], sl["vtok"][pr, vh, :],
                          ALU.mult, ALU.add, [bk, nm + "sc", nm + "vtok"], [Rk])
                Rl.append((R, Rk))
            yield
            vl = []
            for (d, hg, sl, hf, pr, nm), (R, Rk) in zip(groups, Rl):
                bank, bk = next_half(hf)
                for q in range(4):
                    vh = 4 * hg + q
                    S.mm(bank[pr, q, :], sl["TU"][pr, vh, 64 * hf:64 * hf + 64], R[pr, q, :], True, True,
                         [(nm + "TU", vh), Rk], [bk])
                vn, vnk = vn_p.next()
                for q in range(4):
                    vh = 4 * hg + q
                    S.act(vn[pr, q, :], bank[pr, q, :], AF.Copy, [bk, nm + "bg"], [vnk],
                          scale=sl["bg"][pr, d * 8 + vh:d * 8 + vh + 1])
                vl.append((vn, vnk))
            yield
            for gi_, ((d, hg, sl, hf, pr, nm), (vn, vnk)) in enumerate(zip(groups, vl)):
                if gi_ == 2:
                    yield
                ob = osb[(d, s % 2)]
                b1, b1k = next_half(hf)
                for q in range(4):
                    vh = 4 * hg + q
                    S.mm(b1[pr, q, :], sl["AT"][pr, vh, 64 * hf:64 * hf + 64], vn[pr, q, :], True, True,
                         [(nm + "AT", vh), vnk], [b1k])
                b2, b2k = next_half(hf)
                for q in range(4):
                    vh = 4 * hg + q
                    S.mm(b2[pr, q, :], sl["qT"][:, vh // 2, 64 * hf:64 * hf + 64], stbf[(d, hg)][:, q, :], True, True,
                         [nm + "qT", ("stbf", d, hg)], [b2k])
                b3, b3k = next_whole()
                for q in range(4):
                    vh = 4 * hg + q
                    S.mm(b3[:, q, :], sl["kdec"][pr, vh, :], vn[pr, q, :], True, True, [(nm + "kdec", vh), vnk], b3k)
                ot, otk = ot_p.next()
                S.copy("act", ot[pr, :, :], b1[pr, :, :], [b1k], [otk])
                for q in range(4):
                    vh = 4 * hg + q
                    S.stt("dve", ob[pr, vh, :], b2[pr, q, :], sl["sc"][pr, vh:vh + 1], ot[pr, q, :], ALU.mult, ALU.add,
                          [b2k, nm + "sc", otk], [("osb", d, s % 2, vh, hf)])
                for q in range(4):
                    vh = 4 * hg + q
                    S.stt("dve", st32[(d, hg)][:, q, :], st32[(d, hg)][:, q, :],
                          sl["sc"][:, 16 + hf * 8 + vh:17 + hf * 8 + vh], b3[:, q, :], ALU.mult, ALU.add,
                          [("st32", d, hg), nm + "sc"] + b3k, [("st32", d, hg)])
                S.copy("act", stbf[(d, hg)][:], st32[(d, hg)][:], [("st32", d, hg)], [("stbf", d, hg)])
            yield
        for d in range(2):
            blk = blk_of(d, s)
            S.dma("pool", o_s[d, blk * 128:(blk + 1) * 128, :, :], osb[(d, s % 2)][:],
                  [("osb", d, s % 2, vh, hf) for vh in range(NVH) for hf in range(2)], [("o_s", d, blk)], sem=("osb", d, s % 2))

    if 1 in stages:
        for dd in range(2):
            for hg in range(2):
                S.memset("pool", st32[(dd, hg)][:], 0.0, [("st32", dd, hg)])
                S.memset("pool", stbf[(dd, hg)][:], 0.0, [("stbf", dd, hg)])
        def lockstep(gens):
            gens = [(i, g) for i, g in enumerate(gens)]
            r = 0
            while gens:
                alive = []
                for i, g in gens:
                    if r >= i:
                        try:
                            next(g)
                        except StopIteration:
                            continue
                    alive.append((i, g))
                gens = alive
                r += 1

        def pre_gens(s):
            for d in range(2):
                pre_setup(d, s)
            return [pre_group(d, s, hg) for d in range(2) for hg in range(2)]

        lockstep(pre_gens(0))
        for s in range(NB):
            gens = [loop(s)]
            if s + 1 < NB:
                gens = gens + pre_gens(s + 1)
            lockstep(gens)

    if 2 in stages:
        of_p = Rot(S, "s2of", 2, [128, NVH, 128], F32)
        ob_p = Rot(S, "s2ob", 2, [128, NVH, 128], F32)
        on_p = Rot(S, "s2on", 2, [128, NVH, 128], BF16)
        z_p = Rot(S, "s2z", 2, [128, NVH, 128], F32)
        ss_p = Rot(S, "s2ss", 2, [128, NVH], F32)
        for blk in range(NB):
            c0 = blk * 128
            of, ofk = of_p.next()
            ob2, obk = ob_p.next()
            S.dma("sp", of[:], o_s[0, c0:c0 + 128, :, :], [("o_s", 0, blk)], [ofk], sem=ofk)
            S.dma("sp", ob2[:], o_s[1, c0:c0 + 128, :, :], [("o_s", 1, blk)], [obk], sem=obk)
            zt, ztk = z_p.next()
            S.dma("sp", zt[:], zT_d.rearrange("(h p) s -> p h s", p=128)[:, :, c0:c0 + 128], (), [ztk], sem=ztk)
            S.tt("pool", of[:], of[:], ob2[:], ALU.add, [ofk, obk], [ofk])
            sq2, sq2k = ob2, obk
            S.tt("pool", sq2[:], of[:], of[:], ALU.mult, [ofk], [sq2k])
            ss, ssk = ss_p.next()
            S.op("dve", lambda e, ss=ss, sq2=sq2: e.tensor_reduce(ss[:], sq2[:], mybir.AxisListType.X, ALU.add), [sq2k], [ssk])
            S.act(ss[:], ss[:], AF.Sqrt, [ssk, "epsb"], [ssk], bias=epsb[:, 0:1], scale=1.0 / 128)
            S.op("dve", lambda e, ss=ss: e.reciprocal(ss[:], ss[:]), [ssk], [ssk])
            on, onk = on_p.next()
            S.tt("dve", on[:], of[:], ss[:].unsqueeze(2).broadcast_to([128, NVH, 128]), ALU.mult, [ofk, ssk], [onk])
            S.act(zt[:], zt[:], AF.Silu, [ztk], [ztk])
            pt, ptk = psT.next()
            for vh in range(NVH):
                S.tr(pt[:, vh, :], on[:, vh, :], idb[:], [onk, "idb"], [ptk])
            oo, ook = of, ofk
            S.stt("dve", oo[:], pt[:], onw[:, 0:1], zt[:], ALU.mult, ALU.mult, [ptk, "onw", ztk, onk], [ook])
            S.dma("pool", oT_d.rearrange("(h p) s -> p h s", p=128)[:, :, c0:c0 + 128], oo[:], [ook], [], sem=ook)
    S.finish()
    return nc


SEQ = 8192
BATCH = 2
TPC = BATCH * SEQ // NCORES
DN_N = 12416
LRU_N = 4096
_PROG_CACHE = {}


def _c(a):
    return np.ascontiguousarray(a, dtype=np.float32)


def _nw(w):
    return _c(w.reshape(D // 128, 128).T)


def _run(nc, in_maps):
    import os, time
    t0 = time.time()
    r = run_bass_kernel_spmd(nc, in_maps, core_ids=list(range(NCORES))).results
    if os.environ.get("KDEBUG"):
        print(f"[kernel] launch took {time.time() - t0:.1f}s", flush=True)
    return r


def _dense_prog(sig):
    nc = bass.Bass("TRN2", target_bir_lowering=False)
    ops = []
    for o in sig:
        if o[0] == "outproj":
            ops.append(dict(op="outproj", dm=o[1]))
        elif o[0] == "inproj":
            ops.append(dict(op="inproj", n=o[1]))
        else:
            ops.append(dict(op=o[0]))
    dense_phase(nc, TPC, ops)
    return nc


def kernel(x, ffn1_norm, ffn1_w_gate_up, ffn1_w_down, mix_norm, ffn2_norm, ffn2_w_gate_up, ffn2_w_down,
           dn_w_in, dn_conv_w, dn_a_log, dn_dt_bias, dn_out_norm, dn_w_out,
           lru_w_in, lru_conv_w, lru_conv_b, lru_w_gate_a, lru_b_gate_a, lru_w_gate_x, lru_b_gate_x,
           lru_lambda, lru_w_out, final_norm):
    x = np.asarray(x, np.float32)
    depth = ffn1_norm.shape[0]
    seg = lambda c: (c // 4, slice((c % 4) * TPC, (c % 4 + 1) * TPC))
    hT = [_c(x[seg(c)[0], seg(c)[1], :].T) for c in range(NCORES)]
    mix_out = None
    cm, c01 = dn_consts()
    y = None
    for layer in range(depth + 1):
        sig = []
        common = {}
        if layer > 0:
            pl = layer - 1
            j = pl // 2
            wo = dn_w_out[j] if pl % 2 == 0 else lru_w_out[j]
            idx = len(sig)
            sig.append(("outproj", wo.shape[0]))
            common[f"wo{idx}"] = _c(wo)
            oidx = idx
            idx = len(sig)
            sig.append(("ffn",))
            common[f"nw{idx}"] = _nw(ffn2_norm[pl])
            common[f"wgu{idx}"] = _c(ffn2_w_gate_up[pl])
            common[f"wd{idx}"] = _c(ffn2_w_down[pl])
        if layer < depth:
            idx = len(sig)
            sig.append(("ffn",))
            common[f"nw{idx}"] = _nw(ffn1_norm[layer])
            common[f"wgu{idx}"] = _c(ffn1_w_gate_up[layer])
            common[f"wd{idx}"] = _c(ffn1_w_down[layer])
            j = layer // 2
            wi = dn_w_in[j] if layer % 2 == 0 else lru_w_in[j]
            idx = len(sig)
            sig.append(("inproj", wi.shape[1]))
            common[f"nw{idx}"] = _nw(mix_norm[layer])
            common[f"wi{idx}"] = _c(wi)
            pidx = idx
            sig.append(("store_h",))
            hidx = len(sig) - 1
        else:
            idx = len(sig)
            sig.append(("final",))
            common[f"nw{idx}"] = _nw(final_norm)
            fidx = idx
        sig = tuple(sig)
        if sig not in _PROG_CACHE:
            _PROG_CACHE[sig] = _dense_prog(sig)
        in_maps = []
        for c in range(NCORES):
            m = dict(common)
            m["hT_in"] = hT[c]
            if layer > 0:
                m[f"oT{oidx}"] = mix_out[c]
            in_maps.append(m)
        res = _run(_PROG_CACHE[sig], in_maps)
        if layer == depth:
            y = np.empty((BATCH, SEQ, D), np.float32)
            for c in range(NCORES):
                b, sl = seg(c)
                y[b, sl, :] = res[c][f"y{fidx}"].T
            break
        hT = [res[c][f"hT_out{hidx}"] for c in range(NCORES)]
        proj = [np.concatenate([res[b * 4 + s][f"proj{pidx}"] for s in range(4)], axis=1) for b in range(BATCH)]
        del res
        j = layer // 2
        in_maps = []
        if layer % 2 == 0:
            key = ("dn",)
            if key not in _PROG_CACHE:
                nc = bass.Bass("TRN2", target_bir_lowering=False)
                dn_phase(nc, SEQ)
                _PROG_CACHE[key] = nc
            cwv_all = dn_conv_w[j]
            for c in range(NCORES):
                b, g4 = c // 4, c % 4
                P = proj[b]
                kh0, vh0 = 4 * g4, 8 * g4
                ba_rows = lambda kind: np.concatenate([P[12288 + d * 64 + kind * 32 + vh0: 12288 + d * 64 + kind * 32 + vh0 + 8]
                                                       for d in range(2)], axis=0)
                cwl = lambda w, nh: _c(w.reshape(4, nh, 128).transpose(2, 1, 0))
                in_maps.append({
                    "qT": _c(P[kh0 * 128:(kh0 + 4) * 128]), "kT": _c(P[2048 + kh0 * 128:2048 + (kh0 + 4) * 128]),
                    "vT": _c(P[4096 + vh0 * 128:4096 + (vh0 + 8) * 128]), "zT": _c(P[8192 + vh0 * 128:8192 + (vh0 + 8) * 128]),
                    "betaT": _c(ba_rows(0)), "aT": _c(ba_rows(1)),
                    "cwq": cwl(cwv_all[:, kh0 * 128:(kh0 + 4) * 128], 4),
                    "cwk": cwl(cwv_all[:, 2048 + kh0 * 128:2048 + (kh0 + 4) * 128], 4),
                    "cwv": cwl(cwv_all[:, 4096 + vh0 * 128:4096 + (vh0 + 8) * 128], 8),
                    "alog": _c(dn_a_log[j][:, vh0:vh0 + 8].reshape(16, 1)),
                    "dtb": _c(dn_dt_bias[j][:, vh0:vh0 + 8].reshape(16, 1)),
                    "onw": _c(dn_out_norm[j].reshape(128, 1)), "cmask": cm, "c01": c01})
            res = _run(_PROG_CACHE[key], in_maps)
            outs = [res[c]["oT"] for c in range(NCORES)]
        else:
            key = ("lru",)
            if key not in _PROG_CACHE:
                nc = bass.Bass("TRN2", target_bir_lowering=False)
                lru_phase(nc, SEQ)
                _PROG_CACHE[key] = nc
            for c in range(NCORES):
                b, g4 = c // 4, c % 4
                P = proj[b]
                ch = slice(512 * g4, 512 * g4 + 512)
                pl1 = lambda v: _c(v.reshape(4, 128).T)
                pl2 = lambda v: _c(v.reshape(2, 4, 128).transpose(2, 0, 1))
                in_maps.append({
                    "xbT": _c(P[512 * g4:512 * g4 + 512]), "gateT": _c(P[2048 + 512 * g4:2048 + 512 * g4 + 512]),
                    "cw": _c(lru_conv_w[j][:, ch].reshape(4, 4, 128).transpose(2, 1, 0)), "cb": pl1(lru_conv_b[j][ch]),
                    "wga": _c(lru_w_gate_a[j][:, 2 * g4:2 * g4 + 2]), "wgx": _c(lru_w_gate_x[j][:, 2 * g4:2 * g4 + 2]),
                    "bga": pl2(lru_b_gate_a[j][:, ch]), "bgx": pl2(lru_b_gate_x[j][:, ch]), "lam": pl2(lru_lambda[j][:, ch])})
            res = _run(_PROG_CACHE[key], in_maps)
            outs = [res[c]["yT"] for c in range(NCORES)]
        del proj, in_maps
        full = [np.concatenate([outs[b * 4 + g] for g in range(4)], axis=0) for b in range(BATCH)]
        mix_out = [_c(full[c // 4][:, seg(c)[1]]) for c in range(NCORES)]
        del res, outs, full
    return y
```

```python
import contextlib
import numpy as np
import concourse.bass as bass
import concourse.mybir as mybir
from concourse.bass_utils import run_bass_kernel_spmd

F32 = mybir.dt.float32
BF16 = mybir.dt.bfloat16
AF = mybir.ActivationFunctionType
ALU = mybir.AluOpType

D = 2048
FF = 5632
NCORES = 8
EPS = 1e-6
SAME_ENGINE_SYNC = True


class Sched:
    ENG = ("pe", "act", "dve", "pool", "sp")

    def __init__(self, nc):
        self.nc = nc
        self.streams = {e: [] for e in self.ENG}
        self.ccnt = {e: 0 for e in self.ENG}
        self.dcnt = {}
        self.dq = {}
        self.seen = {e: {} for e in self.ENG}
        self.last_w = {}
        self.rd = {}
        self.stack = contextlib.ExitStack()
        self.nbuf = 0
        self.psum_banks = []
        self.psum_i = 0

    def sbuf(self, shape, dt, name=None):
        self.nbuf += 1
        name = "sb_" + (name or f"{self.nbuf}")
        return self.stack.enter_context(self.nc.sbuf_tensor(name, list(shape), dt))

    def psum(self, shape, dt=F32, name=None):
        self.nbuf += 1
        name = "ps_" + (name or f"{self.nbuf}")
        return self.stack.enter_context(self.nc.psum_tensor(name, list(shape), dt))

    def _deps(self, eng, reads, writes):
        deps = []
        for k in reads:
            if k in self.last_w:
                deps.append(self.last_w[k])
        for k in writes:
            if k in self.last_w:
                deps.append(self.last_w[k])
            deps.extend(self.rd.get(k, ()))
        need = {}
        for sk, v in deps:
            if sk == ("c", "pe") and eng == "pe":
                continue
            if sk == ("c", eng) and not SAME_ENGINE_SYNC:
                continue
            if v > need.get(sk, 0):
                need[sk] = v
        for sk, v in need.items():
            if v > self.seen[eng].get(sk, 0):
                self.seen[eng][sk] = v
                self.streams[eng].append(("wait", sk, v))

    def _post(self, tok, reads, writes):
        for k in reads:
            self.rd.setdefault(k, []).append(tok)
        for k in writes:
            self.last_w[k] = tok
            self.rd[k] = []

    def op(self, eng, fn, reads=(), writes=()):
        self._deps(eng, reads, writes)
        self.ccnt[eng] += 1
        tok = (("c", eng), self.ccnt[eng])
        self.streams[eng].append(("op", fn, tok))
        self._post(tok, reads, writes)

    def dma(self, eng, out, in_, reads=(), writes=(), sem=None):
        assert sem is not None
        self._deps(eng, reads, writes)
        sk = ("d", "sw" if eng == "pool" else "hw", sem)
        prev = self.dcnt.get(sk, 0)
        if prev > self.seen[eng].get(sk, 0):
            self.seen[eng][sk] = prev
            self.streams[eng].append(("wait", sk, prev))
        self.dcnt[sk] = prev + 16
        self.dq.setdefault(eng, set()).add(sk)
        tok = (sk, self.dcnt[sk])
        self.streams[eng].append(("dma", (out, in_), tok))
        self._post(tok, reads, writes)

    def mm(self, ps, lhsT, rhs, start, stop, reads, writes):
        self.op("pe", lambda e: e.matmul(ps, lhsT, rhs, start=start, stop=stop), reads, writes)

    def tr(self, ps, in_, ident, reads, writes):
        self.op("pe", lambda e: e.transpose(ps, in_, ident), reads, writes)

    def act(self, out, in_, func, reads, writes, bias=None, scale=None):
        kw = {}
        if bias is not None:
            kw["bias"] = bias
        if scale is not None:
            kw["scale"] = scale
        self.op("act", lambda e: e.activation(out=out, in_=in_, func=func, **kw), reads, writes)

    def tt(self, eng, out, in0, in1, op, reads, writes):
        self.op(eng, lambda e: e.tensor_tensor(out, in0, in1, op), reads, writes)

    def ts(self, eng, out, in0, s1, s2, op0, op1, reads, writes):
        if s2 is None and eng == "pool" and op0 == ALU.mult:
            self.op(eng, lambda e: e.tensor_scalar(out, in0, s1, 1.0, ALU.mult, ALU.mult), reads, writes)
        elif s2 is None:
            self.op(eng, lambda e: e.tensor_scalar(out, in0, s1, None, op0), reads, writes)
        else:
            self.op(eng, lambda e: e.tensor_scalar(out, in0, s1, s2, op0, op1), reads, writes)

    def stt(self, eng, out, in0, scalar, in1, op0, op1, reads, writes):
        self.op(eng, lambda e: e.scalar_tensor_tensor(out, in0, scalar, in1, op0, op1), reads, writes)

    def copy(self, eng, out, in_, reads, writes):
        if eng == "act":
            self.op(eng, lambda e: e.copy(out, in_), reads, writes)
        else:
            self.op(eng, lambda e: e.tensor_copy(out, in_), reads, writes)

    def memset(self, eng, ap, val, writes):
        self.op(eng, lambda e: e.memset(ap, val), (), writes)

    def finish(self):
        nc = self.nc
        st = self.stack
        sems = {}
        for e in self.ENG:
            sems[("c", e)] = st.enter_context(nc.semaphore(f"c_{e}"))
        for i, sk in enumerate(self.dcnt):
            sems[sk] = st.enter_context(nc.semaphore(f"d_{i}"))
        for e, sks in self.dq.items():
            for sk in sks:
                if self.dcnt[sk] > self.seen[e].get(sk, 0):
                    self.streams[e].append(("wait", sk, self.dcnt[sk]))
        streams = self.streams

        def replay(name, eng):
            for item in streams[name]:
                if item[0] == "wait":
                    eng.wait_ge(sems[item[1]], item[2])
                elif item[0] == "op":
                    item[1](eng).then_inc(sems[item[2][0]], 1)
                else:
                    o, i = item[1]
                    eng.dma_start(out=o, in_=i).then_inc(sems[item[2][0]], 16)

        with nc.Block() as block:
            @block.tensor
            def _(e):
                replay("pe", e)

            @block.scalar
            def _(e):
                replay("act", e)

            @block.vector
            def _(e):
                replay("dve", e)

            @block.gpsimd
            def _(e):
                replay("pool", e)

            @block.sync
            def _(e):
                replay("sp", e)
        st.close()


class Rot:
    def __init__(self, S, name, n, shape, dt, psum=False):
        self.tiles = [(S.psum(shape, dt, f"{name}{i}") if psum else S.sbuf(shape, dt, f"{name}{i}")) for i in range(n)]
        self.keys = [(name, i) for i in range(n)]
        self.i = 0

    def next(self):
        t, k = self.tiles[self.i], self.keys[self.i]
        self.i = (self.i + 1) % len(self.tiles)
        return t, k


def dense_phase(nc, T, ops, TT=512):
    S = Sched(nc)
    KC = D // 128
    FC = FF // 128
    NT = T // TT
    dram = {}

    def din(name, shape):
        dram[name] = nc.dram_tensor(name, list(shape), F32, kind="ExternalInput").ap()
        return dram[name]

    def dout(name, shape):
        dram[name] = nc.dram_tensor(name, list(shape), F32, kind="ExternalOutput").ap()
        return dram[name]

    hT_in = din("hT_in", [D, T])
    need_h_out = any(o["op"] == "store_h" for o in ops)
    for idx, o in enumerate(ops):
        if o["op"] == "ffn":
            o["nw"] = din(f"nw{idx}", [128, KC])
            o["wgu"] = din(f"wgu{idx}", [D, 2 * FF])
            o["wd"] = din(f"wd{idx}", [FF, D])
        elif o["op"] == "outproj":
            o["oT"] = din(f"oT{idx}", [o["dm"], T])
            o["wo"] = din(f"wo{idx}", [o["dm"], D])
        elif o["op"] == "inproj":
            o["nw"] = din(f"nw{idx}", [128, KC])
            o["wi"] = din(f"wi{idx}", [D, o["n"]])
            o["out"] = dout(f"proj{idx}", [o["n"], T])
        elif o["op"] == "final":
            o["nw"] = din(f"nw{idx}", [128, KC])
            o["out"] = dout(f"y{idx}", [D, T])
        elif o["op"] == "store_h":
            o["out"] = dout(f"hT_out{idx}", [D, T])

    h = S.sbuf([128, KC, TT], F32, "h")
    xn = S.sbuf([128, KC, TT], BF16, "xn")
    actb = S.sbuf([128, FC, TT], BF16, "actb")
    ones = S.sbuf([128, 128], BF16, "ones")
    nws = {}
    for idx, o in enumerate(ops):
        if "nw" in o:
            nws[idx] = S.sbuf([128, KC], F32, f"nwt{idx}")
    WG = 256
    wgu_pool = Rot(S, "wgu", 2, [128, 2, KC, WG], BF16)
    wd_pool = Rot(S, "wd", 2, [128, FC, WG], BF16)
    sq_pool = Rot(S, "sq", 2, [128, TT], BF16)
    tmp_pool = Rot(S, "tmp", 3, [128, TT], F32)
    rstd = S.sbuf([128, TT], F32, "rstd")
    psp = Rot(S, "psb", 8, [128, TT], F32, psum=True)

    S.memset("dve", ones[:], 1.0, ["ones"])
    for idx in nws:
        S.dma("sp", nws[idx][:], ops[idx]["nw"], (), [("nw", idx)], sem=("nw", idx))

    def rmsnorm(nwidx, out_fp32_cb=None):
        ss, ssk = psp.next()
        for c in range(KC):
            sq, sqk = sq_pool.next()
            S.act(sq[:], h[:, c, :], AF.Square, [("h", c)], [sqk])
            S.mm(ss[:], ones[:], sq[:], c == 0, c == KC - 1, ["ones", sqk], [ssk])
        S.act(rstd[:], ss[:], AF.Sqrt, [ssk], ["rstd"], bias=EPS_AP[0][:, 0:1], scale=1.0 / D)
        S.op("dve", lambda e: e.reciprocal(rstd[:], rstd[:]), ["rstd"], ["rstd"])
        nwt = nws[nwidx]
        for c in range(KC):
            if out_fp32_cb is None:
                S.stt("dve", xn[:, c, :], h[:, c, :], nwt[:, c:c + 1], rstd[:], ALU.mult, ALU.mult,
                      [("h", c), ("nw", nwidx), "rstd"], [("xn", c)])
            else:
                out_fp32_cb(c, nwt)

    EPS_AP = [S.sbuf([128, 1], F32, "epsb")]
    S.memset("dve", EPS_AP[0][:], EPS, ["epsb"])

    def linear_acc(w_ap, kc, ncols, x_tile, xkeys, w_pool_kind, epilogue):
        for g0 in range(0, ncols, WG):
            gw = min(WG, ncols - g0)
            wt, wk = wd_pool.next()
            src = w_ap[:, g0:g0 + gw].rearrange("(c p) n -> p c n", p=128)
            half = (kc + 1) // 2
            S.dma("pool", wt[:, 0:half, 0:gw], src[:, 0:half, :], (), [(wk, 0)], sem=(wk, 0))
            if kc > half:
                S.dma("pool", wt[:, half:kc, 0:gw], src[:, half:kc, :], (), [(wk, 1)], sem=(wk, 1))
            for j in range(gw // 128):
                ps, pk = psp.next()
                for c in range(kc):
                    S.mm(ps[:], wt[:, c, j * 128:(j + 1) * 128], x_tile[:, c, :], c == 0, c == kc - 1,
                         [(wk, 0 if c < half else 1), xkeys(c)], [pk])
                epilogue(g0 // 128 + j, ps, pk)

    for t in range(NT):
        tsl = slice(t * TT, (t + 1) * TT)
        hv = hT_in.rearrange("(c p) t -> p c t", p=128)
        for c0 in range(0, KC, 4):
            S.dma("sp", h[:, c0:c0 + 4, :], hv[:, c0:c0 + 4, tsl], (), [("h", c) for c in range(c0, c0 + 4)], sem=("h", c0))
        for idx, o in enumerate(ops):
            if o["op"] == "ffn":
                rmsnorm(idx)
                for g0 in range(0, FF, WG):
                    wt, wk = wgu_pool.next()
                    for s in range(2):
                        src = o["wgu"][:, s * FF + g0: s * FF + g0 + WG].rearrange("(c p) n -> p c n", p=128)
                        S.dma("pool", wt[:, s, :, :], src, (), [(wk, s)], sem=(wk, s))
                    for j in range(WG // 128):
                        fc = g0 // 128 + j
                        gp, gk = psp.next()
                        up, uk = psp.next()
                        for c in range(KC):
                            S.mm(gp[:], wt[:, 0, c, j * 128:(j + 1) * 128], xn[:, c, :], c == 0, c == KC - 1,
                                 [(wk, 0), ("xn", c)], [gk])
                        for c in range(KC):
                            S.mm(up[:], wt[:, 1, c, j * 128:(j + 1) * 128], xn[:, c, :], c == 0, c == KC - 1,
                                 [(wk, 1), ("xn", c)], [uk])
                        tm, tk = tmp_pool.next()
                        S.act(tm[:], gp[:], AF.Silu, [gk], [tk])
                        S.tt("dve", actb[:, fc, :], tm[:], up[:], ALU.mult, [tk, uk], [("act", fc)])

                def epi_down(dc, ps, pk):
                    S.stt("dve", h[:, dc, :], ps[:], 0.5, h[:, dc, :], ALU.mult, ALU.add,
                          [pk, ("h", dc)], [("h", dc)])
                linear_acc(o["wd"], FC, D, actb, lambda c: ("act", c), "wd", epi_down)
            elif o["op"] == "outproj":
                kc = o["dm"] // 128
                ov = o["oT"].rearrange("(c p) t -> p c t", p=128)
                for c0 in range(0, kc, 8):
                    S.dma("pool", actb[:, c0:c0 + 8, :], ov[:, c0:c0 + 8, tsl], (),
                          [("act", c) for c in range(c0, c0 + 8)], sem=("actb", c0))

                def epi_out(dc, ps, pk):
                    S.tt("dve", h[:, dc, :], ps[:], h[:, dc, :], ALU.add, [pk, ("h", dc)], [("h", dc)])
                linear_acc(o["wo"], kc, D, actb, lambda c: ("act", c), "wd", epi_out)
            elif o["op"] == "inproj":
                rmsnorm(idx)
                outv = o["out"]

                def epi_in(nc_, ps, pk, outv=outv):
                    tm, tk = tmp_pool.next()
                    S.copy("act", tm[:], ps[:], [pk], [tk])
                    S.dma("sp", outv[nc_ * 128:(nc_ + 1) * 128, tsl], tm[:], [tk], [], sem=tk)
                linear_acc(o["wi"], KC, o["n"], xn, lambda c: ("xn", c), "wd", epi_in)
            elif o["op"] == "final":
                outv = o["out"].rearrange("(c p) t -> p c t", p=128)

                def cb(c, nwt, outv=outv, idx=idx):
                    tm, tk = tmp_pool.next()
                    S.stt("dve", tm[:], h[:, c, :], nwt[:, c:c + 1], rstd[:], ALU.mult, ALU.mult,
                          [("h", c), ("nw", idx), "rstd"], [tk])
                    S.dma("sp", outv[:, c, tsl], tm[:], [tk], [], sem=tk)
                rmsnorm(idx, cb)
            elif o["op"] == "store_h":
                outv = o["out"].rearrange("(c p) t -> p c t", p=128)
                for c0 in range(0, KC, 4):
                    S.dma("sp", outv[:, c0:c0 + 4, tsl], h[:, c0:c0 + 4, :],
                          [("h", c) for c in range(c0, c0 + 4)], [], sem=("h", c0))
    S.finish()
    return nc


def lru_phase(nc, Sq, TT=512):
    S = Sched(nc)
    NT = Sq // TT
    NCH = 4

    def din(name, shape):
        return nc.dram_tensor(name, list(shape), F32, kind="ExternalInput").ap()

    xbT = din("xbT", [512, Sq])
    gateT = din("gateT", [512, Sq])
    cw_d = din("cw", [128, NCH, 4])
    cb_d = din("cb", [128, NCH])
    wga_d = din("wga", [2, 2, 256, 256])
    wgx_d = din("wgx", [2, 2, 256, 256])
    bga_d = din("bga", [128, 2, NCH])
    bgx_d = din("bgx", [128, 2, NCH])
    lam_d = din("lam", [128, 2, NCH])
    yT = nc.dram_tensor("yT", [512, Sq], F32, kind="ExternalOutput").ap()

    cw = S.sbuf([128, NCH, 4], F32, "cw")
    cb = S.sbuf([128, NCH], F32, "cb")
    bga = S.sbuf([128, 2, NCH], F32, "bga")
    bgx = S.sbuf([128, 2, NCH], F32, "bgx")
    lam = S.sbuf([128, 2, NCH], F32, "lam")
    nsp8 = S.sbuf([128, 2, NCH], F32, "nsp8")
    wga = S.sbuf([128, 8, 256], BF16, "wga")
    wgx = S.sbuf([128, 8, 256], BF16, "wgx")
    carry = S.sbuf([128, NCH], F32, "carry")
    S.dma("sp", cw[:], cw_d, (), ["cw"], sem="cw")
    S.dma("sp", cb[:], cb_d, (), ["cb"], sem="cb")
    S.dma("sp", bga[:], bga_d, (), ["bga"], sem="bga")
    S.dma("sp", bgx[:], bgx_d, (), ["bgx"], sem="bgx")
    S.dma("sp", lam[:], lam_d, (), ["lam"], sem="lam")
    S.dma("pool", wga[:], wga_d.rearrange("d b (ic p) j -> p (d b ic) j", p=128), (), ["wga"], sem="wga")
    S.dma("pool", wgx[:], wgx_d.rearrange("d b (ic p) j -> p (d b ic) j", p=128), (), ["wgx"], sem="wgx")
    S.act(nsp8[:], lam[:], AF.Exp, ["lam"], ["nsp8"], scale=-1.0)
    S.act(nsp8[:], nsp8[:], AF.Ln, ["nsp8"], ["nsp8"], bias=1.0)
    S.ts("dve", nsp8[:], nsp8[:], -8.0, None, ALU.mult, None, ["nsp8"], ["nsp8"])

    xr_pool = Rot(S, "xr", 2, [128, 2, TT + 3], F32)
    xc_pool = Rot(S, "xc", 2, [128, 2, TT], F32)
    xcb_pool = Rot(S, "xcb", 2, [128, 2, TT], BF16)
    tp = Rot(S, "lt", 12, [128, TT], F32)
    psp = Rot(S, "lps", 6, [128, TT], F32, psum=True)
    GC = 2.0 * float(np.sqrt(2.0 / np.pi))

    for d in (0, 1):
        order = list(range(NT)) if d == 0 else list(range(NT - 1, -1, -1))
        for ti, t in enumerate(order):
            t0 = t * TT
            for blk in range(2):
                xr, xk = xr_pool.next()
                lo, hi = t0 - 2, t0 + TT + 1
                slo, shi = max(lo, 0), min(hi, Sq)
                if lo < 0 or hi > Sq:
                    S.memset("pool", xr[:], 0.0, [xk])
                src = xbT.rearrange("(c p) s -> p c s", p=128)[:, 2 * blk:2 * blk + 2, slo:shi]
                S.dma("sp", xr[:, :, slo - lo:shi - lo], src, (), [xk], sem=xk)
                xc, xck = xc_pool.next()
                xcb, xcbk = xcb_pool.next()
                for jc in range(2):
                    c = 2 * blk + jc
                    S.ts("dve", xc[:, jc, :], xr[:, jc, 0:TT], cw[:, c, 0:1], cb[:, c:c + 1], ALU.mult, ALU.add,
                         [xk, "cw", "cb"], [(xck, jc)])
                    for j in range(1, 4):
                        S.stt("dve", xc[:, jc, :], xr[:, jc, j:j + TT], cw[:, c, j:j + 1], xc[:, jc, :],
                              ALU.mult, ALU.add, [xk, "cw", (xck, jc)], [(xck, jc)])
                    S.copy("pool", xcb[:, jc, :], xc[:, jc, :], [(xck, jc)], [(xcbk, jc)])
                for jc in range(2):
                    c = 2 * blk + jc
                    pr, prk = psp.next()
                    pi, pik = psp.next()
                    for ic in range(2):
                        S.mm(pr[:], wga[:, d * 4 + blk * 2 + ic, jc * 128:(jc + 1) * 128], xcb[:, ic, :],
                             ic == 0, ic == 1, ["wga", (xcbk, ic)], [prk])
                    for ic in range(2):
                        S.mm(pi[:], wgx[:, d * 4 + blk * 2 + ic, jc * 128:(jc + 1) * 128], xcb[:, ic, :],
                             ic == 0, ic == 1, ["wgx", (xcbk, ic)], [pik])
                    r, rk = tp.next()
                    S.act(r[:], pr[:], AF.Sigmoid, [prk, "bga"], [rk], bias=bga[:, d, c:c + 1])
                    gi, gik = tp.next()
                    S.act(gi[:], pi[:], AF.Sigmoid, [pik, "bgx"], [gik], bias=bgx[:, d, c:c + 1])
                    a, ak = tp.next()
                    S.act(a[:], r[:], AF.Exp, [rk, "nsp8"], [ak], scale=nsp8[:, d, c:c + 1])
                    S.tt("pool", r[:], a[:], a[:], ALU.mult, [ak], [rk])
                    S.act(r[:], r[:], AF.Sqrt, [rk], [rk], bias=1.0, scale=-1.0)
                    S.tt("pool", gi[:], gi[:], xc[:, jc, :], ALU.mult, [gik, (xck, jc)], [gik])
                    S.tt("dve", gi[:], gi[:], r[:], ALU.mult, [gik, rk], [gik])
                    hh, hk = tp.next()
                    init = 0.0 if ti == 0 else carry[:, c:c + 1]
                    ckey = ("carry", c)
                    if d == 0:
                        S.op("dve", lambda e, hh=hh, a=a, gi=gi, init=init: e.tensor_tensor_scan(
                            hh[:], a[:], gi[:], init, ALU.mult, ALU.add), [ak, gik, ckey], [hk])
                        S.copy("pool", carry[:, c:c + 1], hh[:, TT - 1:TT], [hk], [ckey])
                        S.dma("pool", yT[c * 128:(c + 1) * 128, t0:t0 + TT], hh[:], [hk], [("y", c, t)], sem=hk)
                    else:
                        S.op("dve", lambda e, hh=hh, a=a, gi=gi, init=init: e.tensor_tensor_scan(
                            hh[:, ::-1], a[:, ::-1], gi[:, ::-1], init, ALU.mult, ALU.add), [ak, gik, ckey], [hk])
                        S.copy("pool", carry[:, c:c + 1], hh[:, 0:1], [hk], [ckey])
                        hf, hfk = tp.next()
                        S.dma("sp", hf[:], yT[c * 128:(c + 1) * 128, t0:t0 + TT], [("y", c, t)], [hfk], sem=hfk)
                        g, gk = tp.next()
                        S.dma("sp", g[:], gateT[c * 128:(c + 1) * 128, t0:t0 + TT], (), [gk], sem=gk)
                        u, uk = tp.next()
                        S.act(u[:], g[:], AF.Square, [gk], [uk])
                        S.ts("pool", u[:], u[:], 0.044715, 1.0, ALU.mult, ALU.add, [uk], [uk])
                        S.tt("pool", u[:], u[:], g[:], ALU.mult, [uk, gk], [uk])
                        S.act(u[:], u[:], AF.Sigmoid, [uk], [uk], scale=GC)
                        S.tt("pool", u[:], u[:], g[:], ALU.mult, [uk, gk], [uk])
                        S.tt("dve", hh[:], hh[:], hf[:], ALU.add, [hk, hfk], [hk])
                        S.tt("dve", hh[:], hh[:], u[:], ALU.mult, [hk, uk], [hk])
                        S.dma("pool", yT[c * 128:(c + 1) * 128, t0:t0 + TT], hh[:], [hk], [("y", c, t)], sem=hk)
    S.finish()
    return nc


def dn_consts():
    m = np.arange(128)[:, None]
    i = np.arange(128)[None, :]
    same = (m // 64) == (i // 64)
    cm = np.stack([(m <= i) & same, (m < i) & same, (m >= i) & same, (m > i) & same, m == i]).astype(np.float32)
    cm = np.ascontiguousarray(cm.transpose(1, 0, 2))
    c01 = np.stack([(np.arange(128) < 64), (np.arange(128) >= 64)], axis=1).astype(np.float32)
    return cm, np.ascontiguousarray(c01)


def dn_phase(nc, Sq, stages=(0, 1, 2), dbg=False):
    S = Sched(nc)
    NB = Sq // 128
    TT0 = 512 if Sq >= 512 else Sq
    NKH, NVH = 4, 8
    LE, LT, GE, GT, ID = 0, 1, 2, 3, 4

    def din(name, shape):
        return nc.dram_tensor(name, list(shape), F32, kind="ExternalInput").ap()

    qT_d = din("qT", [NKH * 128, Sq])
    kT_d = din("kT", [NKH * 128, Sq])
    vT_d = din("vT", [NVH * 128, Sq])
    zT_d = din("zT", [NVH * 128, Sq])
    beta_d = din("betaT", [16, Sq])
    a_d = din("aT", [16, Sq])
    cwq_d = din("cwq", [128, NKH, 4])
    cwk_d = din("cwk", [128, NKH, 4])
    cwv_d = din("cwv", [128, NVH, 4])
    alog_d = din("alog", [16, 1])
    dtb_d = din("dtb", [16, 1])
    onw_d = din("onw", [128, 1])
    cm_d = din("cmask", [128, 5, 128])
    c01_d = din("c01", [128, 2])
    oT_d = nc.dram_tensor("oT", [NVH * 128, Sq], F32, kind="ExternalOutput").ap()
    skind = "ExternalOutput" if dbg else "Internal"
    kTn_s = nc.dram_tensor("kTn_s", [128, NKH, Sq], BF16, kind=skind).ap()
    qTn_s = nc.dram_tensor("qTn_s", [128, NKH, Sq], BF16, kind=skind).ap()
    ktok_s = nc.dram_tensor("ktok_s", [Sq, NKH, 128], BF16, kind=skind).ap()
    vtok_s = nc.dram_tensor("vtok_s", [Sq, NVH, 128], BF16, kind=skind).ap()
    bg_s = nc.dram_tensor("bg_s", [Sq, 32], F32, kind=skind).ap()
    o_s = nc.dram_tensor("o_s", [2, Sq, NVH, 128], F32, kind=skind).ap()

    cm = S.sbuf([128, 5, 128], F32, "cm")
    c01 = S.sbuf([128, 2], F32, "c01")
    cwq = S.sbuf([128, NKH, 4], F32, "cwq")
    cwk = S.sbuf([128, NKH, 4], F32, "cwk")
    cwv = S.sbuf([128, NVH, 4], F32, "cwv")
    alog = S.sbuf([16, 1], F32, "alog")
    dtb = S.sbuf([16, 1], F32, "dtb")
    onw = S.sbuf([128, 1], F32, "onw")
    for t, dsrc, nm in ((cm, cm_d, "cm"), (c01, c01_d, "c01"), (cwq, cwq_d, "cwq"), (cwk, cwk_d, "cwk"),
                        (cwv, cwv_d, "cwv"), (alog, alog_d, "alog"), (dtb, dtb_d, "dtb"), (onw, onw_d, "onw")):
        S.dma("sp", t[:], dsrc, (), [nm], sem=nm)
    idb = S.sbuf([128, 128], BF16, "idb")
    S.copy("dve", idb[:], cm[:, ID, :], ["cm"], ["idb"])
    onesb = S.sbuf([128, 128], BF16, "onesb")
    S.memset("dve", onesb[:], 1.0, ["onesb"])
    onesf = S.sbuf([128, 128], F32, "onesf")
    S.memset("dve", onesf[:], 1.0, ["onesf"])
    epsb = S.sbuf([128, 1], F32, "epsb")
    S.memset("dve", epsb[:], EPS, ["epsb"])
    negA = S.sbuf([16, 1], F32, "negA")
    S.act(negA[:], alog[:], AF.Exp, ["alog"], ["negA"])
    S.ts("dve", negA[:], negA[:], -1.0, None, ALU.mult, None, ["negA"], ["negA"])

    psA = Rot(S, "dpa", 3, [128, 4, 128], F32, psum=True)
    psT = Rot(S, "dpt", 1, [128, 8, 128], BF16, psum=True)
    psS = Rot(S, "dps", 1, [128, 4, 32], F32, psum=True)
    NLB = 3
    psL_t = [S.psum([128, 4, 128], F32, f"dpl{i}") for i in range(NLB)]
    psL_i = [0, 0, 0]

    def next_half(hf):
        i = psL_i[0]
        psL_i[0] = (i + 1) % NLB
        return psL_t[i], ("dpl", i, hf)

    def next_whole():
        i = psL_i[0]
        psL_i[0] = (i + 1) % NLB
        return psL_t[i], [("dpl", i, 0), ("dpl", i, 1)]

    raw_p = Rot(S, "s0raw", 3, [128, TT0 + 3], F32)
    x_p = Rot(S, "s0x", 3, [128, TT0], F32)
    sq_p = Rot(S, "s0sq", 2, [128, TT0], BF16)
    rs_p = Rot(S, "s0rs", 2, [128, TT0], F32)
    xn_p = Rot(S, "s0xn", 3, [128, TT0], BF16)
    tk_p = Rot(S, "s0tk", 3, [128, TT0 // 128, 128], BF16)
    sc_p = Rot(S, "s0sc", 2, [16, TT0], F32)
    bgo_p = Rot(S, "s0bgo", 2, [128, TT0 // 128, 32], F32)
    QS = float(128 ** -0.5)
    nb0 = TT0 // 128
    if 0 in stages:
        for t0 in range(0, Sq, TT0):
            for kind, src_d, cwt, nh in (("k", kT_d, cwk, NKH), ("q", qT_d, cwq, NKH), ("v", vT_d, cwv, NVH)):
                for hh in range(nh):
                    raw, rk = raw_p.next()
                    lo, hi = t0 - 2, t0 + TT0 + 1
                    slo, shi = max(lo, 0), min(hi, Sq)
                    if lo < 0 or hi > Sq:
                        S.memset("pool", raw[:], 0.0, [rk])
                    S.dma("sp", raw[:, slo - lo:shi - lo], src_d[hh * 128:(hh + 1) * 128, slo:shi], (), [rk], sem=rk)
                    x, xk = x_p.next()
                    S.ts("dve", x[:], raw[:, 0:TT0], cwt[:, hh, 0:1], None, ALU.mult, None, [rk, "cw" + kind], [xk])
                    for j in range(1, 4):
                        S.stt("dve", x[:], raw[:, j:j + TT0], cwt[:, hh, j:j + 1], x[:], ALU.mult, ALU.add,
                              [rk, "cw" + kind, xk], [xk])
                    xn, xnk = xn_p.next()
                    if kind == "v":
                        S.act(xn[:], x[:], AF.Silu, [xk], [xnk])
                    else:
                        S.act(x[:], x[:], AF.Silu, [xk], [xk])
                        sq, sqk = sq_p.next()
                        S.tt("pool", sq[:], x[:], x[:], ALU.mult, [xk], [sqk])
                        ps, pk = psA.next()
                        psv = ps[:].rearrange("p a b -> p (a b)")
                        S.mm(psv[:, 0:TT0], onesb[:], sq[:], True, True, ["onesb", sqk], [pk])
                        rs, rsk = rs_p.next()
                        S.act(rs[:], psv[:, 0:TT0], AF.Sqrt, [pk, "epsb"], [rsk], bias=epsb[:, 0:1])
                        S.op("dve", lambda e, rs=rs: e.reciprocal(rs[:], rs[:]), [rsk], [rsk])
                        if kind == "q":
                            S.stt("dve", xn[:], x[:], QS, rs[:], ALU.mult, ALU.mult, [xk, rsk], [xnk])
                        else:
                            S.tt("dve", xn[:], x[:], rs[:], ALU.mult, [xk, rsk], [xnk])
                        dst = (kTn_s if kind == "k" else qTn_s)[:, hh, t0:t0 + TT0]
                        S.dma("pool", dst, xn[:], [xnk], [(kind + "Tn", t0, hh)], sem=xnk)
                    if kind in ("k", "v"):
                        pt, ptk = psT.next()
                        for jb in range(nb0):
                            S.tr(pt[:, jb, :], xn[:, jb * 128:(jb + 1) * 128], idb[:], [xnk, "idb"], [ptk])
                        tk, tkk = tk_p.next()
                        S.copy("act", tk[:], pt[:, 0:nb0, :], [ptk], [tkk])
                        dst_t = (ktok_s if kind == "k" else vtok_s).rearrange("(nb p) h d -> p nb h d", p=128)
                        S.dma("act", dst_t[:, t0 // 128:t0 // 128 + nb0, hh, :], tk[:], [tkk], [(kind + "tok", t0, hh)], sem=tkk)
            bet, bk = sc_p.next()
            S.dma("sp", bet[:], beta_d[:, t0:t0 + TT0], (), [bk], sem=bk)
            av, ak = sc_p.next()
            S.dma("sp", av[:], a_d[:, t0:t0 + TT0], (), [ak], sem=ak)
            S.act(bet[:], bet[:], AF.Sigmoid, [bk], [bk])
            S.act(av[:], av[:], AF.Exp, [ak, "dtb"], [ak], bias=dtb[:, 0:1])
            S.act(av[:], av[:], AF.Ln, [ak], [ak], bias=1.0)
            S.ts("dve", av[:], av[:], negA[:, 0:1], None, ALU.mult, None, [ak, "negA"], [ak])
            pss, psk = psS.next()
            for jb in range(nb0):
                S.tr(pss[:, jb, 0:16], bet[:, jb * 128:(jb + 1) * 128], cm[0:16, ID, 0:16], [bk, "cm"], [psk])
                S.tr(pss[:, jb, 16:32], av[:, jb * 128:(jb + 1) * 128], cm[0:16, ID, 0:16], [ak, "cm"], [psk])
            bgo, bgk = bgo_p.next()
            S.copy("act", bgo[:], pss[:, 0:nb0, :], [psk], [bgk])
            S.dma("act", bg_s.rearrange("(nb p) c -> p nb c", p=128)[:, t0 // 128:t0 // 128 + nb0, :], bgo[:],
                  [bgk], [("bg", t0)], sem=bgk)

    NSLOT = 2
    slots = {}
    for d in range(2):
        for r in range(NSLOT):
            nm = f"sl{d}{r}"
            slots[(d, r)] = dict(
                kT=S.sbuf([128, NKH, 128], BF16, nm + "kT"), qT=S.sbuf([128, NKH, 128], BF16, nm + "qT"),
                ktok=S.sbuf([128, NKH, 128], BF16, nm + "ktok"), vtok=S.sbuf([128, NVH, 128], BF16, nm + "vtok"),
                bg=S.sbuf([128, 32], F32, nm + "bg"), sc=S.sbuf([128, 48], F32, nm + "sc"),
                TU=S.sbuf([128, NVH, 128], BF16, nm + "TU"), AT=S.sbuf([128, NVH, 128], BF16, nm + "AT"),
                kdec=S.sbuf([128, NVH, 128], BF16, nm + "kdec"), nm=nm)
    gm2_p = Rot(S, "gm2", 2, [128, 16], F32)
    GsPs = {d: S.sbuf([128, 2, NKH, 128], F32, f"gsps{d}") for d in range(2)}
    Gm4_p = Rot(S, "gm4", 2, [128, 4, 128], F32)
    E4 = {(d, hg): S.sbuf([128, 4, 128], F32, f"e4_{d}{hg}") for d in range(2) for hg in range(2)}
    zb_pg = {(d, hg): Rot(S, f"zb{d}{hg}", 9, [128, 4, 128], BF16) for d in range(2) for hg in range(2)}
    st32 = {(d, hg): S.sbuf([128, 4, 128], F32, f"st32_{d}{hg}") for d in range(2) for hg in range(2)}
    stbf = {(d, hg): S.sbuf([128, 4, 128], BF16, f"stbf_{d}{hg}") for d in range(2) for hg in range(2)}
    R_p = Rot(S, "Rp", 4, [128, 4, 128], BF16)
    vn_p = Rot(S, "vnp", 4, [128, 4, 128], BF16)
    ot_p = Rot(S, "otp", 3, [128, 4, 128], F32)
    osb = {(d, r): S.sbuf([128, NVH, 128], F32, f"osb{d}{r}") for d in range(2) for r in range(2)}

    def blk_of(d, s):
        return s if d == 0 else NB - 1 - s

    def pre_setup(d, s):
        blk = blk_of(d, s)
        sl = slots[(d, s % NSLOT)]
        nm = sl["nm"]
        c0 = blk * 128
        Mincl, Mrev = (LE, GT) if d == 0 else (GE, LT)
        MstrU, MinclU = (LT, LE) if d == 0 else (GT, GE)
        tb = (c0 // TT0) * TT0
        S.dma("sp", sl["kT"][:], kTn_s[:, :, c0:c0 + 128], [("kTn", tb, hh) for hh in range(NKH)], [nm + "kT"], sem=nm + "kT")
        S.dma("sp", sl["qT"][:], qTn_s[:, :, c0:c0 + 128], [("qTn", tb, hh) for hh in range(NKH)], [nm + "qT"], sem=nm + "qT")
        S.dma("sp", sl["ktok"][:], ktok_s[c0:c0 + 128, :, :], [("ktok", tb, hh) for hh in range(NKH)], [nm + "ktok"], sem=nm + "ktok")
        S.dma("sp", sl["vtok"][:], vtok_s[c0:c0 + 128, :, :], [("vtok", tb, hh) for hh in range(NVH)], [nm + "vtok"], sem=nm + "vtok")
        S.dma("sp", sl["bg"][:], bg_s[c0:c0 + 128, :], [("bg", (c0 // TT0) * TT0)], [nm + "bg"], sem=nm + "bg")
        bg, sc = sl["bg"], sl["sc"]
        gsel = bg[:, 16 + d * 8:16 + d * 8 + 8]
        gm2, gm2k = gm2_p.next()
        S.ts("pool", gm2[:, 0:8], gsel, c01[:, 0:1], None, ALU.mult, None, [nm + "bg", "c01"], [gm2k])
        S.ts("pool", gm2[:, 8:16], gsel, c01[:, 1:2], None, ALU.mult, None, [nm + "bg", "c01"], [gm2k])
        pss, psk = psS.next()
        pv = pss[:].rearrange("p a b -> p (a b)")
        S.mm(pv[:, 0:8], cm[:, Mincl, :], gsel, True, True, ["cm", nm + "bg"], [psk])
        S.mm(pv[:, 8:16], cm[:, Mrev, :], gsel, True, True, ["cm", nm + "bg"], [psk])
        S.mm(pv[:, 16:32], onesf[:], gm2[:], True, True, ["onesf", gm2k], [psk])
        S.act(sc[:, 0:32], pv[:, 0:32], AF.Exp, [psk], [nm + "sc"])
        S.ts("pool", sc[:, 32:40], sc[:, 0:8], -1.0, None, ALU.mult, None, [nm + "sc"], [nm + "sc"])
        S.ts("pool", sc[:, 40:48], bg[:, d * 8:d * 8 + 8], -1.0, None, ALU.mult, None, [nm + "bg"], [nm + "sc"])
        pG, pGk = psA.next()
        pP, pPk = psA.next()
        for kh in range(NKH):
            S.mm(pG[:, kh, :], sl["kT"][:, kh, :], sl["kT"][:, kh, :], True, True, [nm + "kT"], [pGk])
            S.mm(pP[:, kh, :], sl["kT"][:, kh, :], sl["qT"][:, kh, :], True, True, [nm + "kT", nm + "qT"], [pPk])
        gp, gpk = GsPs[d], ("gsps", d)
        S.tt("dve", gp[:, 0, :, :], pG[:], cm[:, MstrU:MstrU + 1, :].broadcast_to([128, NKH, 128]), ALU.mult,
             [pGk, "cm"], [(gpk, 0)])
        S.tt("dve", gp[:, 1, :, :], pP[:], cm[:, MinclU:MinclU + 1, :].broadcast_to([128, NKH, 128]), ALU.mult,
             [pPk, "cm"], [(gpk, 1)])

    def pre_group(d, s, hg):
        sl = slots[(d, s % NSLOT)]
        nm = sl["nm"]
        Mincl, Mrev = (LE, GT) if d == 0 else (GE, LT)
        bg, sc = sl["bg"], sl["sc"]
        gp, gpk = GsPs[d], ("gsps", d)
        zb = zb_pg[(d, hg)]
        gm4, gm4k = Gm4_p.next()
        pD, pDk = psA.next()
        for q in range(4):
            vh = 4 * hg + q
            S.ts("pool", gm4[:, q, :], cm[:, Mincl, :], bg[:, 16 + d * 8 + vh:16 + d * 8 + vh + 1], None, ALU.mult, None,
                 ["cm", nm + "bg"], [(gm4k, q)])
            S.mm(pD[:, q, :], cm[:, Mrev, :], gm4[:, q, :], True, True, ["cm", (gm4k, q)], [pDk])
        e4, e4k = E4[(d, hg)], ("e4", d, hg)
        S.act(e4[:], pD[:], AF.Exp, [pDk], [e4k])
        yield
        zu, zuk = zb.next()
        for q in range(4):
            vh = 4 * hg + q
            kh = vh // 2
            S.tt("pool", sl["AT"][:, vh, :], e4[:, q, :], gp[:, 1, kh, :], ALU.mult, [e4k, (gpk, 1)], [(nm + "AT", vh)])
            S.stt("dve", zu[:, q, :], e4[:, q, :], sc[:, 40 + vh:41 + vh], gp[:, 0, kh, :], ALU.mult, ALU.mult,
                  [e4k, nm + "sc", (gpk, 0)], [(zuk, q)])
            S.act(sl["kdec"][:, vh, :], sl["ktok"][:, kh, :], AF.Copy, [nm + "ktok", nm + "sc"], [(nm + "kdec", vh)],
                  scale=sc[:, 8 + vh:9 + vh])
        pt, ptk = psT.next()
        for q in range(4):
            S.tr(pt[:, q, :], zu[:, q, :], idb[:], [(zuk, q), "idb"], [ptk])
        z, zk = zb.next()
        S.copy("act", z[:], pt[:, 0:4, :], [ptk], [zk])
        zuk_all = [(zuk, q) for q in range(4)]
        tu, tuk = zb.next()
        tl, tlk = zb.next()
        idbb = idb[:].unsqueeze(1).broadcast_to([128, 4, 128])
        S.tt("pool", tu[:], zu[:], idbb, ALU.add, zuk_all + ["idb"], [tuk])
        S.tt("pool", tl[:], z[:], idbb, ALU.add, [zk, "idb"], [tlk])
        zp, zpk, zup, zupk = z, [zk], zu, zuk_all
        yield
        for k in range(1, 6):
            pB, pBk = psA.next()
            for q in range(4):
                S.mm(pB[:, q, :], zp[:, q, :], zup[:, q, :], True, True, zpk + zupk, [pBk])
            if k < 5:
                pA_, pAk = psA.next()
                for q in range(4):
                    S.mm(pA_[:, q, :], zup[:, q, :], zp[:, q, :], True, True, zpk + zupk, [pAk])
            nzup, nzupk = zb.next()
            S.copy("dve", nzup[:], pB[:], [pBk], [nzupk])
            if k < 5:
                nzp, nzpk = zb.next()
                S.copy("act", nzp[:], pA_[:], [pAk], [nzpk])
            yield
            pC, pCk = psA.next()
            for q in range(4):
                S.mm(pC[:, q, :], tl[:, q, :], nzup[:, q, :], True, True, [tlk, nzupk], [pCk])
            if k < 5:
                pE, pEk = psA.next()
                for q in range(4):
                    S.mm(pE[:, q, :], tu[:, q, :], nzp[:, q, :], True, True, [tuk, nzpk], [pEk])
                ntu, ntuk = zb.next()
                S.tt("dve", ntu[:], pC[:], tu[:], ALU.add, [pCk, tuk], [ntuk])
                ntl, ntlk = zb.next()
                S.tt("dve", ntl[:], pE[:], tl[:], ALU.add, [pEk, tlk], [ntlk])
                tu, tuk, tl, tlk = ntu, ntuk, ntl, ntlk
                zp, zpk, zup, zupk = nzp, [nzpk], nzup, [nzupk]
            else:
                S.tt("dve", sl["TU"][:, 4 * hg:4 * hg + 4, :], pC[:], tu[:], ALU.add, [pCk, tuk],
                     [(nm + "TU", 4 * hg + q) for q in range(4)])
            yield

    def loop(s):
        for hi in range(2):
            groups = []
            for d in range(2):
                sl = slots[(d, s % NSLOT)]
                hf = hi if d == 0 else 1 - hi
                for hg in range(2):
                    groups.append((d, hg, sl, hf, slice(64 * hf, 64 * hf + 64), sl["nm"]))
            ksl = []
            for d, hg, sl, hf, pr, nm in groups:
                bank, bk = next_half(hf)
                for q in range(4):
                    vh = 4 * hg + q
                    S.mm(bank[pr, q, :], sl["kT"][:, vh // 2, 64 * hf:64 * hf + 64], stbf[(d, hg)][:, q, :], True, True,
                         [nm + "kT", ("stbf", d, hg)], [bk])
                ksl.append((bank, bk))
            yield
            Rl = []
            for (d, hg, sl, hf, pr, nm), (bank, bk) in zip(groups, ksl):
                R, Rk = R_p.next()
                for q in range(4):
                    vh = 4 * hg + q
                    S.stt("dve", R[pr, q, :], bank[pr, q, :], sl["sc"][pr, 32 + vh:33 + vh], sl["vtok"][pr, vh, :],
                          ALU.mult, ALU.add, [bk, nm + "sc", nm + "vtok"], [Rk])
                Rl.append((R, Rk))
            yield
            vl = []
            for (d, hg, sl, hf, pr, nm), (R, Rk) in zip(groups, Rl):
                bank, bk = next_half(hf)
                for q in range(4):
                    vh = 4 * hg + q
                    S.mm(bank[pr, q, :], sl["TU"][pr, vh, 64 * hf:64 * hf + 64], R[pr, q, :], True, True,
                         [(nm + "TU", vh), Rk], [bk])
                vn, vnk = vn_p.next()
                for q in range(4):
                    vh = 4 * hg + q
                    S.act(vn[pr, q, :], bank[pr, q, :], AF.Copy, [bk, nm + "bg"], [vnk],
                          scale=sl["bg"][pr, d * 8 + vh:d * 8 + vh + 1])
                vl.append((vn, vnk))
            yield
            for gi_, ((d, hg, sl, hf, pr, nm), (vn, vnk)) in enumerate(zip(groups, vl)):
                if gi_ == 2:
                    yield
                ob = osb[(d, s % 2)]
                b1, b1k = next_half(hf)
                for q in range(4):
                    vh = 4 * hg + q
                    S.mm(b1[pr, q, :], sl["AT"][pr, vh, 64 * hf:64 * hf + 64], vn[pr, q, :], True, True,
                         [(nm + "AT", vh), vnk], [b1k])
                b2, b2k = next_half(hf)
                for q in range(4):
                    vh = 4 * hg + q
                    S.mm(b2[pr, q, :], sl["qT"][:, vh // 2, 64 * hf:64 * hf + 64], stbf[(d, hg)][:, q, :], True, True,
                         [nm + "qT", ("stbf", d, hg)], [b2k])
                b3, b3k = next_whole()
                for q in range(4):
                    vh = 4 * hg + q
                    S.mm(b3[:, q, :], sl["kdec"][pr, vh, :], vn[pr, q, :], True, True, [(nm + "kdec", vh), vnk], b3k)
                ot, otk = ot_p.next()
                S.copy("act", ot[pr, :, :], b1[pr, :, :], [b1k], [otk])
                for q in range(4):
                    vh = 4 * hg + q
                    S.stt("dve", ob[pr, vh, :], b2[pr, q, :], sl["sc"][pr, vh:vh + 1], ot[pr, q, :], ALU.mult, ALU.add,
                          [b2k, nm + "sc", otk], [("osb", d, s % 2, vh, hf)])
                for q in range(4):
                    vh = 4 * hg + q
                    S.stt("dve", st32[(d, hg)][:, q, :], st32[(d, hg)][:, q, :],
                          sl["sc"][:, 16 + hf * 8 + vh:17 + hf * 8 + vh], b3[:, q, :], ALU.mult, ALU.add,
                          [("st32", d, hg), nm + "sc"] + b3k, [("st32", d, hg)])
                S.copy("act", stbf[(d, hg)][:], st32[(d, hg)][:], [("st32", d, hg)], [("stbf", d, hg)])
            yield
        for d in range(2):
            blk = blk_of(d, s)
            S.dma("pool", o_s[d, blk * 128:(blk + 1) * 128, :, :], osb[(d, s % 2)][:],
                  [("osb", d, s % 2, vh, hf) for vh in range(NVH) for hf in range(2)], [("o_s", d, blk)], sem=("osb", d, s % 2))

    if 1 in stages:
        for dd in range(2):
            for hg in range(2):
                S.memset("pool", st32[(dd, hg)][:], 0.0, [("st32", dd, hg)])
                S.memset("pool", stbf[(dd, hg)][:], 0.0, [("stbf", dd, hg)])
        def lockstep(gens):
            gens = [(i, g) for i, g in enumerate(gens)]
            r = 0
            while gens:
                alive = []
                for i, g in gens:
                    if r >= i:
                        try:
                            next(g)
                        except StopIteration:
                            continue
                    alive.append((i, g))
                gens = alive
                r += 1

        def pre_gens(s):
            for d in range(2):
                pre_setup(d, s)
            return [pre_group(d, s, hg) for d in range(2) for hg in range(2)]

        lockstep(pre_gens(0))
        for s in range(NB):
            gens = [loop(s)]
            if s + 1 < NB:
                gens = gens + pre_gens(s + 1)
            lockstep(gens)

    if 2 in stages:
        of_p = Rot(S, "s2of", 2, [128, NVH, 128], F32)
        ob_p = Rot(S, "s2ob", 2, [128, NVH, 128], F32)
        on_p = Rot(S, "s2on", 2, [128, NVH, 128], BF16)
        z_p = Rot(S, "s2z", 2, [128, NVH, 128], F32)
        ss_p = Rot(S, "s2ss", 2, [128, NVH], F32)
        for blk in range(NB):
            c0 = blk * 128
            of, ofk = of_p.next()
            ob2, obk = ob_p.next()
            S.dma("sp", of[:], o_s[0, c0:c0 + 128, :, :], [("o_s", 0, blk)], [ofk], sem=ofk)
            S.dma("sp", ob2[:], o_s[1, c0:c0 + 128, :, :], [("o_s", 1, blk)], [obk], sem=obk)
            zt, ztk = z_p.next()
            S.dma("sp", zt[:], zT_d.rearrange("(h p) s -> p h s", p=128)[:, :, c0:c0 + 128], (), [ztk], sem=ztk)
            S.tt("pool", of[:], of[:], ob2[:], ALU.add, [ofk, obk], [ofk])
            sq2, sq2k = ob2, obk
            S.tt("pool", sq2[:], of[:], of[:], ALU.mult, [ofk], [sq2k])
            ss, ssk = ss_p.next()
            S.op("dve", lambda e, ss=ss, sq2=sq2: e.tensor_reduce(ss[:], sq2[:], mybir.AxisListType.X, ALU.add), [sq2k], [ssk])
            S.act(ss[:], ss[:], AF.Sqrt, [ssk, "epsb"], [ssk], bias=epsb[:, 0:1], scale=1.0 / 128)
            S.op("dve", lambda e, ss=ss: e.reciprocal(ss[:], ss[:]), [ssk], [ssk])
            on, onk = on_p.next()
            S.tt("dve", on[:], of[:], ss[:].unsqueeze(2).broadcast_to([128, NVH, 128]), ALU.mult, [ofk, ssk], [onk])
            S.act(zt[:], zt[:], AF.Silu, [ztk], [ztk])
            pt, ptk = psT.next()
            for vh in range(NVH):
                S.tr(pt[:, vh, :], on[:, vh, :], idb[:], [onk, "idb"], [ptk])
            oo, ook = of, ofk
            S.stt("dve", oo[:], pt[:], onw[:, 0:1], zt[:], ALU.mult, ALU.mult, [ptk, "onw", ztk, onk], [ook])
            S.dma("pool", oT_d.rearrange("(h p) s -> p h s", p=128)[:, :, c0:c0 + 128], oo[:], [ook], [], sem=ook)
    S.finish()
    return nc


SEQ = 8192
BATCH = 2
TPC = BATCH * SEQ // NCORES
DN_N = 12416
LRU_N = 4096
_PROG_CACHE = {}


def _c(a):
    return np.ascontiguousarray(a, dtype=np.float32)


def _nw(w):
    return _c(w.reshape(D // 128, 128).T)


def _run(nc, in_maps):
    import os, time
    t0 = time.time()
    r = run_bass_kernel_spmd(nc, in_maps, core_ids=list(range(NCORES))).results
    if os.environ.get("KDEBUG"):
        print(f"[kernel] launch took {time.time() - t0:.1f}s", flush=True)
    return r


def _dense_prog(sig):
    nc = bass.Bass("TRN2", target_bir_lowering=False)
    ops = []
    for o in sig:
        if o[0] == "outproj":
            ops.append(dict(op="outproj", dm=o[1]))
        elif o[0] == "inproj":
            ops.append(dict(op="inproj", n=o[1]))
        else:
            ops.append(dict(op=o[0]))
    dense_phase(nc, TPC, ops)
    return nc


def kernel(x, ffn1_norm, ffn1_w_gate_up, ffn1_w_down, mix_norm, ffn2_norm, ffn2_w_gate_up, ffn2_w_down,
           dn_w_in, dn_conv_w, dn_a_log, dn_dt_bias, dn_out_norm, dn_w_out,
           lru_w_in, lru_conv_w, lru_conv_b, lru_w_gate_a, lru_b_gate_a, lru_w_gate_x, lru_b_gate_x,
           lru_lambda, lru_w_out, final_norm):
    x = np.asarray(x, np.float32)
    depth = ffn1_norm.shape[0]
    seg = lambda c: (c // 4, slice((c % 4) * TPC, (c % 4 + 1) * TPC))
    hT = [_c(x[seg(c)[0], seg(c)[1], :].T) for c in range(NCORES)]
    mix_out = None
    cm, c01 = dn_consts()
    y = None
    for layer in range(depth + 1):
        sig = []
        common = {}
        if layer > 0:
            pl = layer - 1
            j = pl // 2
            wo = dn_w_out[j] if pl % 2 == 0 else lru_w_out[j]
            idx = len(sig)
            sig.append(("outproj", wo.shape[0]))
            common[f"wo{idx}"] = _c(wo)
            oidx = idx
            idx = len(sig)
            sig.append(("ffn",))
            common[f"nw{idx}"] = _nw(ffn2_norm[pl])
            common[f"wgu{idx}"] = _c(ffn2_w_gate_up[pl])
            common[f"wd{idx}"] = _c(ffn2_w_down[pl])
        if layer < depth:
            idx = len(sig)
            sig.append(("ffn",))
            common[f"nw{idx}"] = _nw(ffn1_norm[layer])
            common[f"wgu{idx}"] = _c(ffn1_w_gate_up[layer])
            common[f"wd{idx}"] = _c(ffn1_w_down[layer])
            j = layer // 2
            wi = dn_w_in[j] if layer % 2 == 0 else lru_w_in[j]
            idx = len(sig)
            sig.append(("inproj", wi.shape[1]))
            common[f"nw{idx}"] = _nw(mix_norm[layer])
            common[f"wi{idx}"] = _c(wi)
            pidx = idx
            sig.append(("store_h",))
            hidx = len(sig) - 1
        else:
            idx = len(sig)
            sig.append(("final",))
            common[f"nw{idx}"] = _nw(final_norm)
            fidx = idx
        sig = tuple(sig)
        if sig not in _PROG_CACHE:
            _PROG_CACHE[sig] = _dense_prog(sig)
        in_maps = []
        for c in range(NCORES):
            m = dict(common)
            m["hT_in"] = hT[c]
            if layer > 0:
                m[f"oT{oidx}"] = mix_out[c]
            in_maps.append(m)
        res = _run(_PROG_CACHE[sig], in_maps)
        if layer == depth:
            y = np.empty((BATCH, SEQ, D), np.float32)
            for c in range(NCORES):
                b, sl = seg(c)
                y[b, sl, :] = res[c][f"y{fidx}"].T
            break
        hT = [res[c][f"hT_out{hidx}"] for c in range(NCORES)]
        proj = [np.concatenate([res[b * 4 + s][f"proj{pidx}"] for s in range(4)], axis=1) for b in range(BATCH)]
        del res
        j = layer // 2
        in_maps = []
        if layer % 2 == 0:
            key = ("dn",)
            if key not in _PROG_CACHE:
                nc = bass.Bass("TRN2", target_bir_lowering=False)
                dn_phase(nc, SEQ)
                _PROG_CACHE[key] = nc
            cwv_all = dn_conv_w[j]
            for c in range(NCORES):
                b, g4 = c // 4, c % 4
                P = proj[b]
                kh0, vh0 = 4 * g4, 8 * g4
                ba_rows = lambda kind: np.concatenate([P[12288 + d * 64 + kind * 32 + vh0: 12288 + d * 64 + kind * 32 + vh0 + 8]
                                                       for d in range(2)], axis=0)
                cwl = lambda w, nh: _c(w.reshape(4, nh, 128).transpose(2, 1, 0))
                in_maps.append({
                    "qT": _c(P[kh0 * 128:(kh0 + 4) * 128]), "kT": _c(P[2048 + kh0 * 128:2048 + (kh0 + 4) * 128]),
                    "vT": _c(P[4096 + vh0 * 128:4096 + (vh0 + 8) * 128]), "zT": _c(P[8192 + vh0 * 128:8192 + (vh0 + 8) * 128]),
                    "betaT": _c(ba_rows(0)), "aT": _c(ba_rows(1)),
                    "cwq": cwl(cwv_all[:, kh0 * 128:(kh0 + 4) * 128], 4),
                    "cwk": cwl(cwv_all[:, 2048 + kh0 * 128:2048 + (kh0 + 4) * 128], 4),
                    "cwv": cwl(cwv_all[:, 4096 + vh0 * 128:4096 + (vh0 + 8) * 128], 8),
                    "alog": _c(dn_a_log[j][:, vh0:vh0 + 8].reshape(16, 1)),
                    "dtb": _c(dn_dt_bias[j][:, vh0:vh0 + 8].reshape(16, 1)),
                    "onw": _c(dn_out_norm[j].reshape(128, 1)), "cmask": cm, "c01": c01})
            res = _run(_PROG_CACHE[key], in_maps)
            outs = [res[c]["oT"] for c in range(NCORES)]
        else:
            key = ("lru",)
            if key not in _PROG_CACHE:
                nc = bass.Bass("TRN2", target_bir_lowering=False)
                lru_phase(nc, SEQ)
                _PROG_CACHE[key] = nc
            for c in range(NCORES):
                b, g4 = c // 4, c % 4
                P = proj[b]
                ch = slice(512 * g4, 512 * g4 + 512)
                pl1 = lambda v: _c(v.reshape(4, 128).T)
                pl2 = lambda v: _c(v.reshape(2, 4, 128).transpose(2, 0, 1))
                in_maps.append({
                    "xbT": _c(P[512 * g4:512 * g4 + 512]), "gateT": _c(P[2048 + 512 * g4:2048 + 512 * g4 + 512]),
                    "cw": _c(lru_conv_w[j][:, ch].reshape(4, 4, 128).transpose(2, 1, 0)), "cb": pl1(lru_conv_b[j][ch]),
                    "wga": _c(lru_w_gate_a[j][:, 2 * g4:2 * g4 + 2]), "wgx": _c(lru_w_gate_x[j][:, 2 * g4:2 * g4 + 2]),
                    "bga": pl2(lru_b_gate_a[j][:, ch]), "bgx": pl2(lru_b_gate_x[j][:, ch]), "lam": pl2(lru_lambda[j][:, ch])})
            res = _run(_PROG_CACHE[key], in_maps)
            outs = [res[c]["yT"] for c in range(NCORES)]
        del proj, in_maps
        full = [np.concatenate([outs[b * 4 + g] for g in range(4)], axis=0) for b in range(BATCH)]
        mix_out = [_c(full[c // 4][:, seg(c)[1]]) for c in range(NCORES)]
        del res, outs, full
    return y
```

```python
import contextlib
import numpy as np
import concourse.bass as bass
import concourse.mybir as mybir
from concourse.bass_utils import run_bass_kernel_spmd

F32 = mybir.dt.float32
BF16 = mybir.dt.bfloat16
AF = mybir.ActivationFunctionType
ALU = mybir.AluOpType

D = 2048
FF = 5632
NCORES = 8
EPS = 1e-6
SAME_ENGINE_SYNC = True


class Sched:
    ENG = ("pe", "act", "dve", "pool", "sp")

    def __init__(self, nc):
        self.nc = nc
        self.streams = {e: [] for e in self.ENG}
        self.ccnt = {e: 0 for e in self.ENG}
        self.dcnt = {}
        self.dq = {}
        self.seen = {e: {} for e in self.ENG}
        self.last_w = {}
        self.rd = {}
        self.stack = contextlib.ExitStack()
        self.nbuf = 0
        self.psum_banks = []
        self.psum_i = 0

    def sbuf(self, shape, dt, name=None):
        self.nbuf += 1
        name = "sb_" + (name or f"{self.nbuf}")
        return self.stack.enter_context(self.nc.sbuf_tensor(name, list(shape), dt))

    def psum(self, shape, dt=F32, name=None):
        self.nbuf += 1
        name = "ps_" + (name or f"{self.nbuf}")
        return self.stack.enter_context(self.nc.psum_tensor(name, list(shape), dt))

    def _deps(self, eng, reads, writes):
        deps = []
        for k in reads:
            if k in self.last_w:
                deps.append(self.last_w[k])
        for k in writes:
            if k in self.last_w:
                deps.append(self.last_w[k])
            deps.extend(self.rd.get(k, ()))
        need = {}
        for sk, v in deps:
            if sk == ("c", "pe") and eng == "pe":
                continue
            if sk == ("c", eng) and not SAME_ENGINE_SYNC:
                continue
            if v > need.get(sk, 0):
                need[sk] = v
        for sk, v in need.items():
            if v > self.seen[eng].get(sk, 0):
                self.seen[eng][sk] = v
                self.streams[eng].append(("wait", sk, v))

    def _post(self, tok, reads, writes):
        for k in reads:
            self.rd.setdefault(k, []).append(tok)
        for k in writes:
            self.last_w[k] = tok
            self.rd[k] = []

    def op(self, eng, fn, reads=(), writes=()):
        self._deps(eng, reads, writes)
        self.ccnt[eng] += 1
        tok = (("c", eng), self.ccnt[eng])
        self.streams[eng].append(("op", fn, tok))
        self._post(tok, reads, writes)

    def dma(self, eng, out, in_, reads=(), writes=(), sem=None):
        assert sem is not None
        self._deps(eng, reads, writes)
        sk = ("d", "sw" if eng == "pool" else "hw", sem)
        prev = self.dcnt.get(sk, 0)
        if prev > self.seen[eng].get(sk, 0):
            self.seen[eng][sk] = prev
            self.streams[eng].append(("wait", sk, prev))
        self.dcnt[sk] = prev + 16
        self.dq.setdefault(eng, set()).add(sk)
        tok = (sk, self.dcnt[sk])
        self.streams[eng].append(("dma", (out, in_), tok))
        self._post(tok, reads, writes)

    def mm(self, ps, lhsT, rhs, start, stop, reads, writes):
        self.op("pe", lambda e: e.matmul(ps, lhsT, rhs, start=start, stop=stop), reads, writes)

    def tr(self, ps, in_, ident, reads, writes):
        self.op("pe", lambda e: e.transpose(ps, in_, ident), reads, writes)

    def act(self, out, in_, func, reads, writes, bias=None, scale=None):
        kw = {}
        if bias is not None:
            kw["bias"] = bias
        if scale is not None:
            kw["scale"] = scale
        self.op("act", lambda e: e.activation(out=out, in_=in_, func=func, **kw), reads, writes)

    def tt(self, eng, out, in0, in1, op, reads, writes):
        self.op(eng, lambda e: e.tensor_tensor(out, in0, in1, op), reads, writes)

    def ts(self, eng, out, in0, s1, s2, op0, op1, reads, writes):
        if s2 is None and eng == "pool" and op0 == ALU.mult:
            self.op(eng, lambda e: e.tensor_scalar(out, in0, s1, 1.0, ALU.mult, ALU.mult), reads, writes)
        elif s2 is None:
            self.op(eng, lambda e: e.tensor_scalar(out, in0, s1, None, op0), reads, writes)
        else:
            self.op(eng, lambda e: e.tensor_scalar(out, in0, s1, s2, op0, op1), reads, writes)

    def stt(self, eng, out, in0, scalar, in1, op0, op1, reads, writes):
        self.op(eng, lambda e: e.scalar_tensor_tensor(out, in0, scalar, in1, op0, op1), reads, writes)

    def copy(self, eng, out, in_, reads, writes):
        if eng == "act":
            self.op(eng, lambda e: e.copy(out, in_), reads, writes)
        else:
            self.op(eng, lambda e: e.tensor_copy(out, in_), reads, writes)

    def memset(self, eng, ap, val, writes):
        self.op(eng, lambda e: e.memset(ap, val), (), writes)

    def finish(self):
        nc = self.nc
        st = self.stack
        sems = {}
        for e in self.ENG:
            sems[("c", e)] = st.enter_context(nc.semaphore(f"c_{e}"))
        for i, sk in enumerate(self.dcnt):
            sems[sk] = st.enter_context(nc.semaphore(f"d_{i}"))
        for e, sks in self.dq.items():
            for sk in sks:
                if self.dcnt[sk] > self.seen[e].get(sk, 0):
                    self.streams[e].append(("wait", sk, self.dcnt[sk]))
        streams = self.streams

        def replay(name, eng):
            for item in streams[name]:
                if item[0] == "wait":
                    eng.wait_ge(sems[item[1]], item[2])
                elif item[0] == "op":
                    item[1](eng).then_inc(sems[item[2][0]], 1)
                else:
                    o, i = item[1]
                    eng.dma_start(out=o, in_=i).then_inc(sems[item[2][0]], 16)

        with nc.Block() as block:
            @block.tensor
            def _(e):
                replay("pe", e)

            @block.scalar
            def _(e):
                replay("act", e)

            @block.vector
            def _(e):
                replay("dve", e)

            @block.gpsimd
            def _(e):
                replay("pool", e)

            @block.sync
            def _(e):
                replay("sp", e)
        st.close()


class Rot:
    def __init__(self, S, name, n, shape, dt, psum=False):
        self.tiles = [(S.psum(shape, dt, f"{name}{i}") if psum else S.sbuf(shape, dt, f"{name}{i}")) for i in range(n)]
        self.keys = [(name, i) for i in range(n)]
        self.i = 0

    def next(self):
        t, k = self.tiles[self.i], self.keys[self.i]
        self.i = (self.i + 1) % len(self.tiles)
        return t, k


def dense_phase(nc, T, ops, TT=512):
    S = Sched(nc)
    KC = D // 128
    FC = FF // 128
    NT = T // TT
    dram = {}

    def din(name, shape):
        dram[name] = nc.dram_tensor(name, list(shape), F32, kind="ExternalInput").ap()
        return dram[name]

    def dout(name, shape):
        dram[name] = nc.dram_tensor(name, list(shape), F32, kind="ExternalOutput").ap()
        return dram[name]

    hT_in = din("hT_in", [D, T])
    need_h_out = any(o["op"] == "store_h" for o in ops)
    for idx, o in enumerate(ops):
        if o["op"] == "ffn":
            o["nw"] = din(f"nw{idx}", [128, KC])
            o["wgu"] = din(f"wgu{idx}", [D, 2 * FF])
            o["wd"] = din(f"wd{idx}", [FF, D])
        elif o["op"] == "outproj":
            o["oT"] = din(f"oT{idx}", [o["dm"], T])
            o["wo"] = din(f"wo{idx}", [o["dm"], D])
        elif o["op"] == "inproj":
            o["nw"] = din(f"nw{idx}", [128, KC])
            o["wi"] = din(f"wi{idx}", [D, o["n"]])
            o["out"] = dout(f"proj{idx}", [o["n"], T])
        elif o["op"] == "final":
            o["nw"] = din(f"nw{idx}", [128, KC])
            o["out"] = dout(f"y{idx}", [D, T])
        elif o["op"] == "store_h":
            o["out"] = dout(f"hT_out{idx}", [D, T])

    h = S.sbuf([128, KC, TT], F32, "h")
    xn = S.sbuf([128, KC, TT], BF16, "xn")
    actb = S.sbuf([128, FC, TT], BF16, "actb")
    ones = S.sbuf([128, 128], BF16, "ones")
    nws = {}
    for idx, o in enumerate(ops):
        if "nw" in o:
            nws[idx] = S.sbuf([128, KC], F32, f"nwt{idx}")
    WG = 256
    wgu_pool = Rot(S, "wgu", 2, [128, 2, KC, WG], BF16)
    wd_pool = Rot(S, "wd", 2, [128, FC, WG], BF16)
    sq_pool = Rot(S, "sq", 2, [128, TT], BF16)
    tmp_pool = Rot(S, "tmp", 3, [128, TT], F32)
    rstd = S.sbuf([128, TT], F32, "rstd")
    psp = Rot(S, "psb", 8, [128, TT], F32, psum=True)

    S.memset("dve", ones[:], 1.0, ["ones"])
    for idx in nws:
        S.dma("sp", nws[idx][:], ops[idx]["nw"], (), [("nw", idx)], sem=("nw", idx))

    def rmsnorm(nwidx, out_fp32_cb=None):
        ss, ssk = psp.next()
        for c in range(KC):
            sq, sqk = sq_pool.next()
            S.act(sq[:], h[:, c, :], AF.Square, [("h", c)], [sqk])
            S.mm(ss[:], ones[:], sq[:], c == 0, c == KC - 1, ["ones", sqk], [ssk])
        S.act(rstd[:], ss[:], AF.Sqrt, [ssk], ["rstd"], bias=EPS_AP[0][:, 0:1], scale=1.0 / D)
        S.op("dve", lambda e: e.reciprocal(rstd[:], rstd[:]), ["rstd"], ["rstd"])
        nwt = nws[nwidx]
        for c in range(KC):
            if out_fp32_cb is None:
                S.stt("dve", xn[:, c, :], h[:, c, :], nwt[:, c:c + 1], rstd[:], ALU.mult, ALU.mult,
                      [("h", c), ("nw", nwidx), "rstd"], [("xn", c)])
            else:
                out_fp32_cb(c, nwt)

    EPS_AP = [S.sbuf([128, 1], F32, "epsb")]
    S.memset("dve", EPS_AP[0][:], EPS, ["epsb"])

    def linear_acc(w_ap, kc, ncols, x_tile, xkeys, w_pool_kind, epilogue):
        for g0 in range(0, ncols, WG):
            gw = min(WG, ncols - g0)
            wt, wk = wd_pool.next()
            src = w_ap[:, g0:g0 + gw].rearrange("(c p) n -> p c n", p=128)
            half = (kc + 1) // 2
            S.dma("pool", wt[:, 0:half, 0:gw], src[:, 0:half, :], (), [(wk, 0)], sem=(wk, 0))
            if kc > half:
                S.dma("pool", wt[:, half:kc, 0:gw], src[:, half:kc, :], (), [(wk, 1)], sem=(wk, 1))
            for j in range(gw // 128):
                ps, pk = psp.next()
                for c in range(kc):
                    S.mm(ps[:], wt[:, c, j * 128:(j + 1) * 128], x_tile[:, c, :], c == 0, c == kc - 1,
                         [(wk, 0 if c < half else 1), xkeys(c)], [pk])
                epilogue(g0 // 128 + j, ps, pk)

    for t in range(NT):
        tsl = slice(t * TT, (t + 1) * TT)
        hv = hT_in.rearrange("(c p) t -> p c t", p=128)
        for c0 in range(0, KC, 4):
            S.dma("sp", h[:, c0:c0 + 4, :], hv[:, c0:c0 + 4, tsl], (), [("h", c) for c in range(c0, c0 + 4)], sem=("h", c0))
        for idx, o in enumerate(ops):
            if o["op"] == "ffn":
                rmsnorm(idx)
                for g0 in range(0, FF, WG):
                    wt, wk = wgu_pool.next()
                    for s in range(2):
                        src = o["wgu"][:, s * FF + g0: s * FF + g0 + WG].rearrange("(c p) n -> p c n", p=128)
                        S.dma("pool", wt[:, s, :, :], src, (), [(wk, s)], sem=(wk, s))
                    for j in range(WG // 128):
                        fc = g0 // 128 + j
                        gp, gk = psp.next()
                        up, uk = psp.next()
                        for c in range(KC):
                            S.mm(gp[:], wt[:, 0, c, j * 128:(j + 1) * 128], xn[:, c, :], c == 0, c == KC - 1,
                                 [(wk, 0), ("xn", c)], [gk])
                        for c in range(KC):
                            S.mm(up[:], wt[:, 1, c, j * 128:(j + 1) * 128], xn[:, c, :], c == 0, c == KC - 1,
                                 [(wk, 1), ("xn", c)], [uk])
                        tm, tk = tmp_pool.next()
                        S.act(tm[:], gp[:], AF.Silu, [gk], [tk])
                        S.tt("dve", actb[:, fc, :], tm[:], up[:], ALU.mult, [tk, uk], [("act", fc)])

                def epi_down(dc, ps, pk):
                    S.stt("dve", h[:, dc, :], ps[:], 0.5, h[:, dc, :], ALU.mult, ALU.add,
                          [pk, ("h", dc)], [("h", dc)])
                linear_acc(o["wd"], FC, D, actb, lambda c: ("act", c), "wd", epi_down)
            elif o["op"] == "outproj":
                kc = o["dm"] // 128
                ov = o["oT"].rearrange("(c p) t -> p c t", p=128)
                for c0 in range(0, kc, 8):
                    S.dma("pool", actb[:, c0:c0 + 8, :], ov[:, c0:c0 + 8, tsl], (),
                          [("act", c) for c in range(c0, c0 + 8)], sem=("actb", c0))

                def epi_out(dc, ps, pk):
                    S.tt("dve", h[:, dc, :], ps[:], h[:, dc, :], ALU.add, [pk, ("h", dc)], [("h", dc)])
                linear_acc(o["wo"], kc, D, actb, lambda c: ("act", c), "wd", epi_out)
            elif o["op"] == "inproj":
                rmsnorm(idx)
                outv = o["out"]

                def epi_in(nc_, ps, pk, outv=outv):
                    tm, tk = tmp_pool.next()
                    S.copy("act", tm[:], ps[:], [pk], [tk])
                    S.dma("sp", outv[nc_ * 128:(nc_ + 1) * 128, tsl], tm[:], [tk], [], sem=tk)
                linear_acc(o["wi"], KC, o["n"], xn, lambda c: ("xn", c), "wd", epi_in)
            elif o["op"] == "final":
                outv = o["out"].rearrange("(c p) t -> p c t", p=128)

                def cb(c, nwt, outv=outv, idx=idx):
                    tm, tk = tmp_pool.next()
                    S.stt("dve", tm[:], h[:, c, :], nwt[:, c:c + 1], rstd[:], ALU.mult, ALU.mult,
                          [("h", c), ("nw", idx), "rstd"], [tk])
                    S.dma("sp", outv[:, c, tsl], tm[:], [tk], [], sem=tk)
                rmsnorm(idx, cb)
            elif o["op"] == "store_h":
                outv = o["out"].rearrange("(c p) t -> p c t", p=128)
                for c0 in range(0, KC, 4):
                    S.dma("sp", outv[:, c0:c0 + 4, tsl], h[:, c0:c0 + 4, :],
                          [("h", c) for c in range(c0, c0 + 4)], [], sem=("h", c0))
    S.finish()
    return nc


def lru_phase(nc, Sq, TT=512):
    S = Sched(nc)
    NT = Sq // TT
    NCH = 4

    def din(name, shape):
        return nc.dram_tensor(name, list(shape), F32, kind="ExternalInput").ap()

    xbT = din("xbT", [512, Sq])
    gateT = din("gateT", [512, Sq])
    cw_d = din("cw", [128, NCH, 4])
    cb_d = din("cb", [128, NCH])
    wga_d = din("wga", [2, 2, 256, 256])
    wgx_d = din("wgx", [2, 2, 256, 256])
    bga_d = din("bga", [128, 2, NCH])
    bgx_d = din("bgx", [128, 2, NCH])
    lam_d = din("lam", [128, 2, NCH])
    yT = nc.dram_tensor("yT", [512, Sq], F32, kind="ExternalOutput").ap()

    cw = S.sbuf([128, NCH, 4], F32, "cw")
    cb = S.sbuf([128, NCH], F32, "cb")
    bga = S.sbuf([128, 2, NCH], F32, "bga")
    bgx = S.sbuf([128, 2, NCH], F32, "bgx")
    lam = S.sbuf([128, 2, NCH], F32, "lam")
    nsp8 = S.sbuf([128, 2, NCH], F32, "nsp8")
    wga = S.sbuf([128, 8, 256], BF16, "wga")
    wgx = S.sbuf([128, 8, 256], BF16, "wgx")
    carry = S.sbuf([128, NCH], F32, "carry")
    S.dma("sp", cw[:], cw_d, (), ["cw"], sem="cw")
    S.dma("sp", cb[:], cb_d, (), ["cb"], sem="cb")
    S.dma("sp", bga[:], bga_d, (), ["bga"], sem="bga")
    S.dma("sp", bgx[:], bgx_d, (), ["bgx"], sem="bgx")
    S.dma("sp", lam[:], lam_d, (), ["lam"], sem="lam")
    S.dma("pool", wga[:], wga_d.rearrange("d b (ic p) j -> p (d b ic) j", p=128), (), ["wga"], sem="wga")
    S.dma("pool", wgx[:], wgx_d.rearrange("d b (ic p) j -> p (d b ic) j", p=128), (), ["wgx"], sem="wgx")
    S.act(nsp8[:], lam[:], AF.Exp, ["lam"], ["nsp8"], scale=-1.0)
    S.act(nsp8[:], nsp8[:], AF.Ln, ["nsp8"], ["nsp8"], bias=1.0)
    S.ts("dve", nsp8[:], nsp8[:], -8.0, None, ALU.mult, None, ["nsp8"], ["nsp8"])

    xr_pool = Rot(S, "xr", 2, [128, 2, TT + 3], F32)
    xc_pool = Rot(S, "xc", 2, [128, 2, TT], F32)
    xcb_pool = Rot(S, "xcb", 2, [128, 2, TT], BF16)
    tp = Rot(S, "lt", 12, [128, TT], F32)
    psp = Rot(S, "lps", 6, [128, TT], F32, psum=True)
    GC = 2.0 * float(np.sqrt(2.0 / np.pi))

    for d in (0, 1):
        order = list(range(NT)) if d == 0 else list(range(NT - 1, -1, -1))
        for ti, t in enumerate(order):
            t0 = t * TT
            for blk in range(2):
                xr, xk = xr_pool.next()
                lo, hi = t0 - 2, t0 + TT + 1
                slo, shi = max(lo, 0), min(hi, Sq)
                if lo < 0 or hi > Sq:
                    S.memset("pool", xr[:], 0.0, [xk])
                src = xbT.rearrange("(c p) s -> p c s", p=128)[:, 2 * blk:2 * blk + 2, slo:shi]
                S.dma("sp", xr[:, :, slo - lo:shi - lo], src, (), [xk], sem=xk)
                xc, xck = xc_pool.next()
                xcb, xcbk = xcb_pool.next()
                for jc in range(2):
                    c = 2 * blk + jc
                    S.ts("dve", xc[:, jc, :], xr[:, jc, 0:TT], cw[:, c, 0:1], cb[:, c:c + 1], ALU.mult, ALU.add,
                         [xk, "cw", "cb"], [(xck, jc)])
                    for j in range(1, 4):
                        S.stt("dve", xc[:, jc, :], xr[:, jc, j:j + TT], cw[:, c, j:j + 1], xc[:, jc, :],
                              ALU.mult, ALU.add, [xk, "cw", (xck, jc)], [(xck, jc)])
                    S.copy("pool", xcb[:, jc, :], xc[:, jc, :], [(xck, jc)], [(xcbk, jc)])
                for jc in range(2):
                    c = 2 * blk + jc
                    pr, prk = psp.next()
                    pi, pik = psp.next()
                    for ic in range(2):
                        S.mm(pr[:], wga[:, d * 4 + blk * 2 + ic, jc * 128:(jc + 1) * 128], xcb[:, ic, :],
                             ic == 0, ic == 1, ["wga", (xcbk, ic)], [prk])
                    for ic in range(2):
                        S.mm(pi[:], wgx[:, d * 4 + blk * 2 + ic, jc * 128:(jc + 1) * 128], xcb[:, ic, :],
                             ic == 0, ic == 1, ["wgx", (xcbk, ic)], [pik])
                    r, rk = tp.next()
                    S.act(r[:], pr[:], AF.Sigmoid, [prk, "bga"], [rk], bias=bga[:, d, c:c + 1])
                    gi, gik = tp.next()
                    S.act(gi[:], pi[:], AF.Sigmoid, [pik, "bgx"], [gik], bias=bgx[:, d, c:c + 1])
                    a, ak = tp.next()
                    S.act(a[:], r[:], AF.Exp, [rk, "nsp8"], [ak], scale=nsp8[:, d, c:c + 1])
                    S.tt("pool", r[:], a[:], a[:], ALU.mult, [ak], [rk])
                    S.act(r[:], r[:], AF.Sqrt, [rk], [rk], bias=1.0, scale=-1.0)
                    S.tt("pool", gi[:], gi[:], xc[:, jc, :], ALU.mult, [gik, (xck, jc)], [gik])
                    S.tt("dve", gi[:], gi[:], r[:], ALU.mult, [gik, rk], [gik])
                    hh, hk = tp.next()
                    init = 0.0 if ti == 0 else carry[:, c:c + 1]
                    ckey = ("carry", c)
                    if d == 0:
                        S.op("dve", lambda e, hh=hh, a=a, gi=gi, init=init: e.tensor_tensor_scan(
                            hh[:], a[:], gi[:], init, ALU.mult, ALU.add), [ak, gik, ckey], [hk])
                        S.copy("pool", carry[:, c:c + 1], hh[:, TT - 1:TT], [hk], [ckey])
                        S.dma("pool", yT[c * 128:(c + 1) * 128, t0:t0 + TT], hh[:], [hk], [("y", c, t)], sem=hk)
                    else:
                        S.op("dve", lambda e, hh=hh, a=a, gi=gi, init=init: e.tensor_tensor_scan(
                            hh[:, ::-1], a[:, ::-1], gi[:, ::-1], init, ALU.mult, ALU.add), [ak, gik, ckey], [hk])
                        S.copy("pool", carry[:, c:c + 1], hh[:, 0:1], [hk], [ckey])
                        hf, hfk = tp.next()
                        S.dma("sp", hf[:], yT[c * 128:(c + 1) * 128, t0:t0 + TT], [("y", c, t)], [hfk], sem=hfk)
                        g, gk = tp.next()
                        S.dma("sp", g[:], gateT[c * 128:(c + 1) * 128, t0:t0 + TT], (), [gk], sem=gk)
                        u, uk = tp.next()
                        S.act(u[:], g[:], AF.Square, [gk], [uk])
                        S.ts("pool", u[:], u[:], 0.044715, 1.0, ALU.mult, ALU.add, [uk], [uk])
                        S.tt("pool", u[:], u[:], g[:], ALU.mult, [uk, gk], [uk])
                        S.act(u[:], u[:], AF.Sigmoid, [uk], [uk], scale=GC)
                        S.tt("pool", u[:], u[:], g[:], ALU.mult, [uk, gk], [uk])
                        S.tt("dve", hh[:], hh[:], hf[:], ALU.add, [hk, hfk], [hk])
                        S.tt("dve", hh[:], hh[:], u[:], ALU.mult, [hk, uk], [hk])
                        S.dma("pool", yT[c * 128:(c + 1) * 128, t0:t0 + TT], hh[:], [hk], [("y", c, t)], sem=hk)
    S.finish()
    return nc


def dn_consts():
    m = np.arange(128)[:, None]
    i = np.arange(128)[None, :]
    same = (m // 64) == (i // 64)
    cm = np.stack([(m <= i) & same, (m < i) & same, (m >= i) & same, (m > i) & same, m == i]).astype(np.float32)
    cm = np.ascontiguousarray(cm.transpose(1, 0, 2))
    c01 = np.stack([(np.arange(128) < 64), (np.arange(128) >= 64)], axis=1).astype(np.float32)
    return cm, np.ascontiguousarray(c01)


def dn_phase(nc, Sq, stages=(0, 1, 2), dbg=False):
    S = Sched(nc)
    NB = Sq // 128
    TT0 = 512 if Sq >= 512 else Sq
    NKH, NVH = 4, 8
    LE, LT, GE, GT, ID = 0, 1, 2, 3, 4

    def din(name, shape):
        return nc.dram_tensor(name, list(shape), F32, kind="ExternalInput").ap()

    qT_d = din("qT", [NKH * 128, Sq])
    kT_d = din("kT", [NKH * 128, Sq])
    vT_d = din("vT", [NVH * 128, Sq])
    zT_d = din("zT", [NVH * 128, Sq])
    beta_d = din("betaT", [16, Sq])
    a_d = din("aT", [16, Sq])
    cwq_d = din("cwq", [128, NKH, 4])
    cwk_d = din("cwk", [128, NKH, 4])
    cwv_d = din("cwv", [128, NVH, 4])
    alog_d = din("alog", [16, 1])
    dtb_d = din("dtb", [16, 1])
    onw_d = din("onw", [128, 1])
    cm_d = din("cmask", [128, 5, 128])
    c01_d = din("c01", [128, 2])
    oT_d = nc.dram_tensor("oT", [NVH * 128, Sq], F32, kind="ExternalOutput").ap()
    skind = "ExternalOutput" if dbg else "Internal"
    kTn_s = nc.dram_tensor("kTn_s", [128, NKH, Sq], BF16, kind=skind).ap()
    qTn_s = nc.dram_tensor("qTn_s", [128, NKH, Sq], BF16, kind=skind).ap()
    ktok_s = nc.dram_tensor("ktok_s", [Sq, NKH, 128], BF16, kind=skind).ap()
    vtok_s = nc.dram_tensor("vtok_s", [Sq, NVH, 128], BF16, kind=skind).ap()
    bg_s = nc.dram_tensor("bg_s", [Sq, 32], F32, kind=skind).ap()
    o_s = nc.dram_tensor("o_s", [2, Sq, NVH, 128], F32, kind=skind).ap()

    cm = S.sbuf([128, 5, 128], F32, "cm")
    c01 = S.sbuf([128, 2], F32, "c01")
    cwq = S.sbuf([128, NKH, 4], F32, "cwq")
    cwk = S.sbuf([128, NKH, 4], F32, "cwk")
    cwv = S.sbuf([128, NVH, 4], F32, "cwv")
    alog = S.sbuf([16, 1], F32, "alog")
    dtb = S.sbuf([16, 1], F32, "dtb")
    onw = S.sbuf([128, 1], F32, "onw")
    for t, dsrc, nm in ((cm, cm_d, "cm"), (c01, c01_d, "c01"), (cwq, cwq_d, "cwq"), (cwk, cwk_d, "cwk"),
                        (cwv, cwv_d, "cwv"), (alog, alog_d, "alog"), (dtb, dtb_d, "dtb"), (onw, onw_d, "onw")):
        S.dma("sp", t[:], dsrc, (), [nm], sem=nm)
    idb = S.sbuf([128, 128], BF16, "idb")
    S.copy("dve", idb[:], cm[:, ID, :], ["cm"], ["idb"])
    onesb = S.sbuf([128, 128], BF16, "onesb")
    S.memset("dve", onesb[:], 1.0, ["onesb"])
    onesf = S.sbuf([128, 128], F32, "onesf")
    S.memset("dve", onesf[:], 1.0, ["onesf"])
    epsb = S.sbuf([128, 1], F32, "epsb")
    S.memset("dve", epsb[:], EPS, ["epsb"])
    negA = S.sbuf([16, 1], F32, "negA")
    S.act(negA[:], alog[:], AF.Exp, ["alog"], ["negA"])
    S.ts("dve", negA[:], negA[:], -1.0, None, ALU.mult, None, ["negA"], ["negA"])

    psA = Rot(S, "dpa", 3, [128, 4, 128], F32, psum=True)
    psT = Rot(S, "dpt", 1, [128, 8, 128], BF16, psum=True)
    psS = Rot(S, "dps", 1, [128, 4, 32], F32, psum=True)
    NLB = 3
    psL_t = [S.psum([128, 4, 128], F32, f"dpl{i}") for i in range(NLB)]
    psL_i = [0, 0, 0]

    def next_half(hf):
        i = psL_i[0]
        psL_i[0] = (i + 1) % NLB
        return psL_t[i], ("dpl", i, hf)

    def next_whole():
        i = psL_i[0]
        psL_i[0] = (i + 1) % NLB
        return psL_t[i], [("dpl", i, 0), ("dpl", i, 1)]

    raw_p = Rot(S, "s0raw", 2, [128, TT0 + 3], F32)
    x_p = Rot(S, "s0x", 8, [128, TT0], F32)
    xv_p = Rot(S, "s0xv", 1, [128, TT0], F32)
    sq_p = Rot(S, "s0sq", 2, [128, TT0], BF16)
    rs_p = Rot(S, "s0rs", 2, [128, TT0], F32)
    xn_p = Rot(S, "s0xn", 6, [128, TT0], BF16)
    tk_p = Rot(S, "s0tk", 4, [128, TT0 // 128, 128], BF16)
    sc_p = Rot(S, "s0sc", 2, [16, TT0], F32)
    bgo_p = Rot(S, "s0bgo", 2, [128, TT0 // 128, 32], F32)
    QS = float(128 ** -0.5)
    nb0 = TT0 // 128
    if 0 in stages:
        pending = []

        def flush(keep):
            while len(pending) > keep:
                pending.pop(0)()

        for t0 in range(0, Sq, TT0):
            heads = []
            for kind, src_d, cwt, nh in (("k", kT_d, cwk, NKH), ("q", qT_d, cwq, NKH), ("v", vT_d, cwv, NVH)):
                for hh in range(nh):
                    raw, rk = raw_p.next()
                    lo, hi = t0 - 2, t0 + TT0 + 1
                    slo, shi = max(lo, 0), min(hi, Sq)
                    if lo < 0 or hi > Sq:
                        S.memset("dve", raw[:], 0.0, [rk])
                    S.dma("sp", raw[:, slo - lo:shi - lo], src_d[hh * 128:(hh + 1) * 128, slo:shi], (), [rk], sem=rk)
                    if kind == "v":
                        x, xk = xv_p.next()
                    else:
                        x, xk = x_p.next()
                    S.ts("dve", x[:], raw[:, 0:TT0], cwt[:, hh, 0:1], None, ALU.mult, None, [rk, "cw" + kind], [xk])
                    for j in range(1, 4):
                        S.stt("dve", x[:], raw[:, j:j + TT0], cwt[:, hh, j:j + 1], x[:], ALU.mult, ALU.add,
                              [rk, "cw" + kind, xk], [xk])
                    if kind == "v":
                        xn, xnk = xn_p.next()
                        S.act(xn[:], x[:], AF.Silu, [xk], [xnk])
                        pt, ptk = psT.next()
                        for jb in range(nb0):
                            S.tr(pt[:, jb, :], xn[:, jb * 128:(jb + 1) * 128], idb[:], [xnk, "idb"], [ptk])
                        tk, tkk = tk_p.next()
                        S.copy("act", tk[:], pt[:, 0:nb0, :], [ptk], [tkk])
                        dst_t = vtok_s.rearrange("(nb p) h d -> p nb h d", p=128)
                        pending.append(lambda dst_t=dst_t, tk=tk, tkk=tkk, hh=hh, t0=t0: S.dma(
                            "act", dst_t[:, t0 // 128:t0 // 128 + nb0, hh, :], tk[:], [tkk], [("vtok", t0, hh)], sem=tkk))
                        flush(2)
                    else:
                        S.act(x[:], x[:], AF.Silu, [xk], [xk])
                        heads.append((kind, hh, x, xk, None, None))
            for kind, hh, x, xk, xn, xnk in heads:
                if kind != "v":
                    xn, xnk = xn_p.next()
                    sq, sqk = sq_p.next()
                    S.tt("pool", sq[:], x[:], x[:], ALU.mult, [xk], [sqk])
                    ps, pk = psA.next()
                    psv = ps[:].rearrange("p a b -> p (a b)")
                    S.mm(psv[:, 0:TT0], onesb[:], sq[:], True, True, ["onesb", sqk], [pk])
                    rs, rsk = rs_p.next()
                    S.act(rs[:], psv[:, 0:TT0], AF.Sqrt, [pk, "epsb"], [rsk], bias=epsb[:, 0:1])
                    S.op("dve", lambda e, rs=rs: e.reciprocal(rs[:], rs[:]), [rsk], [rsk])
                    if kind == "q":
                        S.stt("dve", xn[:], x[:], QS, rs[:], ALU.mult, ALU.mult, [xk, rsk], [xnk])
                    else:
                        S.tt("dve", xn[:], x[:], rs[:], ALU.mult, [xk, rsk], [xnk])
                    dst = (kTn_s if kind == "k" else qTn_s)[:, hh, t0:t0 + TT0]
                    pending.append(lambda dst=dst, xn=xn, xnk=xnk, kind=kind, hh=hh, t0=t0: S.dma(
                        "pool", dst, xn[:], [xnk], [(kind + "Tn", t0, hh)], sem=xnk))
                if kind == "k":
                    pt, ptk = psT.next()
                    for jb in range(nb0):
                        S.tr(pt[:, jb, :], xn[:, jb * 128:(jb + 1) * 128], idb[:], [xnk, "idb"], [ptk])
                    tk, tkk = tk_p.next()
                    S.copy("act", tk[:], pt[:, 0:nb0, :], [ptk], [tkk])
                    dst_t = (ktok_s if kind == "k" else vtok_s).rearrange("(nb p) h d -> p nb h d", p=128)
                    pending.append(lambda dst_t=dst_t, tk=tk, tkk=tkk, kind=kind, hh=hh, t0=t0: S.dma(
                        "act", dst_t[:, t0 // 128:t0 // 128 + nb0, hh, :], tk[:], [tkk], [(kind + "tok", t0, hh)], sem=tkk))
                flush(2)
            flush(0)
            bet, bk = sc_p.next()
            S.dma("sp", bet[:], beta_d[:, t0:t0 + TT0], (), [bk], sem=bk)
            av, ak = sc_p.next()
            S.dma("sp", av[:], a_d[:, t0:t0 + TT0], (), [ak], sem=ak)
            S.act(bet[:], bet[:], AF.Sigmoid, [bk], [bk])
            S.act(av[:], av[:], AF.Exp, [ak, "dtb"], [ak], bias=dtb[:, 0:1])
            S.act(av[:], av[:], AF.Ln, [ak], [ak], bias=1.0)
            S.ts("dve", av[:], av[:], negA[:, 0:1], None, ALU.mult, None, [ak, "negA"], [ak])
            pss, psk = psS.next()
            for jb in range(nb0):
                S.tr(pss[:, jb, 0:16], bet[:, jb * 128:(jb + 1) * 128], cm[0:16, ID, 0:16], [bk, "cm"], [psk])
                S.tr(pss[:, jb, 16:32], av[:, jb * 128:(jb + 1) * 128], cm[0:16, ID, 0:16], [ak, "cm"], [psk])
            bgo, bgk = bgo_p.next()
            S.copy("act", bgo[:], pss[:, 0:nb0, :], [psk], [bgk])
            S.dma("act", bg_s.rearrange("(nb p) c -> p nb c", p=128)[:, t0 // 128:t0 // 128 + nb0, :], bgo[:],
                  [bgk], [("bg", t0)], sem=bgk)

    NSLOT = 2
    slots = {}
    for d in range(2):
        for r in range(NSLOT):
            nm = f"sl{d}{r}"
            slots[(d, r)] = dict(
                kT=S.sbuf([128, NKH, 128], BF16, nm + "kT"), qT=S.sbuf([128, NKH, 128], BF16, nm + "qT"),
                ktok=S.sbuf([128, NKH, 128], BF16, nm + "ktok"), vtok=S.sbuf([128, NVH, 128], BF16, nm + "vtok"),
                bg=S.sbuf([128, 32], F32, nm + "bg"), sc=S.sbuf([128, 48], F32, nm + "sc"),
                TU=S.sbuf([128, NVH, 128], BF16, nm + "TU"), AT=S.sbuf([128, NVH, 128], BF16, nm + "AT"),
                kdec=S.sbuf([128, NVH, 128], BF16, nm + "kdec"), nm=nm)
    gm2_p = Rot(S, "gm2", 2, [128, 16], F32)
    GsPs = {d: S.sbuf([128, 2, NKH, 128], F32, f"gsps{d}") for d in range(2)}
    Gm4_p = Rot(S, "gm4", 2, [128, 4, 128], F32)
    E4 = {(d, hg): S.sbuf([128, 4, 128], F32, f"e4_{d}{hg}") for d in range(2) for hg in range(2)}
    zb_pg = {(d, hg): Rot(S, f"zb{d}{hg}", 9, [128, 4, 128], BF16) for d in range(2) for hg in range(2)}
    st32 = {(d, hg): S.sbuf([128, 4, 128], F32, f"st32_{d}{hg}") for d in range(2) for hg in range(2)}
    stbf = {(d, hg): S.sbuf([128, 4, 128], BF16, f"stbf_{d}{hg}") for d in range(2) for hg in range(2)}
    R_p = Rot(S, "Rp", 4, [128, 4, 128], BF16)
    vn_p = Rot(S, "vnp", 4, [128, 4, 128], BF16)
    ot_p = Rot(S, "otp", 2, [128, 4, 128], F32)
    osb = {(d, r): S.sbuf([128, NVH, 128], F32, f"osb{d}{r}") for d in range(2) for r in range(2)}

    def blk_of(d, s):
        return s if d == 0 else NB - 1 - s

    def pre_setup(d, s):
        blk = blk_of(d, s)
        sl = slots[(d, s % NSLOT)]
        nm = sl["nm"]
        c0 = blk * 128
        Mincl, Mrev = (LE, GT) if d == 0 else (GE, LT)
        MstrU, MinclU = (LT, LE) if d == 0 else (GT, GE)
        tb = (c0 // TT0) * TT0
        S.dma("sp", sl["kT"][:], kTn_s[:, :, c0:c0 + 128], [("kTn", tb, hh) for hh in range(NKH)], [nm + "kT"], sem=nm + "kT")
        S.dma("sp", sl["qT"][:], qTn_s[:, :, c0:c0 + 128], [("qTn", tb, hh) for hh in range(NKH)], [nm + "qT"], sem=nm + "qT")
        S.dma("sp", sl["ktok"][:], ktok_s[c0:c0 + 128, :, :], [("ktok", tb, hh) for hh in range(NKH)], [nm + "ktok"], sem=nm + "ktok")
        S.dma("sp", sl["vtok"][:], vtok_s[c0:c0 + 128, :, :], [("vtok", tb, hh) for hh in range(NVH)], [nm + "vtok"], sem=nm + "vtok")
        S.dma("sp", sl["bg"][:], bg_s[c0:c0 + 128, :], [("bg", (c0 // TT0) * TT0)], [nm + "bg"], sem=nm + "bg")
        bg, sc = sl["bg"], sl["sc"]
        gsel = bg[:, 16 + d * 8:16 + d * 8 + 8]
        gm2, gm2k = gm2_p.next()
        S.ts("pool", gm2[:, 0:8], gsel, c01[:, 0:1], None, ALU.mult, None, [nm + "bg", "c01"], [gm2k])
        S.ts("pool", gm2[:, 8:16], gsel, c01[:, 1:2], None, ALU.mult, None, [nm + "bg", "c01"], [gm2k])
        pss, psk = psS.next()
        pv = pss[:].rearrange("p a b -> p (a b)")
        S.mm(pv[:, 0:8], cm[:, Mincl, :], gsel, True, True, ["cm", nm + "bg"], [psk])
        S.mm(pv[:, 8:16], cm[:, Mrev, :], gsel, True, True, ["cm", nm + "bg"], [psk])
        S.mm(pv[:, 16:32], onesf[:], gm2[:], True, True, ["onesf", gm2k], [psk])
        S.act(sc[:, 0:32], pv[:, 0:32], AF.Exp, [psk], [nm + "sc"])
        S.ts("pool", sc[:, 32:40], sc[:, 0:8], -1.0, None, ALU.mult, None, [nm + "sc"], [nm + "sc"])
        S.ts("pool", sc[:, 40:48], bg[:, d * 8:d * 8 + 8], -1.0, None, ALU.mult, None, [nm + "bg"], [nm + "sc"])
        pG, pGk = psA.next()
        pP, pPk = psA.next()
        for kh in range(NKH):
            S.mm(pG[:, kh, :], sl["kT"][:, kh, :], sl["kT"][:, kh, :], True, True, [nm + "kT"], [pGk])
            S.mm(pP[:, kh, :], sl["kT"][:, kh, :], sl["qT"][:, kh, :], True, True, [nm + "kT", nm + "qT"], [pPk])
        gp, gpk = GsPs[d], ("gsps", d)
        S.tt("dve", gp[:, 0, :, :], pG[:], cm[:, MstrU:MstrU + 1, :].broadcast_to([128, NKH, 128]), ALU.mult,
             [pGk, "cm"], [(gpk, 0)])
        S.tt("dve", gp[:, 1, :, :], pP[:], cm[:, MinclU:MinclU + 1, :].broadcast_to([128, NKH, 128]), ALU.mult,
             [pPk, "cm"], [(gpk, 1)])

    def pre_group(d, s, hg):
        sl = slots[(d, s % NSLOT)]
        nm = sl["nm"]
        Mincl, Mrev = (LE, GT) if d == 0 else (GE, LT)
        bg, sc = sl["bg"], sl["sc"]
        gp, gpk = GsPs[d], ("gsps", d)
        zb = zb_pg[(d, hg)]
        gm4, gm4k = Gm4_p.next()
        pD, pDk = psA.next()
        for q in range(4):
            vh = 4 * hg + q
            S.ts("pool", gm4[:, q, :], cm[:, Mincl, :], bg[:, 16 + d * 8 + vh:16 + d * 8 + vh + 1], None, ALU.mult, None,
                 ["cm", nm + "bg"], [(gm4k, q)])
            S.mm(pD[:, q, :], cm[:, Mrev, :], gm4[:, q, :], True, True, ["cm", (gm4k, q)], [pDk])
        e4, e4k = E4[(d, hg)], ("e4", d, hg)
        S.act(e4[:], pD[:], AF.Exp, [pDk], [e4k])
        yield
        zu, zuk = zb.next()
        for q in range(4):
            vh = 4 * hg + q
            kh = vh // 2
            S.tt("pool", sl["AT"][:, vh, :], e4[:, q, :], gp[:, 1, kh, :], ALU.mult, [e4k, (gpk, 1)], [(nm + "AT", vh)])
            S.stt("dve", zu[:, q, :], e4[:, q, :], sc[:, 40 + vh:41 + vh], gp[:, 0, kh, :], ALU.mult, ALU.mult,
                  [e4k, nm + "sc", (gpk, 0)], [(zuk, q)])
            S.act(sl["kdec"][:, vh, :], sl["ktok"][:, kh, :], AF.Copy, [nm + "ktok", nm + "sc"], [(nm + "kdec", vh)],
                  scale=sc[:, 8 + vh:9 + vh])
        pt, ptk = psT.next()
        for q in range(4):
            S.tr(pt[:, q, :], zu[:, q, :], idb[:], [(zuk, q), "idb"], [ptk])
        z, zk = zb.next()
        S.copy("act", z[:], pt[:, 0:4, :], [ptk], [zk])
        zuk_all = [(zuk, q) for q in range(4)]
        tu, tuk = zb.next()
        tl, tlk = zb.next()
        idbb = idb[:].unsqueeze(1).broadcast_to([128, 4, 128])
        S.tt("pool", tu[:], zu[:], idbb, ALU.add, zuk_all + ["idb"], [tuk])
        S.tt("pool", tl[:], z[:], idbb, ALU.add, [zk, "idb"], [tlk])
        zp, zpk, zup, zupk = z, [zk], zu, zuk_all
        yield
        def squaring(k, zp, zpk, zup, zupk):
            pB, pBk = psA.next()
            for q in range(4):
                S.mm(pB[:, q, :], zp[:, q, :], zup[:, q, :], True, True, zpk + zupk, [pBk])
            if k < 5:
                pA_, pAk = psA.next()
                for q in range(4):
                    S.mm(pA_[:, q, :], zup[:, q, :], zp[:, q, :], True, True, zpk + zupk, [pAk])
            nzup, nzupk = zb.next()
            S.copy("act", nzup[:], pB[:], [pBk], [nzupk])
            nzp, nzpk = None, None
            if k < 5:
                nzp, nzpk = zb.next()
                S.copy("act", nzp[:], pA_[:], [pAk], [nzpk])
            return nzp, nzpk, nzup, nzupk

        nzp, nzpk, nzup, nzupk = squaring(1, zp, zpk, zup, zupk)
        yield
        for k in range(1, 6):
            cur = (nzp, nzpk, nzup, nzupk)
            if k < 5:
                nxt = squaring(k + 1, nzp, [nzpk], nzup, [nzupk])
            nzp, nzpk, nzup, nzupk = cur
            pC, pCk = psA.next()
            for q in range(4):
                S.mm(pC[:, q, :], tl[:, q, :], nzup[:, q, :], True, True, [tlk, nzupk], [pCk])
            if k < 5:
                pE, pEk = psA.next()
                for q in range(4):
                    S.mm(pE[:, q, :], tu[:, q, :], nzp[:, q, :], True, True, [tuk, nzpk], [pEk])
                ntu, ntuk = zb.next()
                S.tt("dve", ntu[:], pC[:], tu[:], ALU.add, [pCk, tuk], [ntuk])
                ntl, ntlk = zb.next()
                S.tt("dve", ntl[:], pE[:], tl[:], ALU.add, [pEk, tlk], [ntlk])
                tu, tuk, tl, tlk = ntu, ntuk, ntl, ntlk
                nzp, nzpk, nzup, nzupk = nxt
            else:
                S.tt("dve", sl["TU"][:, 4 * hg:4 * hg + 4, :], pC[:], tu[:], ALU.add, [pCk, tuk],
                     [(nm + "TU", 4 * hg + q) for q in range(4)])
            yield

    def loop(s):
        for hi in range(2):
            groups = []
            for d in range(2):
                sl = slots[(d, s % NSLOT)]
                hf = hi if d == 0 else 1 - hi
                for hg in range(2):
                    groups.append((d, hg, sl, hf, slice(64 * hf, 64 * hf + 64), sl["nm"]))
            ksl = []
            for d, hg, sl, hf, pr, nm in groups:
                bank, bk = next_half(hf)
                for q in range(4):
                    vh = 4 * hg + q
                    S.mm(bank[pr, q, :], sl["kT"][:, vh // 2, 64 * hf:64 * hf + 64], stbf[(d, hg)][:, q, :], True, True,
                         [nm + "kT", ("stbf", d, hg)], [bk])
                ksl.append((bank, bk))
            yield
            Rl = []
            for (d, hg, sl, hf, pr, nm), (bank, bk) in zip(groups, ksl):
                R, Rk = R_p.next()
                for q in range(4):
                    vh = 4 * hg + q
                    S.stt("dve", R[pr, q, :], bank[pr, q, :], sl["sc"][pr, 32 + vh:33 + vh], sl["vtok"][pr, vh, :],
                          ALU.mult, ALU.add, [bk, nm + "sc", nm + "vtok"], [Rk])
                Rl.append((R, Rk))
            yield
            vl = []
            for (d, hg, sl, hf, pr, nm), (R, Rk) in zip(groups, Rl):
                bank, bk = next_half(hf)
                for q in range(4):
                    vh = 4 * hg + q
                    S.mm(bank[pr, q, :], sl["TU"][pr, vh, 64 * hf:64 * hf + 64], R[pr, q, :], True, True,
                         [(nm + "TU", vh), Rk], [bk])
                vn, vnk = vn_p.next()
                for q in range(4):
                    vh = 4 * hg + q
                    S.act(vn[pr, q, :], bank[pr, q, :], AF.Copy, [bk, nm + "bg"], [vnk],
                          scale=sl["bg"][pr, d * 8 + vh:d * 8 + vh + 1])
                vl.append((vn, vnk))
            yield
            for gi_, ((d, hg, sl, hf, pr, nm), (vn, vnk)) in enumerate(zip(groups, vl)):
                if gi_ == 2:
                    yield
                ob = osb[(d, s % 2)]
                b1, b1k = next_half(hf)
                for q in range(4):
                    vh = 4 * hg + q
                    S.mm(b1[pr, q, :], sl["AT"][pr, vh, 64 * hf:64 * hf + 64], vn[pr, q, :], True, True,
                         [(nm + "AT", vh), vnk], [b1k])
                b2, b2k = next_half(hf)
                for q in range(4):
                    vh = 4 * hg + q
                    S.mm(b2[pr, q, :], sl["qT"][:, vh // 2, 64 * hf:64 * hf + 64], stbf[(d, hg)][:, q, :], True, True,
                         [nm + "qT", ("stbf", d, hg)], [b2k])
                b3, b3k = next_whole()
                for q in range(4):
                    vh = 4 * hg + q
                    S.mm(b3[:, q, :], sl["kdec"][pr, vh, :], vn[pr, q, :], True, True, [(nm + "kdec", vh), vnk], b3k)
                ot, otk = ot_p.next()
                S.copy("act", ot[pr, :, :], b1[pr, :, :], [b1k], [otk])
                for q in range(4):
                    vh = 4 * hg + q
                    S.stt("dve", ob[pr, vh, :], b2[pr, q, :], sl["sc"][pr, vh:vh + 1], ot[pr, q, :], ALU.mult, ALU.add,
                          [b2k, nm + "sc", otk], [("osb", d, s % 2, vh, hf)])
                for q in range(4):
                    vh = 4 * hg + q
                    S.stt("dve", st32[(d, hg)][:, q, :], st32[(d, hg)][:, q, :],
                          sl["sc"][:, 16 + hf * 8 + vh:17 + hf * 8 + vh], b3[:, q, :], ALU.mult, ALU.add,
                          [("st32", d, hg), nm + "sc"] + b3k, [("st32", d, hg)])
                S.copy("act", stbf[(d, hg)][:], st32[(d, hg)][:], [("st32", d, hg)], [("stbf", d, hg)])
            yield
        for d in range(2):
            blk = blk_of(d, s)
            S.dma("pool", o_s[d, blk * 128:(blk + 1) * 128, :, :], osb[(d, s % 2)][:],
                  [("osb", d, s % 2, vh, hf) for vh in range(NVH) for hf in range(2)], [("o_s", d, blk)], sem=("osb", d, s % 2))

    if 1 in stages:
        for dd in range(2):
            for hg in range(2):
                S.memset("pool", st32[(dd, hg)][:], 0.0, [("st32", dd, hg)])
                S.memset("pool", stbf[(dd, hg)][:], 0.0, [("stbf", dd, hg)])
        def lockstep(gens):
            gens = [(i, g) for i, g in enumerate(gens)]
            r = 0
            while gens:
                alive = []
                for i, g in gens:
                    if r >= i:
                        try:
                            next(g)
                        except StopIteration:
                            continue
                    alive.append((i, g))
                gens = alive
                r += 1

        def pre_gens(s):
            for d in range(2):
                pre_setup(d, s)
            return [pre_group(d, s, hg) for d in range(2) for hg in range(2)]

        lockstep(pre_gens(0))
        for s in range(NB):
            gens = [loop(s)]
            if s + 1 < NB:
                gens = gens + pre_gens(s + 1)
            lockstep(gens)

    if 2 in stages:
        of_p = Rot(S, "s2of", 2, [128, NVH, 128], F32)
        ob_p = Rot(S, "s2ob", 1, [128, NVH, 128], F32)
        on_p = Rot(S, "s2on", 1, [128, NVH, 128], BF16)
        z_p = Rot(S, "s2z", 1, [128, NVH, 128], F32)
        ss_p = Rot(S, "s2ss", 2, [128, NVH], F32)
        for blk in range(NB):
            c0 = blk * 128
            of, ofk = of_p.next()
            ob2, obk = ob_p.next()
            S.dma("sp", of[:], o_s[0, c0:c0 + 128, :, :], [("o_s", 0, blk)], [ofk], sem=ofk)
            S.dma("sp", ob2[:], o_s[1, c0:c0 + 128, :, :], [("o_s", 1, blk)], [obk], sem=obk)
            zt, ztk = z_p.next()
            S.dma("sp", zt[:], zT_d.rearrange("(h p) s -> p h s", p=128)[:, :, c0:c0 + 128], (), [ztk], sem=ztk)
            S.tt("pool", of[:], of[:], ob2[:], ALU.add, [ofk, obk], [ofk])
            sq2, sq2k = ob2, obk
            S.tt("pool", sq2[:], of[:], of[:], ALU.mult, [ofk], [sq2k])
            ss, ssk = ss_p.next()
            S.op("dve", lambda e, ss=ss, sq2=sq2: e.tensor_reduce(ss[:], sq2[:], mybir.AxisListType.X, ALU.add), [sq2k], [ssk])
            S.act(ss[:], ss[:], AF.Sqrt, [ssk, "epsb"], [ssk], bias=epsb[:, 0:1], scale=1.0 / 128)
            S.op("dve", lambda e, ss=ss: e.reciprocal(ss[:], ss[:]), [ssk], [ssk])
            on, onk = on_p.next()
            S.tt("dve", on[:], of[:], ss[:].unsqueeze(2).broadcast_to([128, NVH, 128]), ALU.mult, [ofk, ssk], [onk])
            S.act(zt[:], zt[:], AF.Silu, [ztk], [ztk])
            pt, ptk = psT.next()
            for vh in range(NVH):
                S.tr(pt[:, vh, :], on[:, vh, :], idb[:], [onk, "idb"], [ptk])
            oo, ook = of, ofk
            S.stt("dve", oo[:], pt[:], onw[:, 0:1], zt[:], ALU.mult, ALU.mult, [ptk, "onw", ztk, onk], [ook])
            S.dma("pool", oT_d.rearrange("(h p) s -> p h s", p=128)[:, :, c0:c0 + 128], oo[:], [ook], [], sem=ook)
    S.finish()
    return nc


SEQ = 8192
BATCH = 2
TPC = BATCH * SEQ // NCORES
DN_N = 12416
LRU_N = 4096
_PROG_CACHE = {}


def _c(a):
    return np.ascontiguousarray(a, dtype=np.float32)


def _nw(w):
    return _c(w.reshape(D // 128, 128).T)


def _run(nc, in_maps):
    import os, time
    t0 = time.time()
    r = run_bass_kernel_spmd(nc, in_maps, core_ids=list(range(NCORES))).results
    if os.environ.get("KDEBUG"):
        print(f"[kernel] launch took {time.time() - t0:.1f}s", flush=True)
    return r


def _dense_prog(sig):
    nc = bass.Bass("TRN2", target_bir_lowering=False)
    ops = []
    for o in sig:
        if o[0] == "outproj":
            ops.append(dict(op="outproj", dm=o[1]))
        elif o[0] == "inproj":
            ops.append(dict(op="inproj", n=o[1]))
        else:
            ops.append(dict(op=o[0]))
    dense_phase(nc, TPC, ops)
    return nc


def kernel(x, ffn1_norm, ffn1_w_gate_up, ffn1_w_down, mix_norm, ffn2_norm, ffn2_w_gate_up, ffn2_w_down,
           dn_w_in, dn_conv_w, dn_a_log, dn_dt_bias, dn_out_norm, dn_w_out,
           lru_w_in, lru_conv_w, lru_conv_b, lru_w_gate_a, lru_b_gate_a, lru_w_gate_x, lru_b_gate_x,
           lru_lambda, lru_w_out, final_norm):
    x = np.asarray(x, np.float32)
    depth = ffn1_norm.shape[0]
    seg = lambda c: (c // 4, slice((c % 4) * TPC, (c % 4 + 1) * TPC))
    hT = [_c(x[seg(c)[0], seg(c)[1], :].T) for c in range(NCORES)]
    mix_out = None
    cm, c01 = dn_consts()
    y = None
    for layer in range(depth + 1):
        sig = []
        common = {}
        if layer > 0:
            pl = layer - 1
            j = pl // 2
            wo = dn_w_out[j] if pl % 2 == 0 else lru_w_out[j]
            idx = len(sig)
            sig.append(("outproj", wo.shape[0]))
            common[f"wo{idx}"] = _c(wo)
            oidx = idx
            idx = len(sig)
            sig.append(("ffn",))
            common[f"nw{idx}"] = _nw(ffn2_norm[pl])
            common[f"wgu{idx}"] = _c(ffn2_w_gate_up[pl])
            common[f"wd{idx}"] = _c(ffn2_w_down[pl])
        if layer < depth:
            idx = len(sig)
            sig.append(("ffn",))
            common[f"nw{idx}"] = _nw(ffn1_norm[layer])
            common[f"wgu{idx}"] = _c(ffn1_w_gate_up[layer])
            common[f"wd{idx}"] = _c(ffn1_w_down[layer])
            j = layer // 2
            wi = dn_w_in[j] if layer % 2 == 0 else lru_w_in[j]
            idx = len(sig)
            sig.append(("inproj", wi.shape[1]))
            common[f"nw{idx}"] = _nw(mix_norm[layer])
            common[f"wi{idx}"] = _c(wi)
            pidx = idx
            sig.append(("store_h",))
            hidx = len(sig) - 1
        else:
            idx = len(sig)
            sig.append(("final",))
            common[f"nw{idx}"] = _nw(final_norm)
            fidx = idx
        sig = tuple(sig)
        if sig not in _PROG_CACHE:
            _PROG_CACHE[sig] = _dense_prog(sig)
        in_maps = []
        for c in range(NCORES):
            m = dict(common)
            m["hT_in"] = hT[c]
            if layer > 0:
                m[f"oT{oidx}"] = mix_out[c]
            in_maps.append(m)
        res = _run(_PROG_CACHE[sig], in_maps)
        if layer == depth:
            y = np.empty((BATCH, SEQ, D), np.float32)
            for c in range(NCORES):
                b, sl = seg(c)
                y[b, sl, :] = res[c][f"y{fidx}"].T
            break
        hT = [res[c][f"hT_out{hidx}"] for c in range(NCORES)]
        proj = [np.concatenate([res[b * 4 + s][f"proj{pidx}"] for s in range(4)], axis=1) for b in range(BATCH)]
        del res
        j = layer // 2
        in_maps = []
        if layer % 2 == 0:
            key = ("dn",)
            if key not in _PROG_CACHE:
                nc = bass.Bass("TRN2", target_bir_lowering=False)
                dn_phase(nc, SEQ)
                _PROG_CACHE[key] = nc
            cwv_all = dn_conv_w[j]
            for c in range(NCORES):
                b, g4 = c // 4, c % 4
                P = proj[b]
                kh0, vh0 = 4 * g4, 8 * g4
                ba_rows = lambda kind: np.concatenate([P[12288 + d * 64 + kind * 32 + vh0: 12288 + d * 64 + kind * 32 + vh0 + 8]
                                                       for d in range(2)], axis=0)
                cwl = lambda w, nh: _c(w.reshape(4, nh, 128).transpose(2, 1, 0))
                in_maps.append({
                    "qT": _c(P[kh0 * 128:(kh0 + 4) * 128]), "kT": _c(P[2048 + kh0 * 128:2048 + (kh0 + 4) * 128]),
                    "vT": _c(P[4096 + vh0 * 128:4096 + (vh0 + 8) * 128]), "zT": _c(P[8192 + vh0 * 128:8192 + (vh0 + 8) * 128]),
                    "betaT": _c(ba_rows(0)), "aT": _c(ba_rows(1)),
                    "cwq": cwl(cwv_all[:, kh0 * 128:(kh0 + 4) * 128], 4),
                    "cwk": cwl(cwv_all[:, 2048 + kh0 * 128:2048 + (kh0 + 4) * 128], 4),
                    "cwv": cwl(cwv_all[:, 4096 + vh0 * 128:4096 + (vh0 + 8) * 128], 8),
                    "alog": _c(dn_a_log[j][:, vh0:vh0 + 8].reshape(16, 1)),
                    "dtb": _c(dn_dt_bias[j][:, vh0:vh0 + 8].reshape(16, 1)),
                    "onw": _c(dn_out_norm[j].reshape(128, 1)), "cmask": cm, "c01": c01})
            res = _run(_PROG_CACHE[key], in_maps)
            outs = [res[c]["oT"] for c in range(NCORES)]
        else:
            key = ("lru",)
            if key not in _PROG_CACHE:
                nc = bass.Bass("TRN2", target_bir_lowering=False)
                lru_phase(nc, SEQ)
                _PROG_CACHE[key] = nc
            for c in range(NCORES):
                b, g4 = c // 4, c % 4
                P = proj[b]
                ch = slice(512 * g4, 512 * g4 + 512)
                pl1 = lambda v: _c(v.reshape(4, 128).T)
                pl2 = lambda v: _c(v.reshape(2, 4, 128).transpose(2, 0, 1))
                in_maps.append({
                    "xbT": _c(P[512 * g4:512 * g4 + 512]), "gateT": _c(P[2048 + 512 * g4:2048 + 512 * g4 + 512]),
                    "cw": _c(lru_conv_w[j][:, ch].reshape(4, 4, 128).transpose(2, 1, 0)), "cb": pl1(lru_conv_b[j][ch]),
                    "wga": _c(lru_w_gate_a[j][:, 2 * g4:2 * g4 + 2]), "wgx": _c(lru_w_gate_x[j][:, 2 * g4:2 * g4 + 2]),
                    "bga": pl2(lru_b_gate_a[j][:, ch]), "bgx": pl2(lru_b_gate_x[j][:, ch]), "lam": pl2(lru_lambda[j][:, ch])})
            res = _run(_PROG_CACHE[key], in_maps)
            outs = [res[c]["yT"] for c in range(NCORES)]
        del proj, in_maps
        full = [np.concatenate([outs[b * 4 + g] for g in range(4)], axis=0) for b in range(BATCH)]
        mix_out = [_c(full[c // 4][:, seg(c)[1]]) for c in range(NCORES)]
        del res, outs, full
    return y
```
